# Optimizing a Trainium2 kernel written in Bass

```python
import math
import jax, jax.numpy as jnp
from jax import lax
import numpy as np

D_MODEL = 4096
BATCH = 32
SEQ = 256
DEPTH = 1
DEC_BATCH = 2
DEC_SEQ = 2048
PAST_LEN = 256

GRID_W = 64
D_SSM = 2048
SSM_GROUP = 16
N_GROUPS = D_SSM // SSM_GROUP
STATE_DIM = 64
N_DIR = 2
D_CONV = 2048
CONV_W = 3
D_IN = 2 * D_SSM + 4 * D_CONV + 2 * D_MODEL
ALPHA = (2.0 * DEPTH) ** 0.25
BETA = (8.0 * DEPTH) ** -0.25
LN_EPS = 1e-5

kernel_name = "hybrid_s5_shortconv_diffusion_step"


def _ln(x):
    xf = x.astype(jnp.float32)
    xc = xf - jnp.mean(xf, axis=-1, keepdims=True)
    var = jnp.mean(xc * xc, axis=-1, keepdims=True)
    return xc * lax.rsqrt(var + LN_EPS)


def _s5_scan(u, h0, a_re, a_im, log_dt, b_re, b_im, c_re, c_im, reverse):
    f32 = jnp.float32
    a = lax.complex(a_re.astype(f32), a_im.astype(f32))
    dt = jnp.exp(log_dt.astype(f32))[:, None]
    a_bar = jnp.exp(a * dt)
    b = lax.complex(b_re.astype(f32), b_im.astype(f32))
    b_bar = ((a_bar - 1.0) / a)[..., None] * b
    cm = lax.complex(c_re.astype(f32), c_im.astype(f32))
    bu = jnp.einsum('gpc,nlgc->nlgp', b_bar, u.astype(jnp.complex64))
    first = -1 if reverse else 0
    last = 0 if reverse else -1
    bu = bu.at[:, first].add(a_bar * h0)
    a_seq = jnp.broadcast_to(a_bar, bu.shape)

    def combine(left, right):
        a_l, b_l = left
        a_r, b_r = right
        return a_r * a_l, a_r * b_l + b_r

    _, h = lax.associative_scan(combine, (a_seq, bu), axis=1, reverse=reverse)
    y = jnp.einsum('gcp,nlgp->nlgc', cm, h).real
    return y, h[:, last]


def _short_conv(v, w, b):
    L = v.shape[-2]
    p = CONV_W // 2
    vp = jnp.pad(v, [(0, 0)] * (v.ndim - 2) + [(p, p), (0, 0)])
    out = b.astype(jnp.float32)
    for k in range(CONV_W):
        out = out + vp[..., k:k + L, :] * w[k].astype(jnp.float32)
    return out


def _layer(x, cond, h0, on_grid, w_mod, b_mod, w_in, a_re, a_im, log_dt, b_re, b_im,
           c_re, c_im, ssm_d, w_glu, b_glu, conv_w, conv_b, w_branch_a, w_branch_b,
           w_o, ln_g, ln_b):
    f32 = jnp.float32
    n, L, _ = x.shape
    mod = jax.nn.silu(cond.astype(f32)) @ w_mod.astype(f32) + b_mod.astype(f32)
    shift, scale, gate = jnp.split(mod[:, None, :], 3, axis=-1)
    h = _ln(x) * (1.0 + scale) + shift
    proj = h @ w_in.astype(f32)
    idx = [D_SSM, 2 * D_SSM, 2 * D_SSM + D_CONV, 2 * D_SSM + 2 * D_CONV,
           2 * D_SSM + 3 * D_CONV, 2 * D_SSM + 4 * D_CONV, 2 * D_SSM + 4 * D_CONV + D_MODEL]
    u_a, z_a, hb, gb, gc, z_b, r_a, r_b = jnp.split(proj, idx, axis=-1)

    u = u_a.reshape(n, L, N_GROUPS, SSM_GROUP)
    y_f, s_f = _s5_scan(u, h0[:, 0], a_re[0], a_im[0], log_dt[0], b_re[0], b_im[0],
                        c_re[0], c_im[0], reverse=False)
    y_b, s_b = _s5_scan(u, h0[:, 1], a_re[1], a_im[1], log_dt[1], b_re[1], b_im[1],
                        c_re[1], c_im[1], reverse=True)
    y = (y_f + y_b).reshape(n, L, D_SSM) + ssm_d.astype(f32) * u_a
    y = jax.nn.gelu(y)
    y = y * jax.nn.sigmoid(y @ w_glu.astype(f32) + b_glu.astype(f32))
    out_a = (y * jax.nn.silu(z_a)) @ w_branch_a.astype(f32)

    v = gc * hb
    if on_grid:
        rows = L // GRID_W
        v = _short_conv(v.reshape(n, rows, GRID_W, D_CONV), conv_w, conv_b).reshape(n, L, D_CONV)
    else:
        v = _short_conv(v, conv_w, conv_b)
    out_b = (gb * v * jax.nn.silu(z_b)) @ w_branch_b.astype(f32)

    merged = jax.nn.sigmoid(r_a) * out_a + jax.nn.sigmoid(r_b) * out_b
    out = merged @ w_o.astype(f32)
    res = ALPHA * x.astype(f32) + gate * out
    y_out = _ln(res) * ln_g.astype(f32) + ln_b.astype(f32)
    return y_out.astype(x.dtype), jnp.stack([s_f, s_b], axis=1)


def setup_inputs(seed: int = 0) -> dict:
    key = jax.random.key(seed)
    ks = jax.random.split(key, 32)
    f32 = jnp.float32

    def nrm(k, shape, s):
        return jax.random.normal(k, shape, f32) * s

    n_idx = jnp.arange(STATE_DIM, dtype=f32)
    sshape = (DEPTH, N_DIR, N_GROUPS, STATE_DIM)
    return {
        'x_prompt': nrm(ks[0], (BATCH, SEQ, D_MODEL), 1.0),
        'x_sample': nrm(ks[1], (DEC_BATCH, DEC_SEQ, D_MODEL), 1.0),
        'c': nrm(ks[2], (DEC_BATCH, D_MODEL), 1.0),
        'state_ssm_re': nrm(ks[3], (DEC_BATCH, DEPTH, N_DIR, N_GROUPS, STATE_DIM), 0.1),
        'state_ssm_im': nrm(ks[4], (DEC_BATCH, DEPTH, N_DIR, N_GROUPS, STATE_DIM), 0.1),
        'c_ctx': nrm(ks[5], (D_MODEL,), 1.0),
        'w_mod': nrm(ks[6], (DEPTH, D_MODEL, 3 * D_MODEL), 0.5 * D_MODEL ** -0.5),
        'b_mod': nrm(ks[7], (DEPTH, 3 * D_MODEL), 0.01),
        'w_in': nrm(ks[8], (DEPTH, D_MODEL, D_IN), D_MODEL ** -0.5),
        'ssm_a_re': -0.5 + nrm(ks[9], sshape, 0.01),
        'ssm_a_im': jnp.broadcast_to(math.pi * n_idx, sshape) + nrm(ks[10], sshape, 0.01),
        'ssm_log_dt': jax.random.uniform(ks[11], (DEPTH, N_DIR, N_GROUPS), f32,
                                         minval=math.log(1e-3), maxval=math.log(1e-1)),
        'ssm_b_re': nrm(ks[12], (DEPTH, N_DIR, N_GROUPS, STATE_DIM, SSM_GROUP), (2 * SSM_GROUP) ** -0.5),
        'ssm_b_im': nrm(ks[13], (DEPTH, N_DIR, N_GROUPS, STATE_DIM, SSM_GROUP), (2 * SSM_GROUP) ** -0.5),
        'ssm_c_re': nrm(ks[14], (DEPTH, N_DIR, N_GROUPS, SSM_GROUP, STATE_DIM), STATE_DIM ** -0.5),
        'ssm_c_im': nrm(ks[15], (DEPTH, N_DIR, N_GROUPS, SSM_GROUP, STATE_DIM), STATE_DIM ** -0.5),
        'ssm_d': nrm(ks[16], (DEPTH, D_SSM), 1.0),
        'w_glu': nrm(ks[17], (DEPTH, D_SSM, D_SSM), D_SSM ** -0.5),
        'b_glu': nrm(ks[18], (DEPTH, D_SSM), 0.01),
        'conv_w': nrm(ks[19], (DEPTH, CONV_W, D_CONV), CONV_W ** -0.5),
        'conv_b': nrm(ks[20], (DEPTH, D_CONV), 0.01),
        'w_branch_a': nrm(ks[21], (DEPTH, D_SSM, D_MODEL), BETA * D_SSM ** -0.5),
        'w_branch_b': nrm(ks[22], (DEPTH, D_CONV, D_MODEL), BETA * D_CONV ** -0.5),
        'w_o': nrm(ks[23], (DEPTH, D_MODEL, D_MODEL), BETA * D_MODEL ** -0.5),
        'ln_g': 1.0 + nrm(ks[24], (DEPTH, D_MODEL), 0.01),
        'ln_b': nrm(ks[25], (DEPTH, D_MODEL), 0.01),
    }


def reference(x_prompt, x_sample, c, state_ssm_re, state_ssm_im, c_ctx, w_mod, b_mod, w_in,
              ssm_a_re, ssm_a_im, ssm_log_dt, ssm_b_re, ssm_b_im, ssm_c_re, ssm_c_im, ssm_d,
              w_glu, b_glu, conv_w, conv_b, w_branch_a, w_branch_b, w_o, ln_g, ln_b):
    f32 = jnp.float32
    hp = x_prompt
    hs = x_sample
    zero_state = jnp.zeros((x_prompt.shape[0], N_DIR, N_GROUPS, STATE_DIM), jnp.complex64)
    ctx_cond = c_ctx[None, :]
    new_re = []
    new_im = []
    for l in range(DEPTH):
        params = (w_mod[l], b_mod[l], w_in[l], ssm_a_re[l], ssm_a_im[l], ssm_log_dt[l],
                  ssm_b_re[l], ssm_b_im[l], ssm_c_re[l], ssm_c_im[l], ssm_d[l], w_glu[l],
                  b_glu[l], conv_w[l], conv_b[l], w_branch_a[l], w_branch_b[l], w_o[l],
                  ln_g[l], ln_b[l])
        hp, st = _layer(hp, ctx_cond, zero_state, False, *params)
        new_re.append(st.real)
        new_im.append(st.imag)
        cache = lax.complex(state_ssm_re[:, l].astype(f32), state_ssm_im[:, l].astype(f32))
        hs, _ = _layer(hs, c, cache, True, *params)
    new_state_ssm_re = jnp.stack(new_re, axis=1).astype(x_prompt.dtype)
    new_state_ssm_im = jnp.stack(new_im, axis=1).astype(x_prompt.dtype)
    return (hp, hs, new_state_ssm_re, new_state_ssm_im)
```

```python
import math
import numpy as np
import concourse.bass as bass
import concourse.mybir as mybir
from concourse.bass_utils import run_bass_kernel_spmd

F32 = mybir.dt.float32
BF16 = mybir.dt.bfloat16
AF = mybir.ActivationFunctionType
ALU = mybir.AluOpType

LN_EPS = 1e-5
TB = 512
NMC = 64
TWO_PI = 2.0 * math.pi
MAGIC = 12582912.0


class Cfg:
    def __init__(self, D=4096, DS=2048, DC=2048):
        self.D, self.DS, self.DC = D, DS, DC
        self.KC = D // 128
        self.KS = DS // 128
        self.KCV = DC // 128
        self.NG = DS // 16
        self.GQ = self.NG // 2
        self.GQP = min(32, self.GQ)
        self.NPASS = self.GQ // self.GQP
        self.TU = 4
        self.NU = self.GQ // self.TU
        self.UPP = self.GQP // self.TU
        self.DIN = 2 * DS + 4 * DC + 2 * D
        self.NIC = self.DIN // 128
        self.NMOD = 3 * D // 128
        self.ALPHA = 2.0 ** 0.25


class Trk:
    def __init__(self, nc):
        self.nc = nc
        self.eng = {'pe': nc.tensor, 'act': nc.scalar, 'dve': nc.vector, 'pool': nc.gpsimd,
                    'sp': nc.sync}
        self.sem = {e: nc.semaphore('sem_' + e).__enter__() for e in self.eng}
        self.seq = {e: 0 for e in self.eng}
        self.known = {}
        self.st = {}
        self.dstream = {}
        self.n_ops = 0

    def _wait(self, e, dep):
        name, sem, val, _ = dep
        k = (e, name)
        if self.known.get(k, 0) >= val:
            return
        self.known[k] = val
        self.eng[e].wait_ge(sem, val)

    def _collect(self, e, reads, writes, is_dma):
        deps = {}

        def add(d):
            if d is None:
                return
            if d[0] not in deps or deps[d[0]][2] < d[2]:
                deps[d[0]] = d

        for k in reads:
            s = self.st.get(k)
            if s is not None and s[0] is not None:
                d = s[0]
                if (not is_dma) and d[3] == e and e == 'pe':
                    continue
                add(d)
        for k in writes:
            s = self.st.get(k)
            if s is None:
                continue
            d = s[0]
            if d is not None and not ((not is_dma) and d[3] == e and e == 'pe'):
                add(d)
            for d in s[1].values():
                if (not is_dma) and d[3] == e and e == 'pe':
                    continue
                add(d)
        return deps.values()

    def _record(self, rec, reads, writes):
        for k in reads:
            s = self.st.get(k)
            if s is None:
                s = [None, {}]
                self.st[k] = s
            s[1][rec[0]] = rec
        for k in writes:
            self.st[k] = [rec, {}]

    def op(self, e, fn, reads=(), writes=(), inc=True):
        reads = _keys(reads)
        writes = _keys(writes)
        for d in self._collect(e, reads, writes, False):
            self._wait(e, d)
        ins = fn()
        self.n_ops += 1
        if inc:
            self.seq[e] += 1
            ins.then_inc(self.sem[e], 1)
            val = self.seq[e]
        else:
            val = self.seq[e] + 1
        rec = ('c_' + e, self.sem[e], val, e)
        self._record(rec, reads, writes)
        return ins

    def dma(self, q, stream, fn, reads=(), writes=()):
        reads = _keys(reads)
        writes = _keys(writes)
        if stream not in self.dstream:
            self.dstream[stream] = [self.nc.semaphore('dsem_' + stream).__enter__(), 0]
        ds = self.dstream[stream]
        for d in self._collect(q, reads, writes, True):
            self._wait(q, d)
        ins = fn()
        self.n_ops += 1
        ds[1] += 16
        ins.then_inc(ds[0], 16)
        rec = ('d_' + stream, ds[0], ds[1], None)
        self._record(rec, reads, writes)
        return ins

    def final_wait(self, e='sp'):
        for name, (sem, cnt) in self.dstream.items():
            if cnt > 0:
                self.eng[e].wait_ge(sem, cnt)


def _keys(items):
    out = []
    for it in items:
        if isinstance(it, View):
            out.extend(it.keys())
        elif isinstance(it, list):
            out.extend(it)
        else:
            out.append(it)
    return out


class Region:
    def __init__(self, nc, name, nbytes, gran=512, psum=False):
        self.name, self.nbytes, self.gran = name, nbytes, gran
        if psum:
            self.t = nc.psum_tensor(name, [128, nbytes // 4], F32).__enter__()
        else:
            self.t = nc.sbuf_tensor(name, [128, nbytes // 4], F32).__enter__()

    def view(self, off, dtype, shape):
        return View(self, off, dtype, shape)


class View:
    def __init__(self, reg, off, dtype, shape):
        self.reg, self.off, self.dtype, self.shape = reg, off, dtype, tuple(shape)
        self.esz = 4 if dtype == F32 else 2
        n = 1
        for s in shape:
            n *= s
        self.n = n
        self.nbytes = n * self.esz
        assert off % 4 == 0 and self.nbytes % 4 == 0, (off, self.nbytes)
        assert off + self.nbytes <= reg.nbytes, (reg.name, off, self.nbytes, reg.nbytes)
        a = reg.t[:, off // 4:(off + self.nbytes) // 4]
        if dtype != F32:
            a = a.bitcast(dtype)
        if len(shape) > 1:
            names = ' '.join('a%d' % i for i in range(len(shape)))
            kw = {'a%d' % i: shape[i] for i in range(len(shape))}
            a = a.rearrange('p (%s) -> p %s' % (names, names), **kw)
        self.ap = a

    def keys(self, lo=0, hi=None):
        if hi is None:
            hi = self.n
        b0 = self.off + lo * self.esz
        b1 = self.off + hi * self.esz
        g = self.reg.gran
        return [(self.reg.name, i) for i in range(b0 // g, (b1 - 1) // g + 1)]

    def sub(self, lo, hi):
        return self.keys(lo, hi)


def build(cfg):
    nc = bass.Bass("TRN2", target_bir_lowering=False)
    T = Trk(nc)
    D, DS, DC, KC, KS, KCV = cfg.D, cfg.DS, cfg.DC, cfg.KC, cfg.KS, cfg.KCV
    NG, GQ, GQP, TU, NU, UPP = cfg.NG, cfg.GQ, cfg.GQP, cfg.TU, cfg.NU, cfg.UPP
    NIC, NMOD = cfg.NIC, cfg.NMOD

    def din(name, shape, dt=F32):
        return nc.dram_tensor(name, list(shape), dt, kind="ExternalInput").ap()

    def dout(name, shape, dt=F32):
        return nc.dram_tensor(name, list(shape), dt, kind="ExternalOutput").ap()

    def dscr(name, shape, dt):
        return nc.dram_tensor(name, list(shape), dt, kind="Internal").ap()

    xb = din("xb", [6, 4, 128, D])
    condT = din("condT", [128, KC, 2])
    w_mod = din("w_mod", [NMOD, 128, KC * 128])
    b_mod = din("b_mod", [128, NMOD])
    w_in = din("w_in", [NIC, 128, KC * 128])
    w_glu = din("w_glu", [KS, 128, KS * 128])
    b_glu = din("b_glu", [128, KS])
    w_ba = din("w_ba", [KC, 128, KS * 128])
    w_bb = din("w_bb", [KC, 128, KCV * 128])
    w_o = din("w_o", [KC, 128, KC * 128])
    conv_w = din("conv_w", [128, 3, KCV])
    conv_b = din("conv_b", [128, KCV])
    ln_g = din("ln_g", [1, D])
    ln_b = din("ln_b", [1, D])
    s5_sm = din("s5_sm", [128, 3, 2 * GQ])
    s5_bc = din("s5_bc", [128, 4, 2 * GQ * 16])
    s5_h0 = din("s5_h0", [128, 2, 2, GQ])
    ssm_dv = din("ssm_dv", [128, NG])
    consts = din("consts", [128, 3, 128])
    qmask = din("qmask", [128, 16])

    y_out = dout("y_out", [3, 4, 128, D])
    st_out = dout("st_out", [2, 128, 2 * 2 * 2 * GQ])

    Mtab = dscr("Mtab", [NU, 128, 8 * 128], BF16)
    Wintab = dscr("Wintab", [NU, 128, 8 * 2 * 2 * 64], BF16)
    Wouttab = dscr("Wouttab", [NU, 128, TU * 2 * 2 * 128], BF16)

    RX = Region(nc, "RX", 32768)
    RH = Region(nc, "RH", 32768)
    NWS = 3
    RW = Region(nc, "RW", NWS * 8192, gran=8192)
    RU = Region(nc, "RU", 16384)
    RS = Region(nc, "RS", 32768)
    RY = Region(nc, "RY", 16384)
    RT = Region(nc, "RT", 2 * 10240, gran=2048)
    RM = Region(nc, "RM", 36608, gran=256)
    PS = Region(nc, "PSR", 16384, gran=2048, psum=True)

    misc_off = [0]

    def malloc(dtype, shape):
        n = 1
        for s in shape:
            n *= s
        nb = n * (4 if dtype == F32 else 2)
        nb = (nb + 255) // 256 * 256
        v = RM.view(misc_off[0], dtype, shape)
        misc_off[0] += nb
        return v

    def psv(bank, dtype, shape, off=0):
        return PS.view(bank * 2048 + off, dtype, shape)

    pe, act, dve, pool = nc.tensor, nc.scalar, nc.vector, nc.gpsimd

    flip = [0]

    def evac_eng():
        flip[0] ^= 1
        return 'act' if flip[0] else 'dve'

    def copy_on(e, out_ap, in_ap, reads, writes):
        if e == 'act':
            T.op('act', lambda: act.copy(out=out_ap, in_=in_ap), reads, writes)
        elif e == 'dve':
            T.op('dve', lambda: dve.tensor_copy(out=out_ap, in_=in_ap), reads, writes)
        else:
            T.op('pool', lambda: pool.tensor_copy(out=out_ap, in_=in_ap), reads, writes)

    def tt(e, out_ap, a, b, op, reads, writes):
        en = dve if e == 'dve' else pool
        T.op(e, lambda: en.tensor_tensor(out=out_ap, in0=a, in1=b, op=op), reads, writes)

    cst = malloc(F32, [3, 128])
    T.dma('sp', 'su1', lambda: nc.sync.dma_start(out=cst.ap, in_=consts), [], [cst])
    ident_f = cst.ap[:, 0, :]
    identb = malloc(BF16, [128])
    T.op('dve', lambda: dve.tensor_copy(out=identb.ap, in_=cst.ap[:, 0, :]), [cst], [identb])
    ident_b = identb.ap
    maskL = cst.ap[:, 1, :]
    maskU = cst.ap[:, 2, :]
    qm = malloc(F32, [16])
    T.dma('sp', 'su2', lambda: nc.sync.dma_start(out=qm.ap, in_=qmask), [], [qm])
    bmod = malloc(F32, [NMOD])
    T.dma('sp', 'su3', lambda: nc.sync.dma_start(out=bmod.ap, in_=b_mod), [], [bmod])
    bglu = malloc(F32, [KS])
    T.dma('sp', 'su4', lambda: nc.sync.dma_start(out=bglu.ap, in_=b_glu), [], [bglu])
    cw = malloc(F32, [3, KCV])
    T.dma('sp', 'su5', lambda: nc.sync.dma_start(out=cw.ap, in_=conv_w), [], [cw])
    cb = malloc(F32, [KCV])
    T.dma('sp', 'su6', lambda: nc.sync.dma_start(out=cb.ap, in_=conv_b), [], [cb])
    dvec = malloc(F32, [NG])
    T.dma('sp', 'su7', lambda: nc.sync.dma_start(out=dvec.ap, in_=ssm_dv), [], [dvec])
    cnd = malloc(F32, [KC, 2])
    T.dma('sp', 'su8', lambda: nc.sync.dma_start(out=cnd.ap, in_=condT), [], [cnd])

    wslot = [0]

    def load_w(src2d, kcn):
        s = wslot[0] % NWS
        wslot[0] += 1
        v = RW.view(s * 8192, BF16, [kcn, 128])
        flat = RW.view(s * 8192, BF16, [kcn * 128])
        T.dma('pool', 'w%d' % s,
              lambda: nc.gpsimd.dma_start(out=flat.ap, in_=src2d, max_dma_last_dim=4096),
              [], [v])
        return v

    pbank = [0]

    def proj(wv, kcn, rhs_view, ncols=TB):
        b = pbank[0] % 4
        pbank[0] += 1
        pv = psv(b, F32, [ncols])
        for kc in range(kcn):
            T.op('pe', lambda kc=kc: pe.matmul(pv.ap, lhsT=wv.ap[:, kc, :], rhs=rhs_view.ap[:, kc, :ncols],
                                               start=(kc == 0), stop=(kc == kcn - 1)),
                 [wv, rhs_view], [pv], inc=(kc == kcn - 1))
        return pv

    csg = malloc(F32, [KC, 2])
    T.op('act', lambda: act.activation(out=csg.ap, in_=cnd.ap, func=AF.Sigmoid), [cnd], [csg])
    csl = malloc(BF16, [KC, 2])
    tt('dve', csl.ap, csg.ap, cnd.ap, ALU.mult, [csg, cnd], [csl])
    modv = malloc(F32, [NMOD, 2])
    def mod_chunk(mc):
        wv = load_w(w_mod[mc], KC)
        pv = proj(wv, KC, csl, ncols=2)
        add1 = 1.0 if (KC <= mc < 2 * KC) else 0.0
        T.op('dve', lambda: dve.tensor_scalar(
            out=modv.ap[:, mc, :], in0=pv.ap, scalar1=bmod.ap[:, mc:mc + 1], scalar2=add1,
            op0=ALU.add, op1=ALU.add), [pv, bmod], [modv.sub(mc * 2, mc * 2 + 2)])

    mod_next = [0]
    mod_per = (NMOD + NU - 1) // NU

    def mod_some(n):
        for _ in range(n):
            if mod_next[0] < NMOD:
                mod_chunk(mod_next[0])
                mod_next[0] += 1

    G2 = 2 * GQ
    sm_in = None
    h0 = malloc(F32, [2, 2, GQ])
    T.dma('sp', 'su10', lambda: nc.sync.dma_start(out=h0.ap, in_=s5_h0), [], [h0])

    su_regs = [[RU, 0], [RY, 0], [RT, 0]]

    def salloc(dtype, shape):
        n = 1
        for s_ in shape:
            n *= s_
        nb = (n * (4 if dtype == F32 else 2) + 511) // 512 * 512
        for rr in su_regs:
            if rr[1] + nb <= rr[0].nbytes:
                v = rr[0].view(rr[1], dtype, shape)
                rr[1] += nb
                return v
        raise RuntimeError("setup scratch exhausted")

    def sm(persist=False):
        return malloc(F32, [G2]) if persist else salloc(F32, [G2])

    sm_in = RX.view(32768 - 2048, F32, [3, G2])
    T.dma('sp', 'su_smin', lambda: nc.sync.dma_start(out=sm_in.ap, in_=s5_sm), [], [sm_in])

    def dv(fn, reads, writes):
        T.op('dve', fn, reads, writes)

    def av(fn, reads, writes):
        T.op('act', fn, reads, writes)

    dt_ = sm(); ar = sm(); th = sm(); er = sm(); kk = sm(); sn = sm(); cs = sm()
    av(lambda: act.activation(out=dt_.ap, in_=sm_in.ap[:, 0, :], func=AF.Exp), [sm_in], [dt_])
    dv(lambda: dve.tensor_tensor(out=ar.ap, in0=sm_in.ap[:, 1, :], in1=dt_.ap, op=ALU.mult), [sm_in, dt_], [ar])
    dv(lambda: dve.tensor_tensor(out=th.ap, in0=sm_in.ap[:, 2, :], in1=dt_.ap, op=ALU.mult), [sm_in, dt_], [th])
    av(lambda: act.activation(out=er.ap, in_=ar.ap, func=AF.Exp), [ar], [er])

    def sin_of(dst, src, shift):
        t1 = sm()
        dv(lambda: dve.tensor_scalar(out=t1.ap, in0=src.ap, scalar1=shift, scalar2=1.0 / TWO_PI,
                                     op0=ALU.add, op1=ALU.mult), [src], [t1])
        dv(lambda: dve.tensor_scalar(out=kk.ap, in0=t1.ap, scalar1=MAGIC, scalar2=None, op0=ALU.add),
           [t1], [kk])
        dv(lambda: dve.tensor_scalar(out=kk.ap, in0=kk.ap, scalar1=-MAGIC, scalar2=None, op0=ALU.add),
           [kk], [kk])
        dv(lambda: dve.tensor_tensor(out=t1.ap, in0=t1.ap, in1=kk.ap, op=ALU.subtract), [t1, kk], [t1])
        dv(lambda: dve.tensor_scalar(out=t1.ap, in0=t1.ap, scalar1=TWO_PI, scalar2=3.14159,
                                     op0=ALU.mult, op1=ALU.min), [t1], [t1])
        dv(lambda: dve.tensor_scalar(out=t1.ap, in0=t1.ap, scalar1=-3.14159, scalar2=None, op0=ALU.max),
           [t1], [t1])
        av(lambda: act.activation(out=dst.ap, in_=t1.ap, func=AF.Sin), [t1], [dst])

    sin_of(sn, th, 0.0)
    sin_of(cs, th, math.pi / 2)

    cm_t = [sm() for _ in range(4)]

    def cmul(outr, outi, ar_, ai_, br_, bi_, n_=None):
        t1, t2, t3, t4 = cm_t
        dv(lambda: dve.tensor_tensor(out=t1.ap, in0=ar_.ap, in1=br_.ap, op=ALU.mult), [ar_, br_], [t1])
        dv(lambda: dve.tensor_tensor(out=t2.ap, in0=ai_.ap, in1=bi_.ap, op=ALU.mult), [ai_, bi_], [t2])
        dv(lambda: dve.tensor_tensor(out=t3.ap, in0=ar_.ap, in1=bi_.ap, op=ALU.mult), [ar_, bi_], [t3])
        dv(lambda: dve.tensor_tensor(out=t4.ap, in0=ai_.ap, in1=br_.ap, op=ALU.mult), [ai_, br_], [t4])
        dv(lambda: dve.tensor_tensor(out=outr.ap, in0=t1.ap, in1=t2.ap, op=ALU.subtract), [t1, t2], [outr])
        dv(lambda: dve.tensor_tensor(out=outi.ap, in0=t3.ap, in1=t4.ap, op=ALU.add), [t3, t4], [outi])

    PWr = [sm(persist=(k_ in (0, 8))) for k_ in range(9)]
    PWi = [sm(persist=(k_ in (0, 8))) for k_ in range(9)]
    dv(lambda: dve.memset(PWr[0].ap, 1.0), [], [PWr[0]])
    dv(lambda: dve.memset(PWi[0].ap, 0.0), [], [PWi[0]])
    dv(lambda: dve.tensor_tensor(out=PWr[1].ap, in0=er.ap, in1=cs.ap, op=ALU.mult), [er, cs], [PWr[1]])
    dv(lambda: dve.tensor_tensor(out=PWi[1].ap, in0=er.ap, in1=sn.ap, op=ALU.mult), [er, sn], [PWi[1]])
    for k in range(2, 9):
        cmul(PWr[k], PWi[k], PWr[k - 1], PWi[k - 1], PWr[1], PWi[1])
    nr = sm(); den = sm(); qr = sm(); qi = sm(); t5 = sm(); t6 = sm()
    a_re = sm_in.ap[:, 1, :]
    a_im = sm_in.ap[:, 2, :]
    dv(lambda: dve.tensor_scalar(out=nr.ap, in0=PWr[1].ap, scalar1=-1.0, scalar2=None, op0=ALU.add), [PWr[1]], [nr])
    dv(lambda: dve.tensor_tensor(out=den.ap, in0=a_re, in1=a_re, op=ALU.mult), [sm_in], [den])
    dv(lambda: dve.tensor_tensor(out=t5.ap, in0=a_im, in1=a_im, op=ALU.mult), [sm_in], [t5])
    dv(lambda: dve.tensor_tensor(out=den.ap, in0=den.ap, in1=t5.ap, op=ALU.add), [den, t5], [den])
    dv(lambda: dve.reciprocal(out=den.ap, in_=den.ap), [den], [den])
    dv(lambda: dve.tensor_tensor(out=t5.ap, in0=nr.ap, in1=a_re, op=ALU.mult), [nr, sm_in], [t5])
    dv(lambda: dve.tensor_tensor(out=t6.ap, in0=PWi[1].ap, in1=a_im, op=ALU.mult), [PWi[1], sm_in], [t6])
    dv(lambda: dve.tensor_tensor(out=t5.ap, in0=t5.ap, in1=t6.ap, op=ALU.add), [t5, t6], [t5])
    dv(lambda: dve.tensor_tensor(out=qr.ap, in0=t5.ap, in1=den.ap, op=ALU.mult), [t5, den], [qr])
    dv(lambda: dve.tensor_tensor(out=t5.ap, in0=PWi[1].ap, in1=a_re, op=ALU.mult), [PWi[1], sm_in], [t5])
    dv(lambda: dve.tensor_tensor(out=t6.ap, in0=nr.ap, in1=a_im, op=ALU.mult), [nr, sm_in], [t6])
    dv(lambda: dve.tensor_tensor(out=t5.ap, in0=t5.ap, in1=t6.ap, op=ALU.subtract), [t5, t6], [t5])
    dv(lambda: dve.tensor_tensor(out=qi.ap, in0=t5.ap, in1=den.ap, op=ALU.mult), [t5, den], [qi])
    ivr = sm(); ivi = sm()
    dv(lambda: dve.tensor_tensor(out=t5.ap, in0=PWr[8].ap, in1=PWr[8].ap, op=ALU.mult), [PWr[8]], [t5])
    dv(lambda: dve.tensor_tensor(out=t6.ap, in0=PWi[8].ap, in1=PWi[8].ap, op=ALU.mult), [PWi[8]], [t6])
    dv(lambda: dve.tensor_tensor(out=t5.ap, in0=t5.ap, in1=t6.ap, op=ALU.add), [t5, t6], [t5])
    dv(lambda: dve.reciprocal(out=t5.ap, in_=t5.ap), [t5], [t5])
    dv(lambda: dve.tensor_tensor(out=ivr.ap, in0=PWr[8].ap, in1=t5.ap, op=ALU.mult), [PWr[8], t5], [ivr])
    dv(lambda: dve.scalar_tensor_tensor(out=ivi.ap, in0=PWi[8].ap, scalar=-1.0, in1=t5.ap,
                                        op0=ALU.mult, op1=ALU.mult), [PWi[8], t5], [ivi])

    PWin = [salloc(F32, [2, GQ, 8]) for _ in range(2)]
    PWout = [salloc(F32, [2, GQ, 8]) for _ in range(2)]
    PZ = [salloc(F32, [2, GQ, 8]) for _ in range(2)]
    for ri, PWx in ((0, PWr), (1, PWi)):
        for s in range(8):
            for d in range(2):
                kin = (7 - s) if d == 0 else s
                kout = (s + 1) if d == 0 else (8 - s)
                src_in = PWx[kin].ap[:, d * GQ:(d + 1) * GQ]
                src_out = PWx[kout].ap[:, d * GQ:(d + 1) * GQ]
                T.op('pool', lambda ri=ri, d=d, s=s, src_in=src_in: pool.tensor_copy(
                    out=PWin[ri].ap[:, d, :, s], in_=src_in), [PWx[kin]], [PWin[ri]])
                T.op('pool', lambda ri=ri, d=d, s=s, src_out=src_out: pool.tensor_copy(
                    out=PWout[ri].ap[:, d, :, s], in_=src_out), [PWx[kout]], [PWout[ri]])
    ivr3 = ivr.ap.rearrange("p (d g) -> p d g", d=2).unsqueeze(3).to_broadcast([128, 2, GQ, 8])
    ivi3 = ivi.ap.rearrange("p (d g) -> p d g", d=2).unsqueeze(3).to_broadcast([128, 2, GQ, 8])
    z1 = salloc(F32, [2, GQ, 8]); z2 = salloc(F32, [2, GQ, 8])
    tt('dve', z1.ap, PWout[0].ap, ivr3, ALU.mult, [PWout[0], ivr], [z1])
    tt('dve', z2.ap, PWout[1].ap, ivi3, ALU.mult, [PWout[1], ivi], [z2])
    tt('dve', PZ[0].ap, z1.ap, z2.ap, ALU.subtract, [z1, z2], [PZ[0]])
    tt('dve', z1.ap, PWout[0].ap, ivi3, ALU.mult, [PWout[0], ivi], [z1])
    tt('dve', z2.ap, PWout[1].ap, ivr3, ALU.mult, [PWout[1], ivr], [z2])
    tt('dve', PZ[1].ap, z1.ap, z2.ap, ALU.add, [z1, z2], [PZ[1]])

    L8r = PWr[8].ap.rearrange("p (d g) -> p d g", d=2)
    L8i = PWi[8].ap.rearrange("p (d g) -> p d g", d=2)
    LrrN = {}
    LiiN = {}
    for ns_ in (1, 2):
        LrrN[ns_] = malloc(F32, [2, GQ, ns_, 2])
        LiiN[ns_] = malloc(F32, [2, GQ, ns_, 2])
        for s_ in range(ns_):
            for ri in range(2):
                dv(lambda ns_=ns_, s_=s_, ri=ri: dve.tensor_copy(out=LrrN[ns_].ap[:, :, :, s_, ri], in_=L8r),
                   [PWr[8]], [LrrN[ns_]])
            dv(lambda ns_=ns_, s_=s_: dve.tensor_scalar(out=LiiN[ns_].ap[:, :, :, s_, 0], in0=L8i, scalar1=-1.0,
                                                        scalar2=None, op0=ALU.mult), [PWi[8]], [LiiN[ns_]])
            dv(lambda ns_=ns_, s_=s_: dve.tensor_copy(out=LiiN[ns_].ap[:, :, :, s_, 1], in_=L8i), [PWi[8]], [LiiN[ns_]])

    curR, curI = PWr[8], PWi[8]
    pp = [(sm(), sm()), (sm(), sm())]
    bp1 = (sm(True), sm(True))
    for it in range(6):
        nr_, ni_ = bp1 if it == 5 else pp[it % 2]
        cmul(nr_, ni_, curR, curI, curR, curI)
        curR, curI = nr_, ni_
    BPr = [PWr[0], curR, sm(True), sm(True)]
    BPi = [PWi[0], curI, sm(True), sm(True)]
    cmul(BPr[2], BPi[2], curR, curI, curR, curI)
    cmul(BPr[3], BPi[3], BPr[2], BPi[2], curR, curI)

    bc_in = RS.view(0, F32, [4, 2, GQ, 16])
    T.dma('sp', 'su11', lambda: nc.sync.dma_start(out=RS.view(0, F32, [4, G2 * 16]).ap, in_=s5_bc), [], [bc_in])
    BB = RH.view(0, F32, [2, 2, GQ, 16])
    tb1 = RH.view(16384, F32, [2, GQ, 16])
    tb2 = RH.view(24576, F32, [2, GQ, 16])
    q3r = qr.ap.rearrange("p (d g) -> p d g", d=2).unsqueeze(3).to_broadcast([128, 2, GQ, 16])
    q3i = qi.ap.rearrange("p (d g) -> p d g", d=2).unsqueeze(3).to_broadcast([128, 2, GQ, 16])
    tt('dve', tb1.ap, bc_in.ap[:, 0], q3r, ALU.mult, [bc_in, qr], [tb1])
    tt('dve', tb2.ap, bc_in.ap[:, 1], q3i, ALU.mult, [bc_in, qi], [tb2])
    tt('dve', BB.ap[:, 0], tb1.ap, tb2.ap, ALU.subtract, [tb1, tb2], [BB])
    tt('dve', tb1.ap, bc_in.ap[:, 0], q3i, ALU.mult, [bc_in, qi], [tb1])
    tt('dve', tb2.ap, bc_in.ap[:, 1], q3r, ALU.mult, [bc_in, qr], [tb2])
    tt('dve', BB.ap[:, 1], tb1.ap, tb2.ap, ALU.add, [tb1, tb2], [BB])

    uo = [0]

    def rxv(dtype, shape):
        n = 1
        for s_ in shape:
            n *= s_
        nb = n * (4 if dtype == F32 else 2)
        v = RX.view(uo[0], dtype, shape)
        uo[0] += (nb + 511) // 512 * 512
        return v

    m1 = rxv(F32, [2, TU, 8, 16]); m2 = rxv(F32, [2, TU, 8, 16])
    A_t = [rxv(BF16, [2, TU, 128]) for _ in range(2)]
    Wo_t = rxv(BF16, [TU, 2, 2, 128])
    Z_t = [rxv(BF16, [2, TU, 128]) for _ in range(2)]
    Mt_sb = rxv(BF16, [8, 128])
    Win_sb = rxv(BF16, [8, 2, 2, 64])
    mtmp = rxv(F32, [128])
    assert uo[0] <= 32768 - 2048

    def ctable(outr_ap, outi_ap, neg_im, pwr, pwi, br_ap, bi_ap, u, w_keys, rk):
        g0 = u * TU
        for d in range(2):
            pr = pwr.ap[:, d, g0:g0 + TU, :].unsqueeze(3).to_broadcast([128, TU, 8, 16])
            pi_ = pwi.ap[:, d, g0:g0 + TU, :].unsqueeze(3).to_broadcast([128, TU, 8, 16])
            br = br_ap[:, d, g0:g0 + TU, :].unsqueeze(2).to_broadcast([128, TU, 8, 16])
            bi = bi_ap[:, d, g0:g0 + TU, :].unsqueeze(2).to_broadcast([128, TU, 8, 16])
            m1d = m1.ap[:, d]
            m2d = m2.ap[:, d]
            hsz = TU * 8 * 16
            k1 = m1.sub(d * hsz, (d + 1) * hsz)
            k2 = m2.sub(d * hsz, (d + 1) * hsz)
            tt('dve', m1d, pr, br, ALU.mult, rk, [k1])
            tt('pool', m2d, pi_, bi, ALU.mult, rk, [k2])
            tt('dve', outr_ap(d), m1d, m2d, ALU.subtract, [k1, k2], w_keys)
            tt('dve', m1d, pr, bi, ALU.mult, rk, [k1])
            tt('pool', m2d, pi_, br, ALU.mult, rk, [k2])
            if neg_im:
                T.op('dve', lambda d=d, m1d=m1d, m2d=m2d: dve.scalar_tensor_tensor(
                    out=outi_ap(d), in0=m1d, scalar=-1.0, in1=m2d, op0=ALU.mult, op1=ALU.subtract), [k1, k2], w_keys)
            else:
                tt('dve', outi_ap(d), m1d, m2d, ALU.add, [k1, k2], w_keys)

    def v5(view2):
        return lambda d: view2.ap[:, d].rearrange("p g (k c) -> p g k c", k=8)

    for u in range(NU):
        g0 = u * TU
        ctable(v5(A_t[0]), v5(A_t[1]), False, PWin[0], PWin[1], BB.ap[:, 0], BB.ap[:, 1], u,
               [A_t[0], A_t[1]], [PWin[0], PWin[1], BB])
        wo_r = lambda d: Wo_t.ap[:, :, d, 0, :].rearrange("p g (k c) -> p g k c", k=8)
        wo_i = lambda d: Wo_t.ap[:, :, d, 1, :].rearrange("p g (k c) -> p g k c", k=8)
        ctable(wo_r, wo_i, True, PWout[0], PWout[1], bc_in.ap[:, 2], bc_in.ap[:, 3], u,
               [Wo_t], [PWout[0], PWout[1], bc_in])
        ctable(v5(Z_t[0]), v5(Z_t[1]), True, PZ[0], PZ[1], bc_in.ap[:, 2], bc_in.ap[:, 3], u,
               [Z_t[0], Z_t[1]], [PZ[0], PZ[1], bc_in])
        T.dma('sp', 'tabw', lambda u=u: nc.sync.dma_start(
            out=Wouttab[u], in_=RX.view(Wo_t.off, BF16, [TU * 2 * 2 * 128]).ap), [Wo_t], [('Wouttab',)])
        for gl in range(8):
            gql, gp = gl // 2, gl % 2
            g = (g0 + gql) * 2 + gp
            ps_ = slice(gp * 64, gp * 64 + 64)
            pf = psv(4, F32, [128]); pb = psv(5, F32, [128])
            for d, pv in ((0, pf), (1, pb)):
                T.op('pe', lambda d=d, pv=pv: pe.matmul(pv.ap, lhsT=A_t[0].ap[ps_, d, gql, :],
                                                        rhs=Z_t[0].ap[ps_, d, gql, :], start=True, stop=False),
                     [A_t[0], Z_t[0]], [pv], inc=False)
                T.op('pe', lambda d=d, pv=pv: pe.matmul(pv.ap, lhsT=A_t[1].ap[ps_, d, gql, :],
                                                        rhs=Z_t[1].ap[ps_, d, gql, :], start=False, stop=True),
                     [A_t[1], Z_t[1]], [pv])
            tt('dve', mtmp.ap, pf.ap, maskL, ALU.mult, [pf, cst], [mtmp])
            T.op('dve', lambda g=g: dve.scalar_tensor_tensor(out=mtmp.ap, in0=ident_f, scalar=dvec.ap[:, g:g + 1],
                                                            in1=mtmp.ap, op0=ALU.mult, op1=ALU.add),
                 [mtmp, cst, dvec], [mtmp])
            m3 = m1
            m3ap = RX.view(m1.off, F32, [128]).ap
            tt('dve', m3ap, pb.ap, maskU, ALU.mult, [pb, cst], [m1])
            tt('dve', Mt_sb.ap[:, gl, :], mtmp.ap, m3ap, ALU.add, [mtmp, m1], [Mt_sb])
            for d in range(2):
                for ri in range(2):
                    pt = psv(6, BF16, [2, 2, 64])
                    T.op('pe', lambda d=d, ri=ri, pt=pt: pe.transpose(
                        pt.ap[:, d, ri, :], A_t[ri].ap[ps_, d, gql, :], ident_b[ps_, ps_]),
                        [A_t[ri], identb], [pt], inc=(d == 1 and ri == 1))
            copy_on('act', Win_sb.ap[:, gl], psv(6, BF16, [2, 2, 64]).ap, [psv(6, BF16, [2, 2, 64])], [Win_sb])
        T.dma('sp', 'tabm', lambda u=u: nc.sync.dma_start(
            out=Mtab[u], in_=RX.view(Mt_sb.off, BF16, [8 * 128]).ap), [Mt_sb], [('Mtab',)])
        T.dma('sp', 'tabi', lambda u=u: nc.sync.dma_start(
            out=Wintab[u], in_=RX.view(Win_sb.off, BF16, [8 * 2 * 2 * 64]).ap), [Win_sb], [('Wintab',)])
        mod_some(mod_per)
    mod_some(NMOD)

    hT = RH.view(0, BF16, [KC, TB])
    Utm = RU.view(0, BF16, [NG, 64])
    yaT = RU.view(0, BF16, [KS, TB])
    yT = RY.view(0, BF16, [KS, TB])
    ybT = RY.view(0, BF16, [KCV, TB])
    mergedT = RS.view(0, BF16, [KC, TB])
    Sv = RX.view(0, F32, [2, GQP, NMC, 2])
    Ug = RS.view(0, BF16, [GQP * 2, NMC])
    Hb = RS.view(8192, BF16, [2, 2, GQP, NMC])
    Ytm2 = [RS.view(24576 + i_ * 2048, BF16, [4, 128]) for i_ in range(2)]
    Yg2 = [RS.view(24576 + 1024 + i_ * 2048, BF16, [8, NMC]) for i_ in range(2)]
    XT = [RX.view(0, F32, [D]), RX.view(16384, F32, [D]), RH.view(0, F32, [D]), RH.view(16384, F32, [D])]
    lng_bc = RU.view(0, F32, [D])
    lnb_bc = RY.view(0, F32, [D])
    STt = malloc(F32, [2, 2, 2, GQ])
    Ecar = malloc(F32, [3, 2, GQ, 2])
    Fcar = malloc(F32, [2, GQ, 2])
    tmpA = [malloc(F32, [TB]) for _ in range(4)]
    utmp = [malloc(BF16, [TB]) for _ in range(2)]
    stats = malloc(F32, [8, 6])
    mv = malloc(F32, [2])
    rstd = malloc(F32, [1])
    nmr = malloc(F32, [1])
    sP = RM.view(tmpA[2].off, F32, [2, GQP * 2, 2]); sQ = RM.view(tmpA[3].off, F32, [2, GQP * 2, 2])

    tsl = [0, 0]

    def load_tables(u, which):
        if which == 'front':
            sl = tsl[0] % 2
            tsl[0] += 1
            base = sl * 4096
            Wiv = RT.view(base, BF16, [8, 2, 2, 64])
            T.dma('sp', 'tf%d' % sl, lambda: nc.sync.dma_start(
                out=RT.view(base, BF16, [8 * 2 * 2 * 64]).ap, in_=Wintab[u]), [('Wintab',)], [Wiv])
            return None, Wiv, None
        sl = tsl[1] % 2
        tsl[1] += 1
        base = 8192 + sl * 6144
        Mv = RT.view(base, BF16, [8, 128])
        Wov = RT.view(base + 2048, BF16, [TU, 2, 2, 128])
        T.dma('sp', 'tb%d' % sl, lambda: nc.sync.dma_start(
            out=RT.view(base, BF16, [8 * 128]).ap, in_=Mtab[u]), [('Mtab',)], [Mv, Wov])
        T.dma('sp', 'tb%d' % sl, lambda: nc.sync.dma_start(
            out=RT.view(base + 2048, BF16, [TU * 2 * 2 * 128]).ap, in_=Wouttab[u]), [('Wouttab',)], [Mv, Wov])
        return Mv, None, Wov

    def ln_stats(xv):
        nch = max(1, D // 512)
        cw_ = D // nch
        for c in range(nch):
            T.op('dve', lambda c=c: dve.bn_stats(out=stats.ap[:, c, :], in_=xv.ap[:, c * cw_:(c + 1) * cw_]),
                 [xv], [stats])
        T.op('dve', lambda: dve.bn_aggr(out=mv.ap, in_=stats.ap[:, :nch, :].rearrange('p a b -> p (a b)')), [stats], [mv])
        T.op('act', lambda: act.activation(out=rstd.ap, in_=mv.ap[:, 1:2], func=AF.Sqrt, bias=epsv.ap[:, 0:1]),
             [mv, epsv], [rstd])
        T.op('dve', lambda: dve.reciprocal(out=rstd.ap, in_=rstd.ap), [rstd], [rstd])
        T.op('dve', lambda: dve.scalar_tensor_tensor(out=nmr.ap, in0=mv.ap[:, 0:1], scalar=-1.0, in1=rstd.ap,
                                                     op0=ALU.mult, op1=ALU.mult), [mv, rstd], [nmr])
        T.op('act', lambda: act.activation(out=xv.ap, in_=xv.ap, func=AF.Identity, bias=nmr.ap[:, 0:1],
                                           scale=rstd.ap[:, 0:1]), [xv, nmr, rstd], [xv])

    epsv = malloc(F32, [1])
    dv(lambda: dve.memset(epsv.ap, LN_EPS), [], [epsv])

    def stage_A(blk, cond_i):
        def load_ln(j):
            xv = XT[j % 2]
            T.dma('sp', 'x%d' % (j % 2), lambda: nc.sync.dma_start(out=xv.ap, in_=xb[blk, j]), [], [xv])
            ln_stats(xv)

        def transposes(j):
            xv = XT[j % 2]
            for kc in range(KC):
                pv = psv(4 + (kc // 4) % 2, F32, [128], off=(kc % 4) * 512)
                T.op('pe', lambda kc=kc, pv=pv: pe.transpose(pv.ap, xv.ap[:, kc * 128:(kc + 1) * 128], ident_f),
                     [xv.sub(kc * 128, kc * 128 + 128), cst], [pv])
                dst = hT.ap[:, kc, j * 128:(j + 1) * 128]
                wk = hT.sub(kc * TB + j * 128, kc * TB + j * 128 + 128)
                sc = modv.ap[:, KC + kc, cond_i:cond_i + 1]
                sh = modv.ap[:, kc, cond_i:cond_i + 1]
                if evac_eng() == 'act':
                    T.op('act', lambda pv=pv, dst=dst, sc=sc, sh=sh: act.activation(
                        out=dst, in_=pv.ap, func=AF.Identity, bias=sh, scale=sc), [pv, modv], [wk])
                else:
                    T.op('dve', lambda pv=pv, dst=dst, sc=sc, sh=sh: dve.tensor_scalar(
                        out=dst, in0=pv.ap, scalar1=sc, scalar2=sh, op0=ALU.mult, op1=ALU.add), [pv, modv], [wk])

        load_ln(0)
        for j in range(4):
            if j + 1 < 4:
                load_ln(j + 1)
            transposes(j)

    def stage_U():
        for ci in range(KS):
            wv = load_w(w_in[ci], KC)
            pv = proj(wv, KC, hT)
            ut = utmp[ci % 2]
            copy_on('act', ut.ap, pv.ap, [pv], [ut])
            pt = psv(4 + ci % 2, BF16, [4, 128])
            for j in range(4):
                T.op('pe', lambda j=j, pt=pt, ut=ut: pe.transpose(pt.ap[:, j, :], ut.ap[:, j * 128:(j + 1) * 128], ident_b),
                     [ut, identb], [pt], inc=(j == 3))
            wk = Utm.sub(ci * 8 * 64, (ci + 1) * 8 * 64)
            T.op('dve', lambda ci=ci, pt=pt: dve.tensor_copy(
                out=Utm.ap[:, ci * 8:(ci + 1) * 8, :].rearrange("p g (j c) -> p g j c", j=4),
                in_=pt.ap.rearrange("p j (g c) -> p g j c", g=8)), [pt], [wk])

    def s5_front(ps_i, tabs, nseq):
        for ul in range(UPP):
            u = ps_i * UPP + ul
            Mv, Wiv, Wov = tabs[ul]
            pu = psv(6 + ul % 2, BF16, [8, NMC])
            for gl in range(8):
                g = u * 8 + gl
                for h in range(2):
                    hs = slice(h * 64, h * 64 + 64)
                    src = Utm.ap[hs, g, :]
                    T.op('pe', lambda gl=gl, hs=hs, src=src, pu=pu: pe.transpose(
                        pu.ap[hs, gl, :], src, ident_b[hs, hs]), [Utm, identb], [pu], inc=(gl == 7 and h == 1))
            ugd = Ug.ap[:, ul * 8:(ul + 1) * 8, :]
            ugk = Ug.sub(ul * 8 * NMC, (ul + 1) * 8 * NMC)
            copy_on('act', ugd, pu.ap, [pu], [ugk])
            pS = PS.view(0 * 2048 + (ul % 2) * 4096, F32, [2, 2, TU, NMC])
            for gl in range(8):
                gql, gp = gl // 2, gl % 2
                ps_ = slice(gp * 64, gp * 64 + 64)
                for d in range(2):
                    for ri in range(2):
                        last = (gl == 7 and d == 1 and ri == 1)
                        T.op('pe', lambda gl=gl, gql=gql, d=d, ri=ri, ul=ul, pS=pS, ps_=ps_, Wiv=Wiv: pe.matmul(
                            pS.ap[ps_, d, ri, gql, :], lhsT=Wiv.ap[:, gl, d, ri, :], rhs=Ug.ap[:, ul * 8 + gl, :],
                            start=True, stop=True), [Wiv, ugk], [pS], inc=last)
            nl_ = NMC // nseq
            wk = []
            for d in range(2):
                o = (d * GQP + ul * TU) * NMC * 2
                wk += Sv.sub(o, o + TU * NMC * 2)
            dst0 = Sv.ap[:, 0, ul * TU:(ul + 1) * TU, :, :]
            src0 = pS.ap[:, 0].rearrange("p r g n -> p g n r")
            T.op('act', lambda dst0=dst0, src0=src0: act.copy(out=dst0, in_=src0), [pS], [wk])
            for sq in range(nseq):
                dst1 = Sv.ap[:, 1, ul * TU:(ul + 1) * TU, sq * nl_:(sq + 1) * nl_, :]
                src1 = pS.ap[:, 1, :, :, sq * nl_:(sq + 1) * nl_].rearrange("p r g n -> p g n r")[:, :, ::-1, :]
                T.op('dve', lambda dst1=dst1, src1=src1: dve.tensor_copy(out=dst1, in_=src1), [pS], [wk])

    def s5_scan(ps_i, nseq, carry):
        nl = NMC // nseq
        g0 = ps_i * GQP
        S6 = Sv.ap.rearrange("p d g (s k) r -> p d (g s) k r", s=nseq)
        lrr = LrrN[nseq].ap.rearrange("p d g s r -> p d (g s) r")[:, :, g0 * nseq:(g0 + GQP) * nseq, :]
        lii = LiiN[nseq].ap.rearrange("p d g s r -> p d (g s) r")[:, :, g0 * nseq:(g0 + GQP) * nseq, :]
        W = GQP * nseq
        p_ = sP.ap[:, :, :W, :]
        q_ = sQ.ap[:, :, :W, :]
        for k in range(nl):
            cur = S6[:, :, :, k, :]
            if k == 0:
                if carry is None:
                    continue
                prev = carry.ap[:, :, g0:g0 + GQP, :]
                prev_sw = carry.ap[:, :, g0:g0 + GQP, ::-1]
                rk = [carry]
            else:
                prev = S6[:, :, :, k - 1, :]
                prev_sw = S6[:, :, :, k - 1, ::-1]
                rk = [Sv]
            tt('dve', p_, lrr, prev, ALU.mult, rk + [LrrN[nseq]], [sP])
            tt('dve', q_, lii, prev_sw, ALU.mult, rk + [LiiN[nseq]], [sQ])
            tt('dve', p_, p_, q_, ALU.add, [sP, sQ], [sP])
            tt('dve', cur, cur, p_, ALU.add, [Sv, sP], [Sv])

    def s5_back(ps_i, nseq, carry, tabs):
        nl = NMC // nseq
        g0 = ps_i * GQP
        S6 = Sv.ap.rearrange("p d g (s k) r -> p d g s k r", s=nseq)
        H6 = Hb.ap.rearrange("p d r g (s n) -> p d r g s n", s=nseq)
        for r in range(2):
            T.op('act', lambda r=r: act.copy(out=H6[:, 0, r, :, :, 1:], in_=S6[:, 0, :, :, :nl - 1, r]), [Sv], [Hb])
            T.op('dve', lambda r=r: dve.tensor_copy(out=H6[:, 1, r, :, :, :nl - 1], in_=S6[:, 1, :, :, nl - 2::-1, r]), [Sv], [Hb])
        Hm = Hb.ap.rearrange("p d r g (s n) -> p d (r g) s n", s=nseq)
        if carry is None:
            T.op('dve', lambda: dve.memset(Hm[:, 0, :, :, 0], 0.0), [], [Hb])
            T.op('dve', lambda: dve.memset(Hm[:, 1, :, :, nl - 1], 0.0), [], [Hb])
        else:
            for r in range(2):
                T.op('dve', lambda r=r: dve.tensor_copy(out=H6[:, 0, r, :, 0, 0], in_=carry.ap[:, 0, g0:g0 + GQP, r]), [carry], [Hb])
                T.op('dve', lambda r=r: dve.tensor_copy(out=H6[:, 1, r, :, 0, nl - 1], in_=carry.ap[:, 1, g0:g0 + GQP, r]), [carry], [Hb])
        for ul in range(UPP):
            u = ps_i * UPP + ul
            Mv, Wiv, Wov = tabs[ul]
            ugk = Ug.sub(ul * 8 * NMC, (ul + 1) * 8 * NMC)
            pY = psv(ul % 2, F32, [8, NMC])
            for gl in range(8):
                gql, gp = gl // 2, gl % 2
                gq_l = ul * TU + gql
                ps_ = slice(gp * 64, gp * 64 + 64)
                T.op('pe', lambda gl=gl, pY=pY, Mv=Mv: pe.matmul(pY.ap[:, gl, :], lhsT=Mv.ap[:, gl, :],
                                                                 rhs=Ug.ap[:, ul * 8 + gl, :], start=True, stop=False),
                     [Mv, ugk], [pY], inc=False)
                for d in range(2):
                    for ri in range(2):
                        last = (d == 1 and ri == 1)
                        T.op('pe', lambda gl=gl, gql=gql, d=d, ri=ri, pY=pY, Wov=Wov, ps_=ps_, gq_l=gq_l, last=last: pe.matmul(
                            pY.ap[:, gl, :], lhsT=Wov.ap[ps_, gql, d, ri, :], rhs=Hb.ap[ps_, d, ri, gq_l, :],
                            start=False, stop=last), [Wov, Hb], [pY], inc=(last and gl == 7))
            x = PS.view(pY.off, F32, [8 * NMC])
            t0, t1 = (tmpA[0], tmpA[1]) if ul % 2 == 0 else (tmpA[2], tmpA[3])
            Ytm = Ytm2[ul % 2]
            Yg = Yg2[ul % 2]
            T.op('act', lambda: act.activation(out=t0.ap, in_=x.ap, func=AF.Square), [x], [t0])
            T.op('dve', lambda: dve.tensor_scalar(out=t0.ap, in0=t0.ap, scalar1=0.044715, scalar2=1.0,
                                                  op0=ALU.mult, op1=ALU.add), [t0], [t0])
            tt('dve', t0.ap, t0.ap, x.ap, ALU.mult, [t0, x], [t0])
            T.op('act', lambda: act.activation(out=t1.ap, in_=t0.ap, func=AF.Sigmoid, scale=1.5957691216), [t0], [t1])
            tt('dve', Yg.ap.rearrange("p g n -> p (g n)"), t1.ap, x.ap, ALU.mult, [t1, x], [Yg])
            pt = psv(6 + ul % 2, BF16, [8, 4, 16])
            for gl in range(8):
                for h in range(2):
                    hs = slice(h * 64, h * 64 + 64)
                    T.op('pe', lambda gl=gl, hs=hs, pt=pt: pe.transpose(
                        pt.ap[hs, gl, :, :], Yg.ap[hs, gl, :], ident_b[hs, hs]), [Yg, identb], [pt],
                        inc=(gl == 7 and h == 1))
            T.op('dve', lambda pt=pt: dve.tensor_copy(out=Ytm.ap.rearrange("p j (g c) -> p g j c", g=8), in_=pt.ap),
                 [pt], [Ytm])
            pf = psv(4 + ul % 2, BF16, [4, 128])
            for j in range(4):
                T.op('pe', lambda j=j, pf=pf: pe.transpose(pf.ap[:, j, :], Ytm.ap[:, j, :], ident_b),
                     [Ytm, identb], [pf], inc=(j == 3))
            copy_on('act', yT.ap[:, u, :], pf.ap.rearrange("p j c -> p (j c)"), [pf], [yT.sub(u * TB, (u + 1) * TB)])

    def save_states(ps_i, nseq):
        nl = NMC // nseq
        gsl = slice(ps_i * GQP, (ps_i + 1) * GQP)
        S6 = Sv.ap.rearrange("p d g (s k) r -> p d g s k r", s=nseq)
        for sq in range(nseq):
            for d in range(2):
                T.op('dve', lambda sq=sq, d=d: dve.tensor_copy(
                    out=STt.ap[:, sq, d, :, gsl], in_=S6[:, d, :, sq, nl - 1, :].rearrange("p g r -> p r g")),
                    [Sv], [STt])

    def save_E(ps_i, slot):
        gsl = slice(ps_i * GQP, (ps_i + 1) * GQP)
        T.op('dve', lambda: dve.tensor_copy(out=Ecar.ap[:, slot, :, gsl, :], in_=Sv.ap[:, :, :, NMC - 1, :]), [Sv], [Ecar])

    def compute_carry():
        cc_ = [RM.view(tmpA[0].off + i_ * 256, F32, [GQ]) for i_ in range(8)]
        accr, acci, c1, c2, c3, c4, cpr, cpi = cc_
        for d in range(2):
            ds_ = slice(d * GQ, (d + 1) * GQ)

            def mac(pr_ap, pi_ap, er_ap, ei_ap, msk, first, rk):
                tt('dve', c1.ap, pr_ap, er_ap, ALU.mult, rk, [c1])
                tt('dve', c2.ap, pi_ap, ei_ap, ALU.mult, rk, [c2])
                tt('dve', c3.ap, pr_ap, ei_ap, ALU.mult, rk, [c3])
                tt('dve', c4.ap, pi_ap, er_ap, ALU.mult, rk, [c4])
                tt('dve', c1.ap, c1.ap, c2.ap, ALU.subtract, [c1, c2], [c1])
                tt('dve', c3.ap, c3.ap, c4.ap, ALU.add, [c3, c4], [c3])
                if msk is None:
                    if first:
                        dv(lambda: dve.tensor_copy(out=accr.ap, in_=c1.ap), [c1], [accr])
                        dv(lambda: dve.tensor_copy(out=acci.ap, in_=c3.ap), [c3], [acci])
                    else:
                        tt('dve', accr.ap, accr.ap, c1.ap, ALU.add, [accr, c1], [accr])
                        tt('dve', acci.ap, acci.ap, c3.ap, ALU.add, [acci, c3], [acci])
                else:
                    dv(lambda: dve.scalar_tensor_tensor(out=accr.ap, in0=c1.ap, scalar=msk, in1=accr.ap,
                                                        op0=ALU.mult, op1=ALU.add), [c1, accr, qm], [accr])
                    dv(lambda: dve.scalar_tensor_tensor(out=acci.ap, in0=c3.ap, scalar=msk, in1=acci.ap,
                                                        op0=ALU.mult, op1=ALU.add), [c3, acci, qm], [acci])

            ohb = 6 + 4 * d
            dv(lambda: dve.tensor_scalar(out=cpr.ap, in0=BPr[0].ap[:, ds_], scalar1=qm.ap[:, ohb:ohb + 1], scalar2=None,
                                         op0=ALU.mult), [BPr[0], qm], [cpr])
            dv(lambda: dve.tensor_scalar(out=cpi.ap, in0=BPi[0].ap[:, ds_], scalar1=qm.ap[:, ohb:ohb + 1], scalar2=None,
                                         op0=ALU.mult), [BPi[0], qm], [cpi])
            for i in range(1, 4):
                dv(lambda i=i: dve.scalar_tensor_tensor(out=cpr.ap, in0=BPr[i].ap[:, ds_], scalar=qm.ap[:, ohb + i:ohb + i + 1],
                                                        in1=cpr.ap, op0=ALU.mult, op1=ALU.add), [BPr[i], qm, cpr], [cpr])
                dv(lambda i=i: dve.scalar_tensor_tensor(out=cpi.ap, in0=BPi[i].ap[:, ds_], scalar=qm.ap[:, ohb + i:ohb + i + 1],
                                                        in1=cpi.ap, op0=ALU.mult, op1=ALU.add), [BPi[i], qm, cpi], [cpi])
            mac(cpr.ap, cpi.ap, h0.ap[:, d, 0, :], h0.ap[:, d, 1, :], None, True, [cpr, cpi, h0])
            for j in range(1, 4):
                e = (3 - j) if d == 0 else (j - 1)
                mi = (j - 1) + 3 * d
                mac(BPr[e].ap[:, ds_], BPi[e].ap[:, ds_], Ecar.ap[:, j - 1, d, :, 0], Ecar.ap[:, j - 1, d, :, 1],
                    qm.ap[:, mi:mi + 1], False, [BPr[e], BPi[e], Ecar])
            dv(lambda d=d: dve.tensor_copy(out=Fcar.ap[:, d, :, 0], in_=accr.ap), [accr], [Fcar])
            dv(lambda d=d: dve.tensor_copy(out=Fcar.ap[:, d, :, 1], in_=acci.ap), [acci], [Fcar])

    class _Lazy:
        def __init__(self, ps_i, which):
            self.ps_i = ps_i
            self.which = which
            self.cache = {}

        def __getitem__(self, ul):
            if ul not in self.cache:
                self.cache[ul] = load_tables(self.ps_i * UPP + ul, self.which)
            return self.cache[ul]

    def s5_front_units(ps_i, nseq):
        s5_front(ps_i, _Lazy(ps_i, 'front'), nseq)

    def s5_back_units(ps_i, nseq, carry):
        s5_back(ps_i, nseq, carry, _Lazy(ps_i, 'back'))

    def stage_S5_clean(kind, slot=None):
        nseq = 2 if kind == 'prompt' else 1
        carry = Fcar if kind == 'sample' else None
        for ps_i in range(cfg.NPASS):
            s5_front_units(ps_i, nseq)
            s5_scan(ps_i, nseq, carry)
            if kind == 'prompt':
                save_states(ps_i, nseq)
            if kind == 'light':
                save_E(ps_i, slot)
            else:
                s5_back_units(ps_i, nseq, carry)

    def stage_GLU():
        for co in range(KS):
            wv = load_w(w_glu[co], KS)
            pv = proj(wv, KS, yT)
            wz = load_w(w_in[KS + co], KC)
            pz = proj(wz, KC, hT)
            t0, t1, t2 = tmpA[0], tmpA[1], tmpA[2]
            T.op('act', lambda pv=pv, co=co: act.activation(out=t0.ap, in_=pv.ap, func=AF.Sigmoid,
                                                           bias=bglu.ap[:, co:co + 1]), [pv, bglu], [t0])
            tt('dve', t0.ap, t0.ap, yT.ap[:, co, :], ALU.mult, [t0, yT.sub(co * TB, (co + 1) * TB)], [t0])
            T.op('act', lambda pz=pz: act.activation(out=t1.ap, in_=pz.ap, func=AF.Sigmoid), [pz], [t1])
            tt('dve', t1.ap, t1.ap, pz.ap, ALU.mult, [t1, pz], [t1])
            tt('dve', yaT.ap[:, co, :], t0.ap, t1.ap, ALU.mult, [t0, t1], [yaT.sub(co * TB, (co + 1) * TB)])

    def stage_conv(kind):
        per = 32 if kind == 'prompt' else 8
        base_c = 2 * KS
        for cc in range(KCV):
            w1 = load_w(w_in[base_c + cc], KC)
            p_hb = proj(w1, KC, hT)
            w2 = load_w(w_in[base_c + 2 * KCV + cc], KC)
            p_gc = proj(w2, KC, hT)
            t0, t1, t2, t3 = tmpA
            copy_on('act', t0.ap, p_hb.ap, [p_hb], [t0])
            tt('dve', t0.ap, t0.ap, p_gc.ap, ALU.mult, [t0, p_gc], [t0])
            T.op('dve', lambda cc=cc: dve.tensor_scalar(out=t1.ap, in0=t0.ap, scalar1=cw.ap[:, 1, cc:cc + 1],
                                                        scalar2=cb.ap[:, cc:cc + 1], op0=ALU.mult, op1=ALU.add),
                 [t0, cw, cb], [t1])
            v4 = t0.ap.rearrange("p (j h n) -> p j h n", j=4, h=2)
            a4 = t1.ap.rearrange("p (j h n) -> p j h n", j=4, h=2)

            def tap(dst, src, k, cc=cc):
                T.op('dve', lambda: dve.scalar_tensor_tensor(out=dst, in0=src, scalar=cw.ap[:, k, cc:cc + 1], in1=dst,
                                                             op0=ALU.mult, op1=ALU.add), [t0, t1, cw], [t1])
            tap(a4[:, 1:4], v4[:, 0:3], 0)
            tap(a4[:, 0, 1, :], v4[:, 3, 0, :], 0)
            d5 = a4[:, 0, 0, :].rearrange("p (q r) -> p q r", r=per)
            s5_ = v4[:, 3, 1, :].rearrange("p (q r) -> p q r", r=per)
            tap(d5[:, :, 1:], s5_[:, :, :per - 1], 0)
            tap(a4[:, 0:3], v4[:, 1:4], 2)
            tap(a4[:, 3, 0, :], v4[:, 0, 1, :], 2)
            d6 = a4[:, 3, 1, :].rearrange("p (q r) -> p q r", r=per)
            s6 = v4[:, 0, 0, :].rearrange("p (q r) -> p q r", r=per)
            tap(d6[:, :, :per - 1], s6[:, :, 1:], 2)
            w3 = load_w(w_in[base_c + KCV + cc], KC)
            p_gb = proj(w3, KC, hT)
            w4 = load_w(w_in[base_c + 3 * KCV + cc], KC)
            p_zb = proj(w4, KC, hT)
            tt('dve', t1.ap, t1.ap, p_gb.ap, ALU.mult, [t1, p_gb], [t1])
            T.op('act', lambda p_zb=p_zb: act.activation(out=t2.ap, in_=p_zb.ap, func=AF.Sigmoid), [p_zb], [t2])
            tt('dve', t2.ap, t2.ap, p_zb.ap, ALU.mult, [t2, p_zb], [t2])
            tt('dve', ybT.ap[:, cc, :], t1.ap, t2.ap, ALU.mult, [t1, t2], [ybT.sub(cc * TB, (cc + 1) * TB)])

    def stage_merge():
        base_r = 2 * KS + 4 * KCV
        for fc in range(KC):
            wa = load_w(w_ba[fc], KS)
            pa = proj(wa, KS, yaT)
            wb = load_w(w_bb[fc], KCV)
            pb_ = proj(wb, KCV, ybT)
            wra = load_w(w_in[base_r + fc], KC)
            pra = proj(wra, KC, hT)
            wrb = load_w(w_in[base_r + KC + fc], KC)
            prb = proj(wrb, KC, hT)
            t0, t1 = tmpA[0], tmpA[1]
            T.op('act', lambda pra=pra: act.activation(out=t0.ap, in_=pra.ap, func=AF.Sigmoid), [pra], [t0])
            T.op('act', lambda prb=prb: act.activation(out=t1.ap, in_=prb.ap, func=AF.Sigmoid), [prb], [t1])
            tt('dve', t0.ap, t0.ap, pa.ap, ALU.mult, [t0, pa], [t0])
            tt('dve', t1.ap, t1.ap, pb_.ap, ALU.mult, [t1, pb_], [t1])
            tt('dve', mergedT.ap[:, fc, :], t0.ap, t1.ap, ALU.add, [t0, t1], [mergedT.sub(fc * TB, (fc + 1) * TB)])

    def stage_out(blk, cond_i, out_i):
        for j in range(4):
            T.dma('sp', 'xo%d' % j, lambda j=j: nc.sync.dma_start(out=XT[j].ap, in_=xb[blk, j]), [], [XT[j]])
        T.dma('sp', 'lng', lambda: nc.sync.dma_start(out=lng_bc.ap, in_=ln_g.partition_broadcast(128)), [], [lng_bc])
        T.dma('sp', 'lnb', lambda: nc.sync.dma_start(out=lnb_bc.ap, in_=ln_b.partition_broadcast(128)), [], [lnb_bc])
        for fc in range(KC):
            wv = load_w(w_o[fc], KC)
            pv = proj(wv, KC, mergedT)
            t0 = tmpA[2 + fc % 2]
            T.op('act', lambda fc=fc, pv=pv, t0=t0: act.activation(out=t0.ap, in_=pv.ap, func=AF.Identity,
                                                                  scale=modv.ap[:, 2 * KC + fc, cond_i:cond_i + 1]),
                 [pv, modv], [t0])
            pt = psv(4 + fc % 2, F32, [4, 128])
            for j in range(4):
                T.op('pe', lambda j=j, pt=pt, t0=t0: pe.transpose(pt.ap[:, j, :], t0.ap[:, j * 128:(j + 1) * 128], ident_f),
                     [t0, cst], [pt], inc=(j == 3))
            for j in range(4):
                xs = XT[j].ap[:, fc * 128:(fc + 1) * 128]
                T.op('dve', lambda j=j, pt=pt, xs=xs: dve.scalar_tensor_tensor(
                    out=xs, in0=xs, scalar=cfg.ALPHA, in1=pt.ap[:, j, :], op0=ALU.mult, op1=ALU.add),
                    [pt, XT[j].sub(fc * 128, fc * 128 + 128)], [XT[j].sub(fc * 128, fc * 128 + 128)])
        for j in range(4):
            xv = XT[j]
            ln_stats(xv)
            tt('dve', xv.ap, xv.ap, lng_bc.ap, ALU.mult, [xv, lng_bc], [xv])
            tt('dve', xv.ap, xv.ap, lnb_bc.ap, ALU.add, [xv, lnb_bc], [xv])
            T.dma('sp', 'yo%d' % j, lambda j=j, xv=xv: nc.sync.dma_start(out=y_out[out_i, j], in_=xv.ap), [xv], [('y_out', out_i, j)])

    for slot in range(3):
        stage_A(3 + slot, 1)
        stage_U()
        stage_S5_clean('light', slot)
    compute_carry()
    order = [(2, 'sample', 1, 2), (0, 'prompt', 0, 0), (1, 'prompt', 0, 1)]
    for blk, kind, cond_i, out_i in order:
        stage_A(blk, cond_i)
        stage_U()
        stage_S5_clean(kind)
        if kind == 'prompt':
            T.dma('sp', 'sto', lambda out_i=out_i: nc.sync.dma_start(
                out=st_out[out_i], in_=RM.view(STt.off, F32, [2 * 2 * 2 * GQ]).ap), [STt], [('st_out', out_i)])
        stage_GLU()
        stage_conv(kind)
        stage_merge()
        stage_out(blk, cond_i, out_i)
    T.final_wait('sp')
    return nc, T


def _chunk_w(w, kcn):
    K, N = w.shape
    a = w.reshape(kcn, 128, N // 128, 128)
    a = np.ascontiguousarray(a.transpose(2, 1, 0, 3)).reshape(N // 128, 128, kcn * 128)
    return a


def _tok_perm():
    j = np.arange(4)[:, None]
    pi = np.arange(128)[None, :]
    h = pi // 64
    n = pi % 64
    return 8 * n + 4 * h + j


def _gp_layout(a, GQ):
    d_, NG, P = a.shape[:3]
    rest = a.shape[3:]
    b = a.reshape(d_, GQ, 2, P, *rest)
    b = np.moveaxis(b, [2, 3], [0, 1])
    return np.ascontiguousarray(b.reshape(2 * P, d_, GQ, *rest))


def prepare_inputs(cfg, inp, n_cores=8):
    D, DS, DC, KC, KS, KCV, GQ, NG = cfg.D, cfg.DS, cfg.DC, cfg.KC, cfg.KS, cfg.KCV, cfg.GQ, cfg.NG
    f = np.float32
    xp = np.asarray(inp['x_prompt'], f)
    xs = np.asarray(inp['x_sample'], f)
    perm = _tok_perm()
    shared = {}
    shared['w_mod'] = _chunk_w(np.asarray(inp['w_mod'][0], f), KC)
    shared['b_mod'] = np.ascontiguousarray(np.asarray(inp['b_mod'][0], f).reshape(-1, 128).T)
    shared['w_in'] = _chunk_w(np.asarray(inp['w_in'][0], f), KC)
    shared['w_glu'] = _chunk_w(np.asarray(inp['w_glu'][0], f), KS)
    shared['b_glu'] = np.ascontiguousarray(np.asarray(inp['b_glu'][0], f).reshape(-1, 128).T)
    shared['w_ba'] = _chunk_w(np.asarray(inp['w_branch_a'][0], f), KS)
    shared['w_bb'] = _chunk_w(np.asarray(inp['w_branch_b'][0], f), KCV)
    shared['w_o'] = _chunk_w(np.asarray(inp['w_o'][0], f), KC)
    cwv = np.asarray(inp['conv_w'][0], f)
    shared['conv_w'] = np.ascontiguousarray(cwv.reshape(3, KCV, 128).transpose(2, 0, 1))
    shared['conv_b'] = np.ascontiguousarray(np.asarray(inp['conv_b'][0], f).reshape(KCV, 128).T)
    shared['ln_g'] = np.asarray(inp['ln_g'][0], f).reshape(1, D)
    shared['ln_b'] = np.asarray(inp['ln_b'][0], f).reshape(1, D)
    ldt = np.asarray(inp['ssm_log_dt'][0], f)
    ldt_b = np.broadcast_to(ldt[:, :, None], (2, NG, 64))
    sm_ = np.stack([_gp_layout(ldt_b, GQ), _gp_layout(np.asarray(inp['ssm_a_re'][0], f), GQ),
                    _gp_layout(np.asarray(inp['ssm_a_im'][0], f), GQ)], axis=1)
    shared['s5_sm'] = np.ascontiguousarray(sm_.reshape(128, 3, 2 * GQ))
    b_re = _gp_layout(np.asarray(inp['ssm_b_re'][0], f), GQ)
    b_im = _gp_layout(np.asarray(inp['ssm_b_im'][0], f), GQ)
    c_re = _gp_layout(np.asarray(inp['ssm_c_re'][0], f).transpose(0, 1, 3, 2), GQ)
    c_im = _gp_layout(np.asarray(inp['ssm_c_im'][0], f).transpose(0, 1, 3, 2), GQ)
    shared['s5_bc'] = np.ascontiguousarray(np.stack([b_re, b_im, c_re, c_im], axis=1).reshape(128, 4, 2 * GQ * 16))
    sd = np.asarray(inp['ssm_d'][0], f).reshape(NG, 16)
    shared['ssm_dv'] = np.ascontiguousarray(np.broadcast_to(sd.T[None, :, :], (8, 16, NG)).reshape(128, NG))
    ident = np.eye(128, dtype=f)
    blk = np.arange(128) // 16
    mL = (blk[None, :] >= blk[:, None]).astype(f)
    mU = (blk[None, :] <= blk[:, None]).astype(f)
    shared['consts'] = np.ascontiguousarray(np.stack([ident, mL, mU], axis=1))
    c_ctx = np.asarray(inp['c_ctx'], f)
    cs_ = np.asarray(inp['c'], f)
    sre = np.asarray(inp['state_ssm_re'], f)[:, 0]
    sim_ = np.asarray(inp['state_ssm_im'], f)[:, 0]
    maps = []
    for core in range(n_cores):
        m = dict(shared)
        si, q = core // 4, core % 4
        xblk = np.empty((6, 4, 128, D), f)
        for b in range(2):
            seqs = xp[core * 4 + 2 * b: core * 4 + 2 * b + 2].reshape(TB, D)
            xblk[b] = seqs[perm]
        for slot in range(4):
            qq = (q + slot) % 4
            dst = 2 if slot == 0 else 2 + slot
            xblk[dst] = xs[si, qq * TB:(qq + 1) * TB][perm]
        m['xb'] = xblk
        cond = np.stack([c_ctx, cs_[si]], axis=1)
        m['condT'] = np.ascontiguousarray(cond.reshape(KC, 128, 2).transpose(1, 0, 2))
        h0 = np.stack([_gp_layout(sre[si], GQ), _gp_layout(sim_[si], GQ)], axis=2)
        m['s5_h0'] = np.ascontiguousarray(h0)
        qm = np.zeros((128, 16), f)
        for j in range(1, 4):
            qm[:, j - 1] = 1.0 if (q + j >= 4) else 0.0
            qm[:, 3 + j - 1] = 1.0 if (q + j <= 3) else 0.0
        qm[:, 6 + q] = 1.0
        qm[:, 10 + (3 - q)] = 1.0
        m['qmask'] = qm
        maps.append(m)
    return maps


def assemble_outputs(cfg, results, n_cores=8, batch=32, seq=256, dec_batch=2, dec_seq=2048):
    D, GQ, NG = cfg.D, cfg.GQ, cfg.NG
    perm = _tok_perm()
    y_p = np.empty((batch, seq, D), np.float32)
    y_s = np.empty((dec_batch, dec_seq, D), np.float32)
    st_re = np.empty((batch, 1, 2, NG, 64), np.float32)
    st_im = np.empty((batch, 1, 2, NG, 64), np.float32)
    for core in range(n_cores):
        r = results[core]
        yo = np.asarray(r['y_out'])
        so = np.asarray(r['st_out']).reshape(2, 2, 64, 2, 2, 2, GQ)
        si, q = core // 4, core % 4
        for b in range(2):
            tmp = np.empty((TB, D), np.float32)
            tmp[perm] = yo[b]
            y_p[core * 4 + 2 * b: core * 4 + 2 * b + 2] = tmp.reshape(2, seq, D)
            for sq in range(2):
                s_ = so[b, :, :, sq]
                s_ = s_.transpose(2, 3, 4, 0, 1).reshape(2, 2, NG, 64)
                st_re[core * 4 + 2 * b + sq, 0] = s_[:, 0]
                st_im[core * 4 + 2 * b + sq, 0] = s_[:, 1]
        tmp = np.empty((TB, D), np.float32)
        tmp[perm] = yo[2]
        y_s[si, q * TB:(q + 1) * TB] = tmp
    return y_p, y_s, st_re, st_im


_CACHE = {}


def kernel(**inputs):
    cfg = Cfg()
    maps = prepare_inputs(cfg, inputs)
    nc, _ = build(cfg)
    res = run_bass_kernel_spmd(nc, maps, core_ids=list(range(8)))
    return assemble_outputs(cfg, res.results)
```

```python
import math
import numpy as np
import concourse.bass as bass
import concourse.mybir as mybir
from concourse.bass_utils import run_bass_kernel_spmd

F32 = mybir.dt.float32
BF16 = mybir.dt.bfloat16
AF = mybir.ActivationFunctionType
ALU = mybir.AluOpType

LN_EPS = 1e-5
TB = 512
NMC = 64
TWO_PI = 2.0 * math.pi
MAGIC = 12582912.0


class Cfg:
    def __init__(self, D=4096, DS=2048, DC=2048):
        self.D, self.DS, self.DC = D, DS, DC
        self.KC = D // 128
        self.KS = DS // 128
        self.KCV = DC // 128
        self.NG = DS // 16
        self.GQ = self.NG // 2
        self.GQP = min(32, self.GQ)
        self.NPASS = self.GQ // self.GQP
        self.TU = 4
        self.NU = self.GQ // self.TU
        self.UPP = self.GQP // self.TU
        self.DIN = 2 * DS + 4 * DC + 2 * D
        self.NIC = self.DIN // 128
        self.NMOD = 3 * D // 128
        self.ALPHA = 2.0 ** 0.25


class Trk:
    def __init__(self, nc):
        self.nc = nc
        self.eng = {'pe': nc.tensor, 'act': nc.scalar, 'dve': nc.vector, 'pool': nc.gpsimd,
                    'sp': nc.sync}
        self.sem = {e: nc.semaphore('sem_' + e).__enter__() for e in self.eng}
        self.seq = {e: 0 for e in self.eng}
        self.known = {}
        self.st = {}
        self.dstream = {}
        self.n_ops = 0

    def _wait(self, e, dep):
        name, sem, val, _ = dep
        k = (e, name)
        if self.known.get(k, 0) >= val:
            return
        self.known[k] = val
        self.eng[e].wait_ge(sem, val)

    def _collect(self, e, reads, writes, is_dma):
        deps = {}

        def add(d):
            if d is None:
                return
            if d[0] not in deps or deps[d[0]][2] < d[2]:
                deps[d[0]] = d

        for k in reads:
            s = self.st.get(k)
            if s is not None and s[0] is not None:
                d = s[0]
                if (not is_dma) and d[3] == e and e == 'pe':
                    continue
                add(d)
        for k in writes:
            s = self.st.get(k)
            if s is None:
                continue
            d = s[0]
            if d is not None and not ((not is_dma) and d[3] == e and e == 'pe'):
                add(d)
            for d in s[1].values():
                if (not is_dma) and d[3] == e and e == 'pe':
                    continue
                add(d)
        return deps.values()

    def _record(self, rec, reads, writes):
        for k in reads:
            s = self.st.get(k)
            if s is None:
                s = [None, {}]
                self.st[k] = s
            s[1][rec[0]] = rec
        for k in writes:
            self.st[k] = [rec, {}]

    def op(self, e, fn, reads=(), writes=(), inc=True):
        reads = _keys(reads)
        writes = _keys(writes)
        for d in self._collect(e, reads, writes, False):
            self._wait(e, d)
        ins = fn()
        self.n_ops += 1
        if inc:
            self.seq[e] += 1
            ins.then_inc(self.sem[e], 1)
            val = self.seq[e]
        else:
            val = self.seq[e] + 1
        rec = ('c_' + e, self.sem[e], val, e)
        self._record(rec, reads, writes)
        return ins

    def dma(self, q, stream, fn, reads=(), writes=()):
        reads = _keys(reads)
        writes = _keys(writes)
        if stream not in self.dstream:
            self.dstream[stream] = [self.nc.semaphore('dsem_' + stream).__enter__(), 0]
        ds = self.dstream[stream]
        for d in self._collect(q, reads, writes, True):
            self._wait(q, d)
        ins = fn()
        self.n_ops += 1
        ds[1] += 16
        ins.then_inc(ds[0], 16)
        rec = ('d_' + stream, ds[0], ds[1], None)
        self._record(rec, reads, writes)
        return ins

    def final_wait(self, e='sp'):
        for name, (sem, cnt) in self.dstream.items():
            if cnt > 0:
                self.eng[e].wait_ge(sem, cnt)


def _keys(items):
    out = []
    for it in items:
        if isinstance(it, View):
            out.extend(it.keys())
        elif isinstance(it, list):
            out.extend(it)
        else:
            out.append(it)
    return out


class Region:
    def __init__(self, nc, name, nbytes, gran=512, psum=False):
        self.name, self.nbytes, self.gran = name, nbytes, gran
        if psum:
            self.t = nc.psum_tensor(name, [128, nbytes // 4], F32).__enter__()
        else:
            self.t = nc.sbuf_tensor(name, [128, nbytes // 4], F32).__enter__()

    def view(self, off, dtype, shape):
        return View(self, off, dtype, shape)


class View:
    def __init__(self, reg, off, dtype, shape):
        self.reg, self.off, self.dtype, self.shape = reg, off, dtype, tuple(shape)
        self.esz = 4 if dtype == F32 else 2
        n = 1
        for s in shape:
            n *= s
        self.n = n
        self.nbytes = n * self.esz
        assert off % 4 == 0 and self.nbytes % 4 == 0, (off, self.nbytes)
        assert off + self.nbytes <= reg.nbytes, (reg.name, off, self.nbytes, reg.nbytes)
        a = reg.t[:, off // 4:(off + self.nbytes) // 4]
        if dtype != F32:
            a = a.bitcast(dtype)
        if len(shape) > 1:
            names = ' '.join('a%d' % i for i in range(len(shape)))
            kw = {'a%d' % i: shape[i] for i in range(len(shape))}
            a = a.rearrange('p (%s) -> p %s' % (names, names), **kw)
        self.ap = a

    def keys(self, lo=0, hi=None):
        if hi is None:
            hi = self.n
        b0 = self.off + lo * self.esz
        b1 = self.off + hi * self.esz
        g = self.reg.gran
        return [(self.reg.name, i) for i in range(b0 // g, (b1 - 1) // g + 1)]

    def sub(self, lo, hi):
        return self.keys(lo, hi)


def build(cfg):
    nc = bass.Bass("TRN2", target_bir_lowering=False)
    T = Trk(nc)
    D, DS, DC, KC, KS, KCV = cfg.D, cfg.DS, cfg.DC, cfg.KC, cfg.KS, cfg.KCV
    NG, GQ, GQP, TU, NU, UPP = cfg.NG, cfg.GQ, cfg.GQP, cfg.TU, cfg.NU, cfg.UPP
    NIC, NMOD = cfg.NIC, cfg.NMOD

    def din(name, shape, dt=F32):
        return nc.dram_tensor(name, list(shape), dt, kind="ExternalInput").ap()

    def dout(name, shape, dt=F32):
        return nc.dram_tensor(name, list(shape), dt, kind="ExternalOutput").ap()

    def dscr(name, shape, dt):
        return nc.dram_tensor(name, list(shape), dt, kind="Internal").ap()

    xb = din("xb", [6, 4, 128, D])
    condT = din("condT", [128, KC, 2])
    w_mod = din("w_mod", [NMOD, 128, KC * 128])
    b_mod = din("b_mod", [128, NMOD])
    w_in = din("w_in", [NIC, 128, KC * 128])
    w_glu = din("w_glu", [KS, 128, KS * 128])
    b_glu = din("b_glu", [128, KS])
    w_ba = din("w_ba", [KC, 128, KS * 128])
    w_bb = din("w_bb", [KC, 128, KCV * 128])
    w_o = din("w_o", [KC, 128, KC * 128])
    conv_w = din("conv_w", [128, 3, KCV])
    conv_b = din("conv_b", [128, KCV])
    ln_g = din("ln_g", [1, D])
    ln_b = din("ln_b", [1, D])
    s5_sm = din("s5_sm", [128, 3, 2 * GQ])
    s5_bc = din("s5_bc", [128, 4, 2 * GQ * 16])
    s5_h0 = din("s5_h0", [128, 2, 2, GQ])
    ssm_dv = din("ssm_dv", [128, NG])
    consts = din("consts", [128, 3, 128])
    qmask = din("qmask", [128, 16])

    y_out = dout("y_out", [3, 4, 128, D])
    st_out = dout("st_out", [2, 128, 2 * 2 * 2 * GQ])

    Mtab = dscr("Mtab", [NU, 128, 8 * 128], BF16)
    Wintab = dscr("Wintab", [NU, 128, 8 * 2 * 2 * 64], BF16)
    Wouttab = dscr("Wouttab", [NU, 128, TU * 2 * 2 * 128], BF16)

    RX = Region(nc, "RX", 32768)
    RH = Region(nc, "RH", 32768)
    NWS = 3
    RW = Region(nc, "RW", NWS * 8192, gran=8192)
    RU = Region(nc, "RU", 16384)
    RS = Region(nc, "RS", 32768)
    RY = Region(nc, "RY", 16384)
    RT = Region(nc, "RT", 2 * 10240, gran=2048)
    RM = Region(nc, "RM", 36608, gran=256)
    PS = Region(nc, "PSR", 16384, gran=2048, psum=True)

    misc_off = [0]

    def malloc(dtype, shape):
        n = 1
        for s in shape:
            n *= s
        nb = n * (4 if dtype == F32 else 2)
        nb = (nb + 255) // 256 * 256
        v = RM.view(misc_off[0], dtype, shape)
        misc_off[0] += nb
        return v

    def psv(bank, dtype, shape, off=0):
        return PS.view(bank * 2048 + off, dtype, shape)

    pe, act, dve, pool = nc.tensor, nc.scalar, nc.vector, nc.gpsimd

    flip = [0]

    def evac_eng():
        flip[0] ^= 1
        return 'act' if flip[0] else 'dve'

    def copy_on(e, out_ap, in_ap, reads, writes):
        if e == 'act':
            T.op('act', lambda: act.copy(out=out_ap, in_=in_ap), reads, writes)
        elif e == 'dve':
            T.op('dve', lambda: dve.tensor_copy(out=out_ap, in_=in_ap), reads, writes)
        else:
            T.op('pool', lambda: pool.tensor_copy(out=out_ap, in_=in_ap), reads, writes)

    def tt(e, out_ap, a, b, op, reads, writes):
        en = dve if e == 'dve' else pool
        T.op(e, lambda: en.tensor_tensor(out=out_ap, in0=a, in1=b, op=op), reads, writes)

    cst = malloc(F32, [3, 128])
    T.dma('sp', 'su1', lambda: nc.sync.dma_start(out=cst.ap, in_=consts), [], [cst])
    ident_f = cst.ap[:, 0, :]
    identb = malloc(BF16, [128])
    T.op('dve', lambda: dve.tensor_copy(out=identb.ap, in_=cst.ap[:, 0, :]), [cst], [identb])
    ident_b = identb.ap
    maskL = cst.ap[:, 1, :]
    maskU = cst.ap[:, 2, :]
    qm = malloc(F32, [16])
    T.dma('sp', 'su2', lambda: nc.sync.dma_start(out=qm.ap, in_=qmask), [], [qm])
    bmod = malloc(F32, [NMOD])
    T.dma('sp', 'su3', lambda: nc.sync.dma_start(out=bmod.ap, in_=b_mod), [], [bmod])
    bglu = malloc(F32, [KS])
    T.dma('sp', 'su4', lambda: nc.sync.dma_start(out=bglu.ap, in_=b_glu), [], [bglu])
    cw = malloc(F32, [3, KCV])
    T.dma('sp', 'su5', lambda: nc.sync.dma_start(out=cw.ap, in_=conv_w), [], [cw])
    cb = malloc(F32, [KCV])
    T.dma('sp', 'su6', lambda: nc.sync.dma_start(out=cb.ap, in_=conv_b), [], [cb])
    dvec = malloc(F32, [NG])
    T.dma('sp', 'su7', lambda: nc.sync.dma_start(out=dvec.ap, in_=ssm_dv), [], [dvec])
    cnd = malloc(F32, [KC, 2])
    T.dma('sp', 'su8', lambda: nc.sync.dma_start(out=cnd.ap, in_=condT), [], [cnd])

    wslot = [0]

    def load_w(src2d, kcn):
        s = wslot[0] % NWS
        wslot[0] += 1
        v = RW.view(s * 8192, BF16, [kcn, 128])
        flat = RW.view(s * 8192, BF16, [kcn * 128])
        T.dma('pool', 'w%d' % s,
              lambda: nc.gpsimd.dma_start(out=flat.ap, in_=src2d, max_dma_last_dim=4096),
              [], [v])
        return v

    pbank = [0]

    def proj(wv, kcn, rhs_view, ncols=TB):
        b = pbank[0] % 4
        pbank[0] += 1
        pv = psv(b, F32, [ncols])
        for kc in range(kcn):
            T.op('pe', lambda kc=kc: pe.matmul(pv.ap, lhsT=wv.ap[:, kc, :], rhs=rhs_view.ap[:, kc, :ncols],
                                               start=(kc == 0), stop=(kc == kcn - 1)),
                 [wv, rhs_view], [pv], inc=(kc == kcn - 1))
        return pv

    csg = malloc(F32, [KC, 2])
    T.op('act', lambda: act.activation(out=csg.ap, in_=cnd.ap, func=AF.Sigmoid), [cnd], [csg])
    csl = malloc(BF16, [KC, 2])
    tt('dve', csl.ap, csg.ap, cnd.ap, ALU.mult, [csg, cnd], [csl])
    modv = malloc(F32, [NMOD, 2])
    def mod_chunk(mc):
        wv = load_w(w_mod[mc], KC)
        pv = proj(wv, KC, csl, ncols=2)
        add1 = 1.0 if (KC <= mc < 2 * KC) else 0.0
        T.op('dve', lambda: dve.tensor_scalar(
            out=modv.ap[:, mc, :], in0=pv.ap, scalar1=bmod.ap[:, mc:mc + 1], scalar2=add1,
            op0=ALU.add, op1=ALU.add), [pv, bmod], [modv.sub(mc * 2, mc * 2 + 2)])

    mod_next = [0]
    mod_per = (NMOD + NU - 1) // NU

    def mod_some(n):
        for _ in range(n):
            if mod_next[0] < NMOD:
                mod_chunk(mod_next[0])
                mod_next[0] += 1

    G2 = 2 * GQ
    sm_in = None
    h0 = malloc(F32, [2, 2, GQ])
    T.dma('sp', 'su10', lambda: nc.sync.dma_start(out=h0.ap, in_=s5_h0), [], [h0])

    su_regs = [[RU, 0], [RY, 0], [RT, 0]]

    def salloc(dtype, shape):
        n = 1
        for s_ in shape:
            n *= s_
        nb = (n * (4 if dtype == F32 else 2) + 511) // 512 * 512
        for rr in su_regs:
            if rr[1] + nb <= rr[0].nbytes:
                v = rr[0].view(rr[1], dtype, shape)
                rr[1] += nb
                return v
        raise RuntimeError("setup scratch exhausted")

    def sm(persist=False):
        return malloc(F32, [G2]) if persist else salloc(F32, [G2])

    sm_in = RX.view(32768 - 2048, F32, [3, G2])
    T.dma('sp', 'su_smin', lambda: nc.sync.dma_start(out=sm_in.ap, in_=s5_sm), [], [sm_in])

    def dv(fn, reads, writes):
        T.op('dve', fn, reads, writes)

    def av(fn, reads, writes):
        T.op('act', fn, reads, writes)

    dt_ = sm(); ar = sm(); th = sm(); er = sm(); kk = sm(); sn = sm(); cs = sm()
    av(lambda: act.activation(out=dt_.ap, in_=sm_in.ap[:, 0, :], func=AF.Exp), [sm_in], [dt_])
    dv(lambda: dve.tensor_tensor(out=ar.ap, in0=sm_in.ap[:, 1, :], in1=dt_.ap, op=ALU.mult), [sm_in, dt_], [ar])
    dv(lambda: dve.tensor_tensor(out=th.ap, in0=sm_in.ap[:, 2, :], in1=dt_.ap, op=ALU.mult), [sm_in, dt_], [th])
    av(lambda: act.activation(out=er.ap, in_=ar.ap, func=AF.Exp), [ar], [er])

    def sin_of(dst, src, shift):
        t1 = sm()
        dv(lambda: dve.tensor_scalar(out=t1.ap, in0=src.ap, scalar1=shift, scalar2=1.0 / TWO_PI,
                                     op0=ALU.add, op1=ALU.mult), [src], [t1])
        dv(lambda: dve.tensor_scalar(out=kk.ap, in0=t1.ap, scalar1=MAGIC, scalar2=None, op0=ALU.add),
           [t1], [kk])
        dv(lambda: dve.tensor_scalar(out=kk.ap, in0=kk.ap, scalar1=-MAGIC, scalar2=None, op0=ALU.add),
           [kk], [kk])
        dv(lambda: dve.tensor_tensor(out=t1.ap, in0=t1.ap, in1=kk.ap, op=ALU.subtract), [t1, kk], [t1])
        dv(lambda: dve.tensor_scalar(out=t1.ap, in0=t1.ap, scalar1=TWO_PI, scalar2=3.14159,
                                     op0=ALU.mult, op1=ALU.min), [t1], [t1])
        dv(lambda: dve.tensor_scalar(out=t1.ap, in0=t1.ap, scalar1=-3.14159, scalar2=None, op0=ALU.max),
           [t1], [t1])
        av(lambda: act.activation(out=dst.ap, in_=t1.ap, func=AF.Sin), [t1], [dst])

    sin_of(sn, th, 0.0)
    sin_of(cs, th, math.pi / 2)

    cm_t = [sm() for _ in range(4)]

    def cmul(outr, outi, ar_, ai_, br_, bi_, n_=None):
        t1, t2, t3, t4 = cm_t
        dv(lambda: dve.tensor_tensor(out=t1.ap, in0=ar_.ap, in1=br_.ap, op=ALU.mult), [ar_, br_], [t1])
        dv(lambda: dve.tensor_tensor(out=t2.ap, in0=ai_.ap, in1=bi_.ap, op=ALU.mult), [ai_, bi_], [t2])
        dv(lambda: dve.tensor_tensor(out=t3.ap, in0=ar_.ap, in1=bi_.ap, op=ALU.mult), [ar_, bi_], [t3])
        dv(lambda: dve.tensor_tensor(out=t4.ap, in0=ai_.ap, in1=br_.ap, op=ALU.mult), [ai_, br_], [t4])
        dv(lambda: dve.tensor_tensor(out=outr.ap, in0=t1.ap, in1=t2.ap, op=ALU.subtract), [t1, t2], [outr])
        dv(lambda: dve.tensor_tensor(out=outi.ap, in0=t3.ap, in1=t4.ap, op=ALU.add), [t3, t4], [outi])

    PWr = [sm(persist=(k_ in (0, 8))) for k_ in range(9)]
    PWi = [sm(persist=(k_ in (0, 8))) for k_ in range(9)]
    dv(lambda: dve.memset(PWr[0].ap, 1.0), [], [PWr[0]])
    dv(lambda: dve.memset(PWi[0].ap, 0.0), [], [PWi[0]])
    dv(lambda: dve.tensor_tensor(out=PWr[1].ap, in0=er.ap, in1=cs.ap, op=ALU.mult), [er, cs], [PWr[1]])
    dv(lambda: dve.tensor_tensor(out=PWi[1].ap, in0=er.ap, in1=sn.ap, op=ALU.mult), [er, sn], [PWi[1]])
    for k in range(2, 9):
        cmul(PWr[k], PWi[k], PWr[k - 1], PWi[k - 1], PWr[1], PWi[1])
    nr = sm(); den = sm(); qr = sm(); qi = sm(); t5 = sm(); t6 = sm()
    a_re = sm_in.ap[:, 1, :]
    a_im = sm_in.ap[:, 2, :]
    dv(lambda: dve.tensor_scalar(out=nr.ap, in0=PWr[1].ap, scalar1=-1.0, scalar2=None, op0=ALU.add), [PWr[1]], [nr])
    dv(lambda: dve.tensor_tensor(out=den.ap, in0=a_re, in1=a_re, op=ALU.mult), [sm_in], [den])
    dv(lambda: dve.tensor_tensor(out=t5.ap, in0=a_im, in1=a_im, op=ALU.mult), [sm_in], [t5])
    dv(lambda: dve.tensor_tensor(out=den.ap, in0=den.ap, in1=t5.ap, op=ALU.add), [den, t5], [den])
    dv(lambda: dve.reciprocal(out=den.ap, in_=den.ap), [den], [den])
    dv(lambda: dve.tensor_tensor(out=t5.ap, in0=nr.ap, in1=a_re, op=ALU.mult), [nr, sm_in], [t5])
    dv(lambda: dve.tensor_tensor(out=t6.ap, in0=PWi[1].ap, in1=a_im, op=ALU.mult), [PWi[1], sm_in], [t6])
    dv(lambda: dve.tensor_tensor(out=t5.ap, in0=t5.ap, in1=t6.ap, op=ALU.add), [t5, t6], [t5])
    dv(lambda: dve.tensor_tensor(out=qr.ap, in0=t5.ap, in1=den.ap, op=ALU.mult), [t5, den], [qr])
    dv(lambda: dve.tensor_tensor(out=t5.ap, in0=PWi[1].ap, in1=a_re, op=ALU.mult), [PWi[1], sm_in], [t5])
    dv(lambda: dve.tensor_tensor(out=t6.ap, in0=nr.ap, in1=a_im, op=ALU.mult), [nr, sm_in], [t6])
    dv(lambda: dve.tensor_tensor(out=t5.ap, in0=t5.ap, in1=t6.ap, op=ALU.subtract), [t5, t6], [t5])
    dv(lambda: dve.tensor_tensor(out=qi.ap, in0=t5.ap, in1=den.ap, op=ALU.mult), [t5, den], [qi])
    ivr = sm(); ivi = sm()
    dv(lambda: dve.tensor_tensor(out=t5.ap, in0=PWr[8].ap, in1=PWr[8].ap, op=ALU.mult), [PWr[8]], [t5])
    dv(lambda: dve.tensor_tensor(out=t6.ap, in0=PWi[8].ap, in1=PWi[8].ap, op=ALU.mult), [PWi[8]], [t6])
    dv(lambda: dve.tensor_tensor(out=t5.ap, in0=t5.ap, in1=t6.ap, op=ALU.add), [t5, t6], [t5])
    dv(lambda: dve.reciprocal(out=t5.ap, in_=t5.ap), [t5], [t5])
    dv(lambda: dve.tensor_tensor(out=ivr.ap, in0=PWr[8].ap, in1=t5.ap, op=ALU.mult), [PWr[8], t5], [ivr])
    dv(lambda: dve.scalar_tensor_tensor(out=ivi.ap, in0=PWi[8].ap, scalar=-1.0, in1=t5.ap,
                                        op0=ALU.mult, op1=ALU.mult), [PWi[8], t5], [ivi])

    PWin = [salloc(F32, [2, GQ, 8]) for _ in range(2)]
    PWout = [salloc(F32, [2, GQ, 8]) for _ in range(2)]
    PZ = [salloc(F32, [2, GQ, 8]) for _ in range(2)]
    for ri, PWx in ((0, PWr), (1, PWi)):
        for s in range(8):
            for d in range(2):
                kin = (7 - s) if d == 0 else s
                kout = (s + 1) if d == 0 else (8 - s)
                src_in = PWx[kin].ap[:, d * GQ:(d + 1) * GQ]
                src_out = PWx[kout].ap[:, d * GQ:(d + 1) * GQ]
                T.op('pool', lambda ri=ri, d=d, s=s, src_in=src_in: pool.tensor_copy(
                    out=PWin[ri].ap[:, d, :, s], in_=src_in), [PWx[kin]], [PWin[ri]])
                T.op('pool', lambda ri=ri, d=d, s=s, src_out=src_out: pool.tensor_copy(
                    out=PWout[ri].ap[:, d, :, s], in_=src_out), [PWx[kout]], [PWout[ri]])
    ivr3 = ivr.ap.rearrange("p (d g) -> p d g", d=2).unsqueeze(3).to_broadcast([128, 2, GQ, 8])
    ivi3 = ivi.ap.rearrange("p (d g) -> p d g", d=2).unsqueeze(3).to_broadcast([128, 2, GQ, 8])
    z1 = salloc(F32, [2, GQ, 8]); z2 = salloc(F32, [2, GQ, 8])
    tt('dve', z1.ap, PWout[0].ap, ivr3, ALU.mult, [PWout[0], ivr], [z1])
    tt('dve', z2.ap, PWout[1].ap, ivi3, ALU.mult, [PWout[1], ivi], [z2])
    tt('dve', PZ[0].ap, z1.ap, z2.ap, ALU.subtract, [z1, z2], [PZ[0]])
    tt('dve', z1.ap, PWout[0].ap, ivi3, ALU.mult, [PWout[0], ivi], [z1])
    tt('dve', z2.ap, PWout[1].ap, ivr3, ALU.mult, [PWout[1], ivr], [z2])
    tt('dve', PZ[1].ap, z1.ap, z2.ap, ALU.add, [z1, z2], [PZ[1]])

    L8r = PWr[8].ap.rearrange("p (d g) -> p d g", d=2)
    L8i = PWi[8].ap.rearrange("p (d g) -> p d g", d=2)
    LrrN = {}
    LiiN = {}
    for ns_ in (1, 2):
        LrrN[ns_] = malloc(F32, [2, GQ, ns_, 2])
        LiiN[ns_] = malloc(F32, [2, GQ, ns_, 2])
        for s_ in range(ns_):
            for ri in range(2):
                dv(lambda ns_=ns_, s_=s_, ri=ri: dve.tensor_copy(out=LrrN[ns_].ap[:, :, :, s_, ri], in_=L8r),
                   [PWr[8]], [LrrN[ns_]])
            dv(lambda ns_=ns_, s_=s_: dve.tensor_scalar(out=LiiN[ns_].ap[:, :, :, s_, 0], in0=L8i, scalar1=-1.0,
                                                        scalar2=None, op0=ALU.mult), [PWi[8]], [LiiN[ns_]])
            dv(lambda ns_=ns_, s_=s_: dve.tensor_copy(out=LiiN[ns_].ap[:, :, :, s_, 1], in_=L8i), [PWi[8]], [LiiN[ns_]])

    curR, curI = PWr[8], PWi[8]
    pp = [(sm(), sm()), (sm(), sm())]
    bp1 = (sm(True), sm(True))
    for it in range(6):
        nr_, ni_ = bp1 if it == 5 else pp[it % 2]
        cmul(nr_, ni_, curR, curI, curR, curI)
        curR, curI = nr_, ni_
    BPr = [PWr[0], curR, sm(True), sm(True)]
    BPi = [PWi[0], curI, sm(True), sm(True)]
    cmul(BPr[2], BPi[2], curR, curI, curR, curI)
    cmul(BPr[3], BPi[3], BPr[2], BPi[2], curR, curI)

    bc_in = RS.view(0, F32, [4, 2, GQ, 16])
    T.dma('sp', 'su11', lambda: nc.sync.dma_start(out=RS.view(0, F32, [4, G2 * 16]).ap, in_=s5_bc), [], [bc_in])
    BB = RH.view(0, F32, [2, 2, GQ, 16])
    tb1 = RH.view(16384, F32, [2, GQ, 16])
    tb2 = RH.view(24576, F32, [2, GQ, 16])
    q3r = qr.ap.rearrange("p (d g) -> p d g", d=2).unsqueeze(3).to_broadcast([128, 2, GQ, 16])
    q3i = qi.ap.rearrange("p (d g) -> p d g", d=2).unsqueeze(3).to_broadcast([128, 2, GQ, 16])
    tt('dve', tb1.ap, bc_in.ap[:, 0], q3r, ALU.mult, [bc_in, qr], [tb1])
    tt('dve', tb2.ap, bc_in.ap[:, 1], q3i, ALU.mult, [bc_in, qi], [tb2])
    tt('dve', BB.ap[:, 0], tb1.ap, tb2.ap, ALU.subtract, [tb1, tb2], [BB])
    tt('dve', tb1.ap, bc_in.ap[:, 0], q3i, ALU.mult, [bc_in, qi], [tb1])
    tt('dve', tb2.ap, bc_in.ap[:, 1], q3r, ALU.mult, [bc_in, qr], [tb2])
    tt('dve', BB.ap[:, 1], tb1.ap, tb2.ap, ALU.add, [tb1, tb2], [BB])

    uo = [0]

    def rxv(dtype, shape):
        n = 1
        for s_ in shape:
            n *= s_
        nb = n * (4 if dtype == F32 else 2)
        v = RX.view(uo[0], dtype, shape)
        uo[0] += (nb + 511) // 512 * 512
        return v

    m1 = rxv(F32, [2, TU, 8, 16]); m2 = rxv(F32, [2, TU, 8, 16])
    A_t = [rxv(BF16, [2, TU, 128]) for _ in range(2)]
    Wo_t = rxv(BF16, [TU, 2, 2, 128])
    Z_t = [rxv(BF16, [2, TU, 128]) for _ in range(2)]
    Mt_sb = rxv(BF16, [8, 128])
    Win_sb = rxv(BF16, [8, 2, 2, 64])
    mtmp = rxv(F32, [128])
    assert uo[0] <= 32768 - 2048

    def ctable(outr_ap, outi_ap, neg_im, pwr, pwi, br_ap, bi_ap, u, w_keys, rk):
        g0 = u * TU
        for d in range(2):
            pr = pwr.ap[:, d, g0:g0 + TU, :].unsqueeze(3).to_broadcast([128, TU, 8, 16])
            pi_ = pwi.ap[:, d, g0:g0 + TU, :].unsqueeze(3).to_broadcast([128, TU, 8, 16])
            br = br_ap[:, d, g0:g0 + TU, :].unsqueeze(2).to_broadcast([128, TU, 8, 16])
            bi = bi_ap[:, d, g0:g0 + TU, :].unsqueeze(2).to_broadcast([128, TU, 8, 16])
            m1d = m1.ap[:, d]
            m2d = m2.ap[:, d]
            hsz = TU * 8 * 16
            k1 = m1.sub(d * hsz, (d + 1) * hsz)
            k2 = m2.sub(d * hsz, (d + 1) * hsz)
            tt('dve', m1d, pr, br, ALU.mult, rk, [k1])
            tt('pool', m2d, pi_, bi, ALU.mult, rk, [k2])
            tt('dve', outr_ap(d), m1d, m2d, ALU.subtract, [k1, k2], w_keys)
            tt('dve', m1d, pr, bi, ALU.mult, rk, [k1])
            tt('pool', m2d, pi_, br, ALU.mult, rk, [k2])
            if neg_im:
                T.op('dve', lambda d=d, m1d=m1d, m2d=m2d: dve.scalar_tensor_tensor(
                    out=outi_ap(d), in0=m1d, scalar=-1.0, in1=m2d, op0=ALU.mult, op1=ALU.subtract), [k1, k2], w_keys)
            else:
                tt('dve', outi_ap(d), m1d, m2d, ALU.add, [k1, k2], w_keys)

    def v5(view2):
        return lambda d: view2.ap[:, d].rearrange("p g (k c) -> p g k c", k=8)

    for u in range(NU):
        g0 = u * TU
        ctable(v5(A_t[0]), v5(A_t[1]), False, PWin[0], PWin[1], BB.ap[:, 0], BB.ap[:, 1], u,
               [A_t[0], A_t[1]], [PWin[0], PWin[1], BB])
        wo_r = lambda d: Wo_t.ap[:, :, d, 0, :].rearrange("p g (k c) -> p g k c", k=8)
        wo_i = lambda d: Wo_t.ap[:, :, d, 1, :].rearrange("p g (k c) -> p g k c", k=8)
        ctable(wo_r, wo_i, True, PWout[0], PWout[1], bc_in.ap[:, 2], bc_in.ap[:, 3], u,
               [Wo_t], [PWout[0], PWout[1], bc_in])
        ctable(v5(Z_t[0]), v5(Z_t[1]), True, PZ[0], PZ[1], bc_in.ap[:, 2], bc_in.ap[:, 3], u,
               [Z_t[0], Z_t[1]], [PZ[0], PZ[1], bc_in])
        T.dma('sp', 'tabw', lambda u=u: nc.sync.dma_start(
            out=Wouttab[u], in_=RX.view(Wo_t.off, BF16, [TU * 2 * 2 * 128]).ap), [Wo_t], [('Wouttab',)])
        for gl in range(8):
            gql, gp = gl // 2, gl % 2
            g = (g0 + gql) * 2 + gp
            ps_ = slice(gp * 64, gp * 64 + 64)
            pf = psv(4, F32, [128]); pb = psv(5, F32, [128])
            for d, pv in ((0, pf), (1, pb)):
                T.op('pe', lambda d=d, pv=pv: pe.matmul(pv.ap, lhsT=A_t[0].ap[ps_, d, gql, :],
                                                        rhs=Z_t[0].ap[ps_, d, gql, :], start=True, stop=False),
                     [A_t[0], Z_t[0]], [pv], inc=False)
                T.op('pe', lambda d=d, pv=pv: pe.matmul(pv.ap, lhsT=A_t[1].ap[ps_, d, gql, :],
                                                        rhs=Z_t[1].ap[ps_, d, gql, :], start=False, stop=True),
                     [A_t[1], Z_t[1]], [pv])
            tt('dve', mtmp.ap, pf.ap, maskL, ALU.mult, [pf, cst], [mtmp])
            T.op('dve', lambda g=g: dve.scalar_tensor_tensor(out=mtmp.ap, in0=ident_f, scalar=dvec.ap[:, g:g + 1],
                                                            in1=mtmp.ap, op0=ALU.mult, op1=ALU.add),
                 [mtmp, cst, dvec], [mtmp])
            m3 = m1
            m3ap = RX.view(m1.off, F32, [128]).ap
            tt('dve', m3ap, pb.ap, maskU, ALU.mult, [pb, cst], [m1])
            tt('dve', Mt_sb.ap[:, gl, :], mtmp.ap, m3ap, ALU.add, [mtmp, m1], [Mt_sb])
            for d in range(2):
                for ri in range(2):
                    pt = psv(6, BF16, [2, 2, 64])
                    T.op('pe', lambda d=d, ri=ri, pt=pt: pe.transpose(
                        pt.ap[:, d, ri, :], A_t[ri].ap[ps_, d, gql, :], ident_b[ps_, ps_]),
                        [A_t[ri], identb], [pt], inc=(d == 1 and ri == 1))
            copy_on('act', Win_sb.ap[:, gl], psv(6, BF16, [2, 2, 64]).ap, [psv(6, BF16, [2, 2, 64])], [Win_sb])
        T.dma('sp', 'tabm', lambda u=u: nc.sync.dma_start(
            out=Mtab[u], in_=RX.view(Mt_sb.off, BF16, [8 * 128]).ap), [Mt_sb], [('Mtab',)])
        T.dma('sp', 'tabi', lambda u=u: nc.sync.dma_start(
            out=Wintab[u], in_=RX.view(Win_sb.off, BF16, [8 * 2 * 2 * 64]).ap), [Win_sb], [('Wintab',)])
        mod_some(mod_per)
    mod_some(NMOD)

    hT = RH.view(0, BF16, [KC, TB])
    Utm = RU.view(0, BF16, [NG, 64])
    yaT = RU.view(0, BF16, [KS, TB])
    yT = RY.view(0, BF16, [KS, TB])
    ybT = RY.view(0, BF16, [KCV, TB])
    mergedT = RS.view(0, BF16, [KC, TB])
    Sv = RX.view(0, F32, [2, GQP, NMC, 2])
    Ug = RS.view(0, BF16, [GQP * 2, NMC])
    Hb = RS.view(8192, BF16, [2, 2, GQP, NMC])
    Ytm2 = [RS.view(24576 + i_ * 2048, BF16, [4, 128]) for i_ in range(2)]
    Yg2 = [RS.view(24576 + 1024 + i_ * 2048, BF16, [8, NMC]) for i_ in range(2)]
    XT = [RX.view(0, F32, [D]), RX.view(16384, F32, [D]), RH.view(0, F32, [D]), RH.view(16384, F32, [D])]
    lng_bc = RU.view(0, F32, [D])
    lnb_bc = RY.view(0, F32, [D])
    STt = malloc(F32, [2, 2, 2, GQ])
    Ecar = malloc(F32, [3, 2, GQ, 2])
    Fcar = malloc(F32, [2, GQ, 2])
    tmpA = [malloc(F32, [TB]) for _ in range(4)]
    utmp = [malloc(BF16, [TB]) for _ in range(2)]
    stats = malloc(F32, [8, 6])
    mv = malloc(F32, [2])
    rstd = malloc(F32, [1])
    nmr = malloc(F32, [1])
    sP = RM.view(tmpA[2].off, F32, [2, GQP * 2, 2]); sQ = RM.view(tmpA[3].off, F32, [2, GQP * 2, 2])

    tsl = [0, 0]

    def load_tables(u, which):
        if which == 'front':
            sl = tsl[0] % 2
            tsl[0] += 1
            base = sl * 4096
            Wiv = RT.view(base, BF16, [8, 2, 2, 64])
            T.dma('sp', 'tf%d' % sl, lambda: nc.sync.dma_start(
                out=RT.view(base, BF16, [8 * 2 * 2 * 64]).ap, in_=Wintab[u]), [('Wintab',)], [Wiv])
            return None, Wiv, None
        sl = tsl[1] % 2
        tsl[1] += 1
        base = 8192 + sl * 6144
        Mv = RT.view(base, BF16, [8, 128])
        Wov = RT.view(base + 2048, BF16, [TU, 2, 2, 128])
        T.dma('sp', 'tb%d' % sl, lambda: nc.sync.dma_start(
            out=RT.view(base, BF16, [8 * 128]).ap, in_=Mtab[u]), [('Mtab',)], [Mv, Wov])
        T.dma('sp', 'tb%d' % sl, lambda: nc.sync.dma_start(
            out=RT.view(base + 2048, BF16, [TU * 2 * 2 * 128]).ap, in_=Wouttab[u]), [('Wouttab',)], [Mv, Wov])
        return Mv, None, Wov

    def ln_stats(xv):
        nch = max(1, D // 512)
        cw_ = D // nch
        for c in range(nch):
            T.op('dve', lambda c=c: dve.bn_stats(out=stats.ap[:, c, :], in_=xv.ap[:, c * cw_:(c + 1) * cw_]),
                 [xv], [stats])
        T.op('dve', lambda: dve.bn_aggr(out=mv.ap, in_=stats.ap[:, :nch, :].rearrange('p a b -> p (a b)')), [stats], [mv])
        T.op('act', lambda: act.activation(out=rstd.ap, in_=mv.ap[:, 1:2], func=AF.Sqrt, bias=epsv.ap[:, 0:1]),
             [mv, epsv], [rstd])
        T.op('dve', lambda: dve.reciprocal(out=rstd.ap, in_=rstd.ap), [rstd], [rstd])
        T.op('dve', lambda: dve.scalar_tensor_tensor(out=nmr.ap, in0=mv.ap[:, 0:1], scalar=-1.0, in1=rstd.ap,
                                                     op0=ALU.mult, op1=ALU.mult), [mv, rstd], [nmr])
        T.op('act', lambda: act.activation(out=xv.ap, in_=xv.ap, func=AF.Identity, bias=nmr.ap[:, 0:1],
                                           scale=rstd.ap[:, 0:1]), [xv, nmr, rstd], [xv])

    epsv = malloc(F32, [1])
    dv(lambda: dve.memset(epsv.ap, LN_EPS), [], [epsv])

    def stage_A(blk, cond_i):
        XA = [RX.view(0, F32, [D]), RX.view(16384, F32, [D]), RS.view(0, F32, [D]), RS.view(16384, F32, [D])]
        for j in range(4):
            T.dma('sp', 'x%d' % j, lambda j=j: nc.sync.dma_start(out=XA[j].ap, in_=xb[blk, j]), [], [XA[j]])

        def load_ln(j):
            ln_stats(XA[j])

        def transposes(j):
            xv = XA[j]
            for kc in range(KC):
                pv = psv(4 + (kc // 4) % 4, F32, [128], off=(kc % 4) * 512)
                T.op('pe', lambda kc=kc, pv=pv: pe.transpose(pv.ap, xv.ap[:, kc * 128:(kc + 1) * 128], ident_f),
                     [xv.sub(kc * 128, kc * 128 + 128), cst], [pv])
                dst = hT.ap[:, kc, j * 128:(j + 1) * 128]
                wk = hT.sub(kc * TB + j * 128, kc * TB + j * 128 + 128)
                sc = modv.ap[:, KC + kc, cond_i:cond_i + 1]
                sh = modv.ap[:, kc, cond_i:cond_i + 1]
                if evac_eng() == 'act':
                    T.op('act', lambda pv=pv, dst=dst, sc=sc, sh=sh: act.activation(
                        out=dst, in_=pv.ap, func=AF.Identity, bias=sh, scale=sc), [pv, modv], [wk])
                else:
                    T.op('dve', lambda pv=pv, dst=dst, sc=sc, sh=sh: dve.tensor_scalar(
                        out=dst, in0=pv.ap, scalar1=sc, scalar2=sh, op0=ALU.mult, op1=ALU.add), [pv, modv], [wk])

        load_ln(0)
        for j in range(4):
            if j + 1 < 4:
                load_ln(j + 1)
            transposes(j)

    def stage_U():
        for ci in range(KS):
            wv = load_w(w_in[ci], KC)
            pv = proj(wv, KC, hT)
            ut = utmp[ci % 2]
            copy_on('act', ut.ap, pv.ap, [pv], [ut])
            pt = psv(4 + ci % 2, BF16, [4, 128])
            for j in range(4):
                T.op('pe', lambda j=j, pt=pt, ut=ut: pe.transpose(pt.ap[:, j, :], ut.ap[:, j * 128:(j + 1) * 128], ident_b),
                     [ut, identb], [pt], inc=(j == 3))
            wk = Utm.sub(ci * 8 * 64, (ci + 1) * 8 * 64)
            T.op('dve', lambda ci=ci, pt=pt: dve.tensor_copy(
                out=Utm.ap[:, ci * 8:(ci + 1) * 8, :].rearrange("p g (j c) -> p g j c", j=4),
                in_=pt.ap.rearrange("p j (g c) -> p g j c", g=8)), [pt], [wk])

    def s5_front(ps_i, tabs, nseq):
        for ul in range(UPP):
            u = ps_i * UPP + ul
            Mv, Wiv, Wov = tabs[ul]
            pu = psv(6 + ul % 2, BF16, [8, NMC])
            for gl in range(8):
                g = u * 8 + gl
                for h in range(2):
                    hs = slice(h * 64, h * 64 + 64)
                    src = Utm.ap[hs, g, :]
                    T.op('pe', lambda gl=gl, hs=hs, src=src, pu=pu: pe.transpose(
                        pu.ap[hs, gl, :], src, ident_b[hs, hs]), [Utm, identb], [pu], inc=(gl == 7 and h == 1))
            ugd = Ug.ap[:, ul * 8:(ul + 1) * 8, :]
            ugk = Ug.sub(ul * 8 * NMC, (ul + 1) * 8 * NMC)
            copy_on('act', ugd, pu.ap, [pu], [ugk])
            pS = PS.view(0 * 2048 + (ul % 2) * 4096, F32, [2, 2, TU, NMC])
            for gl in range(8):
                gql, gp = gl // 2, gl % 2
                ps_ = slice(gp * 64, gp * 64 + 64)
                for d in range(2):
                    for ri in range(2):
                        last = (gl == 7 and d == 1 and ri == 1)
                        T.op('pe', lambda gl=gl, gql=gql, d=d, ri=ri, ul=ul, pS=pS, ps_=ps_, Wiv=Wiv: pe.matmul(
                            pS.ap[ps_, d, ri, gql, :], lhsT=Wiv.ap[:, gl, d, ri, :], rhs=Ug.ap[:, ul * 8 + gl, :],
                            start=True, stop=True), [Wiv, ugk], [pS], inc=last)
            nl_ = NMC // nseq
            wk = []
            for d in range(2):
                o = (d * GQP + ul * TU) * NMC * 2
                wk += Sv.sub(o, o + TU * NMC * 2)
            dst0 = Sv.ap[:, 0, ul * TU:(ul + 1) * TU, :, :]
            src0 = pS.ap[:, 0].rearrange("p r g n -> p g n r")
            T.op('act', lambda dst0=dst0, src0=src0: act.copy(out=dst0, in_=src0), [pS], [wk])
            for sq in range(nseq):
                dst1 = Sv.ap[:, 1, ul * TU:(ul + 1) * TU, sq * nl_:(sq + 1) * nl_, :]
                src1 = pS.ap[:, 1, :, :, sq * nl_:(sq + 1) * nl_].rearrange("p r g n -> p g n r")[:, :, ::-1, :]
                T.op('dve', lambda dst1=dst1, src1=src1: dve.tensor_copy(out=dst1, in_=src1), [pS], [wk])

    def s5_scan(ps_i, nseq, carry):
        nl = NMC // nseq
        g0 = ps_i * GQP
        S6 = Sv.ap.rearrange("p d g (s k) r -> p d (g s) k r", s=nseq)
        lrr = LrrN[nseq].ap.rearrange("p d g s r -> p d (g s) r")[:, :, g0 * nseq:(g0 + GQP) * nseq, :]
        lii = LiiN[nseq].ap.rearrange("p d g s r -> p d (g s) r")[:, :, g0 * nseq:(g0 + GQP) * nseq, :]
        W = GQP * nseq
        p_ = sP.ap[:, :, :W, :]
        q_ = sQ.ap[:, :, :W, :]
        for k in range(nl):
            cur = S6[:, :, :, k, :]
            if k == 0:
                if carry is None:
                    continue
                prev = carry.ap[:, :, g0:g0 + GQP, :]
                prev_sw = carry.ap[:, :, g0:g0 + GQP, ::-1]
                rk = [carry]
            else:
                prev = S6[:, :, :, k - 1, :]
                prev_sw = S6[:, :, :, k - 1, ::-1]
                rk = [Sv]
            tt('dve', p_, lrr, prev, ALU.mult, rk + [LrrN[nseq]], [sP])
            tt('dve', q_, lii, prev_sw, ALU.mult, rk + [LiiN[nseq]], [sQ])
            tt('dve', p_, p_, q_, ALU.add, [sP, sQ], [sP])
            tt('dve', cur, cur, p_, ALU.add, [Sv, sP], [Sv])

    def s5_back(ps_i, nseq, carry, tabs):
        nl = NMC // nseq
        g0 = ps_i * GQP
        S6 = Sv.ap.rearrange("p d g (s k) r -> p d g s k r", s=nseq)
        H6 = Hb.ap.rearrange("p d r g (s n) -> p d r g s n", s=nseq)
        for r in range(2):
            T.op('act', lambda r=r: act.copy(out=H6[:, 0, r, :, :, 1:], in_=S6[:, 0, :, :, :nl - 1, r]), [Sv], [Hb])
            T.op('dve', lambda r=r: dve.tensor_copy(out=H6[:, 1, r, :, :, :nl - 1], in_=S6[:, 1, :, :, nl - 2::-1, r]), [Sv], [Hb])
        Hm = Hb.ap.rearrange("p d r g (s n) -> p d (r g) s n", s=nseq)
        if carry is None:
            T.op('dve', lambda: dve.memset(Hm[:, 0, :, :, 0], 0.0), [], [Hb])
            T.op('dve', lambda: dve.memset(Hm[:, 1, :, :, nl - 1], 0.0), [], [Hb])
        else:
            for r in range(2):
                T.op('dve', lambda r=r: dve.tensor_copy(out=H6[:, 0, r, :, 0, 0], in_=carry.ap[:, 0, g0:g0 + GQP, r]), [carry], [Hb])
                T.op('dve', lambda r=r: dve.tensor_copy(out=H6[:, 1, r, :, 0, nl - 1], in_=carry.ap[:, 1, g0:g0 + GQP, r]), [carry], [Hb])
        for ul in range(UPP):
            u = ps_i * UPP + ul
            Mv, Wiv, Wov = tabs[ul]
            ugk = Ug.sub(ul * 8 * NMC, (ul + 1) * 8 * NMC)
            pY = psv(ul % 2, F32, [8, NMC])
            for gl in range(8):
                gql, gp = gl // 2, gl % 2
                gq_l = ul * TU + gql
                ps_ = slice(gp * 64, gp * 64 + 64)
                T.op('pe', lambda gl=gl, pY=pY, Mv=Mv: pe.matmul(pY.ap[:, gl, :], lhsT=Mv.ap[:, gl, :],
                                                                 rhs=Ug.ap[:, ul * 8 + gl, :], start=True, stop=False),
                     [Mv, ugk], [pY], inc=False)
                for d in range(2):
                    for ri in range(2):
                        last = (d == 1 and ri == 1)
                        T.op('pe', lambda gl=gl, gql=gql, d=d, ri=ri, pY=pY, Wov=Wov, ps_=ps_, gq_l=gq_l, last=last: pe.matmul(
                            pY.ap[:, gl, :], lhsT=Wov.ap[ps_, gql, d, ri, :], rhs=Hb.ap[ps_, d, ri, gq_l, :],
                            start=False, stop=last), [Wov, Hb], [pY], inc=(last and gl == 7))
            x = PS.view(pY.off, F32, [8 * NMC])
            t0, t1 = (tmpA[0], tmpA[1]) if ul % 2 == 0 else (tmpA[2], tmpA[3])
            Ytm = Ytm2[ul % 2]
            Yg = Yg2[ul % 2]
            T.op('act', lambda: act.activation(out=t0.ap, in_=x.ap, func=AF.Square), [x], [t0])
            T.op('dve', lambda: dve.tensor_scalar(out=t0.ap, in0=t0.ap, scalar1=0.044715, scalar2=1.0,
                                                  op0=ALU.mult, op1=ALU.add), [t0], [t0])
            tt('dve', t0.ap, t0.ap, x.ap, ALU.mult, [t0, x], [t0])
            T.op('act', lambda: act.activation(out=t1.ap, in_=t0.ap, func=AF.Sigmoid, scale=1.5957691216), [t0], [t1])
            tt('dve', Yg.ap.rearrange("p g n -> p (g n)"), t1.ap, x.ap, ALU.mult, [t1, x], [Yg])
            pt = psv(6 + ul % 2, BF16, [8, 4, 16])
            for gl in range(8):
                for h in range(2):
                    hs = slice(h * 64, h * 64 + 64)
                    T.op('pe', lambda gl=gl, hs=hs, pt=pt: pe.transpose(
                        pt.ap[hs, gl, :, :], Yg.ap[hs, gl, :], ident_b[hs, hs]), [Yg, identb], [pt],
                        inc=(gl == 7 and h == 1))
            T.op('dve', lambda pt=pt: dve.tensor_copy(out=Ytm.ap.rearrange("p j (g c) -> p g j c", g=8), in_=pt.ap),
                 [pt], [Ytm])
            pf = psv(4 + ul % 2, BF16, [4, 128])
            for j in range(4):
                T.op('pe', lambda j=j, pf=pf: pe.transpose(pf.ap[:, j, :], Ytm.ap[:, j, :], ident_b),
                     [Ytm, identb], [pf], inc=(j == 3))
            copy_on('act', yT.ap[:, u, :], pf.ap.rearrange("p j c -> p (j c)"), [pf], [yT.sub(u * TB, (u + 1) * TB)])

    def save_states(ps_i, nseq):
        nl = NMC // nseq
        gsl = slice(ps_i * GQP, (ps_i + 1) * GQP)
        S6 = Sv.ap.rearrange("p d g (s k) r -> p d g s k r", s=nseq)
        for sq in range(nseq):
            for d in range(2):
                T.op('dve', lambda sq=sq, d=d: dve.tensor_copy(
                    out=STt.ap[:, sq, d, :, gsl], in_=S6[:, d, :, sq, nl - 1, :].rearrange("p g r -> p r g")),
                    [Sv], [STt])

    def save_E(ps_i, slot):
        gsl = slice(ps_i * GQP, (ps_i + 1) * GQP)
        T.op('dve', lambda: dve.tensor_copy(out=Ecar.ap[:, slot, :, gsl, :], in_=Sv.ap[:, :, :, NMC - 1, :]), [Sv], [Ecar])

    def compute_carry():
        cc_ = [RM.view(tmpA[0].off + i_ * 256, F32, [GQ]) for i_ in range(8)]
        accr, acci, c1, c2, c3, c4, cpr, cpi = cc_
        for d in range(2):
            ds_ = slice(d * GQ, (d + 1) * GQ)

            def mac(pr_ap, pi_ap, er_ap, ei_ap, msk, first, rk):
                tt('dve', c1.ap, pr_ap, er_ap, ALU.mult, rk, [c1])
                tt('dve', c2.ap, pi_ap, ei_ap, ALU.mult, rk, [c2])
                tt('dve', c3.ap, pr_ap, ei_ap, ALU.mult, rk, [c3])
                tt('dve', c4.ap, pi_ap, er_ap, ALU.mult, rk, [c4])
                tt('dve', c1.ap, c1.ap, c2.ap, ALU.subtract, [c1, c2], [c1])
                tt('dve', c3.ap, c3.ap, c4.ap, ALU.add, [c3, c4], [c3])
                if msk is None:
                    if first:
                        dv(lambda: dve.tensor_copy(out=accr.ap, in_=c1.ap), [c1], [accr])
                        dv(lambda: dve.tensor_copy(out=acci.ap, in_=c3.ap), [c3], [acci])
                    else:
                        tt('dve', accr.ap, accr.ap, c1.ap, ALU.add, [accr, c1], [accr])
                        tt('dve', acci.ap, acci.ap, c3.ap, ALU.add, [acci, c3], [acci])
                else:
                    dv(lambda: dve.scalar_tensor_tensor(out=accr.ap, in0=c1.ap, scalar=msk, in1=accr.ap,
                                                        op0=ALU.mult, op1=ALU.add), [c1, accr, qm], [accr])
                    dv(lambda: dve.scalar_tensor_tensor(out=acci.ap, in0=c3.ap, scalar=msk, in1=acci.ap,
                                                        op0=ALU.mult, op1=ALU.add), [c3, acci, qm], [acci])

            ohb = 6 + 4 * d
            dv(lambda: dve.tensor_scalar(out=cpr.ap, in0=BPr[0].ap[:, ds_], scalar1=qm.ap[:, ohb:ohb + 1], scalar2=None,
                                         op0=ALU.mult), [BPr[0], qm], [cpr])
            dv(lambda: dve.tensor_scalar(out=cpi.ap, in0=BPi[0].ap[:, ds_], scalar1=qm.ap[:, ohb:ohb + 1], scalar2=None,
                                         op0=ALU.mult), [BPi[0], qm], [cpi])
            for i in range(1, 4):
                dv(lambda i=i: dve.scalar_tensor_tensor(out=cpr.ap, in0=BPr[i].ap[:, ds_], scalar=qm.ap[:, ohb + i:ohb + i + 1],
                                                        in1=cpr.ap, op0=ALU.mult, op1=ALU.add), [BPr[i], qm, cpr], [cpr])
                dv(lambda i=i: dve.scalar_tensor_tensor(out=cpi.ap, in0=BPi[i].ap[:, ds_], scalar=qm.ap[:, ohb + i:ohb + i + 1],
                                                        in1=cpi.ap, op0=ALU.mult, op1=ALU.add), [BPi[i], qm, cpi], [cpi])
            mac(cpr.ap, cpi.ap, h0.ap[:, d, 0, :], h0.ap[:, d, 1, :], None, True, [cpr, cpi, h0])
            for j in range(1, 4):
                e = (3 - j) if d == 0 else (j - 1)
                mi = (j - 1) + 3 * d
                mac(BPr[e].ap[:, ds_], BPi[e].ap[:, ds_], Ecar.ap[:, j - 1, d, :, 0], Ecar.ap[:, j - 1, d, :, 1],
                    qm.ap[:, mi:mi + 1], False, [BPr[e], BPi[e], Ecar])
            dv(lambda d=d: dve.tensor_copy(out=Fcar.ap[:, d, :, 0], in_=accr.ap), [accr], [Fcar])
            dv(lambda d=d: dve.tensor_copy(out=Fcar.ap[:, d, :, 1], in_=acci.ap), [acci], [Fcar])

    class _Lazy:
        def __init__(self, ps_i, which):
            self.ps_i = ps_i
            self.which = which
            self.cache = {}

        def __getitem__(self, ul):
            if ul not in self.cache:
                self.cache[ul] = load_tables(self.ps_i * UPP + ul, self.which)
            return self.cache[ul]

    def s5_front_units(ps_i, nseq):
        s5_front(ps_i, _Lazy(ps_i, 'front'), nseq)

    def s5_back_units(ps_i, nseq, carry):
        s5_back(ps_i, nseq, carry, _Lazy(ps_i, 'back'))

    def stage_S5_clean(kind, slot=None):
        nseq = 2 if kind == 'prompt' else 1
        carry = Fcar if kind == 'sample' else None
        for ps_i in range(cfg.NPASS):
            s5_front_units(ps_i, nseq)
            s5_scan(ps_i, nseq, carry)
            if kind == 'prompt':
                save_states(ps_i, nseq)
            if kind == 'light':
                save_E(ps_i, slot)
            else:
                s5_back_units(ps_i, nseq, carry)

    def stage_GLU():
        for co in range(KS):
            wv = load_w(w_glu[co], KS)
            pv = proj(wv, KS, yT)
            wz = load_w(w_in[KS + co], KC)
            pz = proj(wz, KC, hT)
            t0, t1, t2 = tmpA[0], tmpA[1], tmpA[2]
            T.op('act', lambda pv=pv, co=co: act.activation(out=t0.ap, in_=pv.ap, func=AF.Sigmoid,
                                                           bias=bglu.ap[:, co:co + 1]), [pv, bglu], [t0])
            tt('dve', t0.ap, t0.ap, yT.ap[:, co, :], ALU.mult, [t0, yT.sub(co * TB, (co + 1) * TB)], [t0])
            T.op('act', lambda pz=pz: act.activation(out=t1.ap, in_=pz.ap, func=AF.Sigmoid), [pz], [t1])
            tt('dve', t1.ap, t1.ap, pz.ap, ALU.mult, [t1, pz], [t1])
            tt('dve', yaT.ap[:, co, :], t0.ap, t1.ap, ALU.mult, [t0, t1], [yaT.sub(co * TB, (co + 1) * TB)])

    def stage_conv(kind):
        per = 32 if kind == 'prompt' else 8
        base_c = 2 * KS
        for cc in range(KCV):
            w1 = load_w(w_in[base_c + cc], KC)
            p_hb = proj(w1, KC, hT)
            w2 = load_w(w_in[base_c + 2 * KCV + cc], KC)
            p_gc = proj(w2, KC, hT)
            t0, t1, t2, t3 = tmpA
            copy_on('act', t0.ap, p_hb.ap, [p_hb], [t0])
            tt('dve', t0.ap, t0.ap, p_gc.ap, ALU.mult, [t0, p_gc], [t0])
            T.op('dve', lambda cc=cc: dve.tensor_scalar(out=t1.ap, in0=t0.ap, scalar1=cw.ap[:, 1, cc:cc + 1],
                                                        scalar2=cb.ap[:, cc:cc + 1], op0=ALU.mult, op1=ALU.add),
                 [t0, cw, cb], [t1])
            v4 = t0.ap.rearrange("p (j h n) -> p j h n", j=4, h=2)
            a4 = t1.ap.rearrange("p (j h n) -> p j h n", j=4, h=2)

            def tap(dst, src, k, cc=cc):
                T.op('dve', lambda: dve.scalar_tensor_tensor(out=dst, in0=src, scalar=cw.ap[:, k, cc:cc + 1], in1=dst,
                                                             op0=ALU.mult, op1=ALU.add), [t0, t1, cw], [t1])
            tap(a4[:, 1:4], v4[:, 0:3], 0)
            tap(a4[:, 0, 1, :], v4[:, 3, 0, :], 0)
            d5 = a4[:, 0, 0, :].rearrange("p (q r) -> p q r", r=per)
            s5_ = v4[:, 3, 1, :].rearrange("p (q r) -> p q r", r=per)
            tap(d5[:, :, 1:], s5_[:, :, :per - 1], 0)
            tap(a4[:, 0:3], v4[:, 1:4], 2)
            tap(a4[:, 3, 0, :], v4[:, 0, 1, :], 2)
            d6 = a4[:, 3, 1, :].rearrange("p (q r) -> p q r", r=per)
            s6 = v4[:, 0, 0, :].rearrange("p (q r) -> p q r", r=per)
            tap(d6[:, :, :per - 1], s6[:, :, 1:], 2)
            w3 = load_w(w_in[base_c + KCV + cc], KC)
            p_gb = proj(w3, KC, hT)
            w4 = load_w(w_in[base_c + 3 * KCV + cc], KC)
            p_zb = proj(w4, KC, hT)
            tt('dve', t1.ap, t1.ap, p_gb.ap, ALU.mult, [t1, p_gb], [t1])
            T.op('act', lambda p_zb=p_zb: act.activation(out=t2.ap, in_=p_zb.ap, func=AF.Sigmoid), [p_zb], [t2])
            tt('dve', t2.ap, t2.ap, p_zb.ap, ALU.mult, [t2, p_zb], [t2])
            tt('dve', ybT.ap[:, cc, :], t1.ap, t2.ap, ALU.mult, [t1, t2], [ybT.sub(cc * TB, (cc + 1) * TB)])

    def stage_merge():
        base_r = 2 * KS + 4 * KCV
        for fc in range(KC):
            wa = load_w(w_ba[fc], KS)
            pa = proj(wa, KS, yaT)
            wb = load_w(w_bb[fc], KCV)
            pb_ = proj(wb, KCV, ybT)
            wra = load_w(w_in[base_r + fc], KC)
            pra = proj(wra, KC, hT)
            wrb = load_w(w_in[base_r + KC + fc], KC)
            prb = proj(wrb, KC, hT)
            t0, t1 = tmpA[0], tmpA[1]
            T.op('act', lambda pra=pra: act.activation(out=t0.ap, in_=pra.ap, func=AF.Sigmoid), [pra], [t0])
            T.op('act', lambda prb=prb: act.activation(out=t1.ap, in_=prb.ap, func=AF.Sigmoid), [prb], [t1])
            tt('dve', t0.ap, t0.ap, pa.ap, ALU.mult, [t0, pa], [t0])
            tt('dve', t1.ap, t1.ap, pb_.ap, ALU.mult, [t1, pb_], [t1])
            tt('dve', mergedT.ap[:, fc, :], t0.ap, t1.ap, ALU.add, [t0, t1], [mergedT.sub(fc * TB, (fc + 1) * TB)])

    def stage_out(blk, cond_i, out_i):
        for j in range(4):
            T.dma('sp', 'xo%d' % j, lambda j=j: nc.sync.dma_start(out=XT[j].ap, in_=xb[blk, j]), [], [XT[j]])
        T.dma('sp', 'lng', lambda: nc.sync.dma_start(out=lng_bc.ap, in_=ln_g.partition_broadcast(128)), [], [lng_bc])
        T.dma('sp', 'lnb', lambda: nc.sync.dma_start(out=lnb_bc.ap, in_=ln_b.partition_broadcast(128)), [], [lnb_bc])
        for fc in range(KC):
            wv = load_w(w_o[fc], KC)
            pv = proj(wv, KC, mergedT)
            t0 = tmpA[2 + fc % 2]
            T.op('act', lambda fc=fc, pv=pv, t0=t0: act.activation(out=t0.ap, in_=pv.ap, func=AF.Identity,
                                                                  scale=modv.ap[:, 2 * KC + fc, cond_i:cond_i + 1]),
                 [pv, modv], [t0])
            pt = psv(4 + fc % 2, F32, [4, 128])
            for j in range(4):
                T.op('pe', lambda j=j, pt=pt, t0=t0: pe.transpose(pt.ap[:, j, :], t0.ap[:, j * 128:(j + 1) * 128], ident_f),
                     [t0, cst], [pt], inc=(j == 3))
            for j in range(4):
                xs = XT[j].ap[:, fc * 128:(fc + 1) * 128]
                T.op('dve', lambda j=j, pt=pt, xs=xs: dve.scalar_tensor_tensor(
                    out=xs, in0=xs, scalar=cfg.ALPHA, in1=pt.ap[:, j, :], op0=ALU.mult, op1=ALU.add),
                    [pt, XT[j].sub(fc * 128, fc * 128 + 128)], [XT[j].sub(fc * 128, fc * 128 + 128)])
        for j in range(4):
            xv = XT[j]
            ln_stats(xv)
            tt('dve', xv.ap, xv.ap, lng_bc.ap, ALU.mult, [xv, lng_bc], [xv])
            tt('dve', xv.ap, xv.ap, lnb_bc.ap, ALU.add, [xv, lnb_bc], [xv])
            T.dma('sp', 'yo%d' % j, lambda j=j, xv=xv: nc.sync.dma_start(out=y_out[out_i, j], in_=xv.ap), [xv], [('y_out', out_i, j)])

    for slot in range(3):
        stage_A(3 + slot, 1)
        stage_U()
        stage_S5_clean('light', slot)
    compute_carry()
    order = [(2, 'sample', 1, 2), (0, 'prompt', 0, 0), (1, 'prompt', 0, 1)]
    for blk, kind, cond_i, out_i in order:
        stage_A(blk, cond_i)
        stage_U()
        stage_S5_clean(kind)
        if kind == 'prompt':
            T.dma('sp', 'sto', lambda out_i=out_i: nc.sync.dma_start(
                out=st_out[out_i], in_=RM.view(STt.off, F32, [2 * 2 * 2 * GQ]).ap), [STt], [('st_out', out_i)])
        stage_GLU()
        stage_conv(kind)
        stage_merge()
        stage_out(blk, cond_i, out_i)
    T.final_wait('sp')
    return nc, T


def _chunk_w(w, kcn):
    K, N = w.shape
    a = w.reshape(kcn, 128, N // 128, 128)
    a = np.ascontiguousarray(a.transpose(2, 1, 0, 3)).reshape(N // 128, 128, kcn * 128)
    return a


def _tok_perm():
    j = np.arange(4)[:, None]
    pi = np.arange(128)[None, :]
    h = pi // 64
    n = pi % 64
    return 8 * n + 4 * h + j


def _gp_layout(a, GQ):
    d_, NG, P = a.shape[:3]
    rest = a.shape[3:]
    b = a.reshape(d_, GQ, 2, P, *rest)
    b = np.moveaxis(b, [2, 3], [0, 1])
    return np.ascontiguousarray(b.reshape(2 * P, d_, GQ, *rest))


def prepare_inputs(cfg, inp, n_cores=8):
    D, DS, DC, KC, KS, KCV, GQ, NG = cfg.D, cfg.DS, cfg.DC, cfg.KC, cfg.KS, cfg.KCV, cfg.GQ, cfg.NG
    f = np.float32
    xp = np.asarray(inp['x_prompt'], f)
    xs = np.asarray(inp['x_sample'], f)
    perm = _tok_perm()
    shared = {}
    shared['w_mod'] = _chunk_w(np.asarray(inp['w_mod'][0], f), KC)
    shared['b_mod'] = np.ascontiguousarray(np.asarray(inp['b_mod'][0], f).reshape(-1, 128).T)
    shared['w_in'] = _chunk_w(np.asarray(inp['w_in'][0], f), KC)
    shared['w_glu'] = _chunk_w(np.asarray(inp['w_glu'][0], f), KS)
    shared['b_glu'] = np.ascontiguousarray(np.asarray(inp['b_glu'][0], f).reshape(-1, 128).T)
    shared['w_ba'] = _chunk_w(np.asarray(inp['w_branch_a'][0], f), KS)
    shared['w_bb'] = _chunk_w(np.asarray(inp['w_branch_b'][0], f), KCV)
    shared['w_o'] = _chunk_w(np.asarray(inp['w_o'][0], f), KC)
    cwv = np.asarray(inp['conv_w'][0], f)
    shared['conv_w'] = np.ascontiguousarray(cwv.reshape(3, KCV, 128).transpose(2, 0, 1))
    shared['conv_b'] = np.ascontiguousarray(np.asarray(inp['conv_b'][0], f).reshape(KCV, 128).T)
    shared['ln_g'] = np.asarray(inp['ln_g'][0], f).reshape(1, D)
    shared['ln_b'] = np.asarray(inp['ln_b'][0], f).reshape(1, D)
    ldt = np.asarray(inp['ssm_log_dt'][0], f)
    ldt_b = np.broadcast_to(ldt[:, :, None], (2, NG, 64))
    sm_ = np.stack([_gp_layout(ldt_b, GQ), _gp_layout(np.asarray(inp['ssm_a_re'][0], f), GQ),
                    _gp_layout(np.asarray(inp['ssm_a_im'][0], f), GQ)], axis=1)
    shared['s5_sm'] = np.ascontiguousarray(sm_.reshape(128, 3, 2 * GQ))
    b_re = _gp_layout(np.asarray(inp['ssm_b_re'][0], f), GQ)
    b_im = _gp_layout(np.asarray(inp['ssm_b_im'][0], f), GQ)
    c_re = _gp_layout(np.asarray(inp['ssm_c_re'][0], f).transpose(0, 1, 3, 2), GQ)
    c_im = _gp_layout(np.asarray(inp['ssm_c_im'][0], f).transpose(0, 1, 3, 2), GQ)
    shared['s5_bc'] = np.ascontiguousarray(np.stack([b_re, b_im, c_re, c_im], axis=1).reshape(128, 4, 2 * GQ * 16))
    sd = np.asarray(inp['ssm_d'][0], f).reshape(NG, 16)
    shared['ssm_dv'] = np.ascontiguousarray(np.broadcast_to(sd.T[None, :, :], (8, 16, NG)).reshape(128, NG))
    ident = np.eye(128, dtype=f)
    blk = np.arange(128) // 16
    mL = (blk[None, :] >= blk[:, None]).astype(f)
    mU = (blk[None, :] <= blk[:, None]).astype(f)
    shared['consts'] = np.ascontiguousarray(np.stack([ident, mL, mU], axis=1))
    c_ctx = np.asarray(inp['c_ctx'], f)
    cs_ = np.asarray(inp['c'], f)
    sre = np.asarray(inp['state_ssm_re'], f)[:, 0]
    sim_ = np.asarray(inp['state_ssm_im'], f)[:, 0]
    maps = []
    for core in range(n_cores):
        m = dict(shared)
        si, q = core // 4, core % 4
        xblk = np.empty((6, 4, 128, D), f)
        for b in range(2):
            seqs = xp[core * 4 + 2 * b: core * 4 + 2 * b + 2].reshape(TB, D)
            xblk[b] = seqs[perm]
        for slot in range(4):
            qq = (q + slot) % 4
            dst = 2 if slot == 0 else 2 + slot
            xblk[dst] = xs[si, qq * TB:(qq + 1) * TB][perm]
        m['xb'] = xblk
        cond = np.stack([c_ctx, cs_[si]], axis=1)
        m['condT'] = np.ascontiguousarray(cond.reshape(KC, 128, 2).transpose(1, 0, 2))
        h0 = np.stack([_gp_layout(sre[si], GQ), _gp_layout(sim_[si], GQ)], axis=2)
        m['s5_h0'] = np.ascontiguousarray(h0)
        qm = np.zeros((128, 16), f)
        for j in range(1, 4):
            qm[:, j - 1] = 1.0 if (q + j >= 4) else 0.0
            qm[:, 3 + j - 1] = 1.0 if (q + j <= 3) else 0.0
        qm[:, 6 + q] = 1.0
        qm[:, 10 + (3 - q)] = 1.0
        m['qmask'] = qm
        maps.append(m)
    return maps


def assemble_outputs(cfg, results, n_cores=8, batch=32, seq=256, dec_batch=2, dec_seq=2048):
    D, GQ, NG = cfg.D, cfg.GQ, cfg.NG
    perm = _tok_perm()
    y_p = np.empty((batch, seq, D), np.float32)
    y_s = np.empty((dec_batch, dec_seq, D), np.float32)
    st_re = np.empty((batch, 1, 2, NG, 64), np.float32)
    st_im = np.empty((batch, 1, 2, NG, 64), np.float32)
    for core in range(n_cores):
        r = results[core]
        yo = np.asarray(r['y_out'])
        so = np.asarray(r['st_out']).reshape(2, 2, 64, 2, 2, 2, GQ)
        si, q = core // 4, core % 4
        for b in range(2):
            tmp = np.empty((TB, D), np.float32)
            tmp[perm] = yo[b]
            y_p[core * 4 + 2 * b: core * 4 + 2 * b + 2] = tmp.reshape(2, seq, D)
            for sq in range(2):
                s_ = so[b, :, :, sq]
                s_ = s_.transpose(2, 3, 4, 0, 1).reshape(2, 2, NG, 64)
                st_re[core * 4 + 2 * b + sq, 0] = s_[:, 0]
                st_im[core * 4 + 2 * b + sq, 0] = s_[:, 1]
        tmp = np.empty((TB, D), np.float32)
        tmp[perm] = yo[2]
        y_s[si, q * TB:(q + 1) * TB] = tmp
    return y_p, y_s, st_re, st_im


_CACHE = {}


def kernel(**inputs):
    cfg = Cfg()
    maps = prepare_inputs(cfg, inputs)
    nc, _ = build(cfg)
    res = run_bass_kernel_spmd(nc, maps, core_ids=list(range(8)))
    return assemble_outputs(cfg, res.results)
```

```python
import math
import numpy as np
import concourse.bass as bass
import concourse.mybir as mybir
from concourse.bass_utils import run_bass_kernel_spmd

F32 = mybir.dt.float32
BF16 = mybir.dt.bfloat16
AF = mybir.ActivationFunctionType
ALU = mybir.AluOpType

LN_EPS = 1e-5
TB = 512
NMC = 64
TWO_PI = 2.0 * math.pi
MAGIC = 12582912.0


class Cfg:
    def __init__(self, D=4096, DS=2048, DC=2048):
        self.D, self.DS, self.DC = D, DS, DC
        self.KC = D // 128
        self.KS = DS // 128
        self.KCV = DC // 128
        self.NG = DS // 16
        self.GQ = self.NG // 2
        self.GQP = min(32, self.GQ)
        self.NPASS = self.GQ // self.GQP
        self.TU = 4
        self.NU = self.GQ // self.TU
        self.UPP = self.GQP // self.TU
        self.DIN = 2 * DS + 4 * DC + 2 * D
        self.NIC = self.DIN // 128
        self.NMOD = 3 * D // 128
        self.ALPHA = 2.0 ** 0.25


class Trk:
    def __init__(self, nc):
        self.nc = nc
        self.eng = {'pe': nc.tensor, 'act': nc.scalar, 'dve': nc.vector, 'pool': nc.gpsimd,
                    'sp': nc.sync}
        self.sem = {e: nc.semaphore('sem_' + e).__enter__() for e in self.eng}
        self.seq = {e: 0 for e in self.eng}
        self.known = {}
        self.st = {}
        self.dstream = {}
        self.n_ops = 0

    def _wait(self, e, dep):
        name, sem, val, _ = dep
        k = (e, name)
        if self.known.get(k, 0) >= val:
            return
        self.known[k] = val
        self.eng[e].wait_ge(sem, val)

    def _collect(self, e, reads, writes, is_dma):
        deps = {}

        def add(d):
            if d is None:
                return
            if d[0] not in deps or deps[d[0]][2] < d[2]:
                deps[d[0]] = d

        for k in reads:
            s = self.st.get(k)
            if s is not None and s[0] is not None:
                d = s[0]
                if (not is_dma) and d[3] == e and e == 'pe':
                    continue
                add(d)
        for k in writes:
            s = self.st.get(k)
            if s is None:
                continue
            d = s[0]
            if d is not None and not ((not is_dma) and d[3] == e and e == 'pe'):
                add(d)
            for d in s[1].values():
                if (not is_dma) and d[3] == e and e == 'pe':
                    continue
                add(d)
        return deps.values()

    def _record(self, rec, reads, writes):
        for k in reads:
            s = self.st.get(k)
            if s is None:
                s = [None, {}]
                self.st[k] = s
            s[1][rec[0]] = rec
        for k in writes:
            self.st[k] = [rec, {}]

    def op(self, e, fn, reads=(), writes=(), inc=True):
        reads = _keys(reads)
        writes = _keys(writes)
        for d in self._collect(e, reads, writes, False):
            self._wait(e, d)
        ins = fn()
        self.n_ops += 1
        if inc:
            self.seq[e] += 1
            ins.then_inc(self.sem[e], 1)
            val = self.seq[e]
        else:
            val = self.seq[e] + 1
        rec = ('c_' + e, self.sem[e], val, e)
        self._record(rec, reads, writes)
        return ins

    def dma(self, q, stream, fn, reads=(), writes=()):
        reads = _keys(reads)
        writes = _keys(writes)
        if stream not in self.dstream:
            self.dstream[stream] = [self.nc.semaphore('dsem_' + stream).__enter__(), 0]
        ds = self.dstream[stream]
        for d in self._collect(q, reads, writes, True):
            self._wait(q, d)
        ins = fn()
        self.n_ops += 1
        ds[1] += 16
        ins.then_inc(ds[0], 16)
        rec = ('d_' + stream, ds[0], ds[1], None)
        self._record(rec, reads, writes)
        return ins

    def final_wait(self, e='sp'):
        for name, (sem, cnt) in self.dstream.items():
            if cnt > 0:
                self.eng[e].wait_ge(sem, cnt)


def _keys(items):
    out = []
    for it in items:
        if isinstance(it, View):
            out.extend(it.keys())
        elif isinstance(it, list):
            out.extend(it)
        else:
            out.append(it)
    return out


class Region:
    def __init__(self, nc, name, nbytes, gran=512, psum=False):
        self.name, self.nbytes, self.gran = name, nbytes, gran
        if psum:
            self.t = nc.psum_tensor(name, [128, nbytes // 4], F32).__enter__()
        else:
            self.t = nc.sbuf_tensor(name, [128, nbytes // 4], F32).__enter__()

    def view(self, off, dtype, shape):
        return View(self, off, dtype, shape)


class View:
    def __init__(self, reg, off, dtype, shape):
        self.reg, self.off, self.dtype, self.shape = reg, off, dtype, tuple(shape)
        self.esz = 4 if dtype == F32 else 2
        n = 1
        for s in shape:
            n *= s
        self.n = n
        self.nbytes = n * self.esz
        assert off % 4 == 0 and self.nbytes % 4 == 0, (off, self.nbytes)
        assert off + self.nbytes <= reg.nbytes, (reg.name, off, self.nbytes, reg.nbytes)
        a = reg.t[:, off // 4:(off + self.nbytes) // 4]
        if dtype != F32:
            a = a.bitcast(dtype)
        if len(shape) > 1:
            names = ' '.join('a%d' % i for i in range(len(shape)))
            kw = {'a%d' % i: shape[i] for i in range(len(shape))}
            a = a.rearrange('p (%s) -> p %s' % (names, names), **kw)
        self.ap = a

    def keys(self, lo=0, hi=None):
        if hi is None:
            hi = self.n
        b0 = self.off + lo * self.esz
        b1 = self.off + hi * self.esz
        g = self.reg.gran
        return [(self.reg.name, i) for i in range(b0 // g, (b1 - 1) // g + 1)]

    def sub(self, lo, hi):
        return self.keys(lo, hi)


def build(cfg):
    nc = bass.Bass("TRN2", target_bir_lowering=False)
    T = Trk(nc)
    D, DS, DC, KC, KS, KCV = cfg.D, cfg.DS, cfg.DC, cfg.KC, cfg.KS, cfg.KCV
    NG, GQ, GQP, TU, NU, UPP = cfg.NG, cfg.GQ, cfg.GQP, cfg.TU, cfg.NU, cfg.UPP
    NIC, NMOD = cfg.NIC, cfg.NMOD

    def din(name, shape, dt=F32):
        return nc.dram_tensor(name, list(shape), dt, kind="ExternalInput").ap()

    def dout(name, shape, dt=F32):
        return nc.dram_tensor(name, list(shape), dt, kind="ExternalOutput").ap()

    def dscr(name, shape, dt):
        return nc.dram_tensor(name, list(shape), dt, kind="Internal").ap()

    xb = din("xb", [6, 4, 128, D])
    condT = din("condT", [128, KC, 2])
    w_mod = din("w_mod", [NMOD, 128, KC * 128])
    b_mod = din("b_mod", [128, NMOD])
    w_in = din("w_in", [NIC, 128, KC * 128])
    w_glu = din("w_glu", [KS, 128, KS * 128])
    b_glu = din("b_glu", [128, KS])
    w_ba = din("w_ba", [KC, 128, KS * 128])
    w_bb = din("w_bb", [KC, 128, KCV * 128])
    w_o = din("w_o", [KC, 128, KC * 128])
    conv_w = din("conv_w", [128, 3, KCV])
    conv_b = din("conv_b", [128, KCV])
    ln_g = din("ln_g", [1, D])
    ln_b = din("ln_b", [1, D])
    s5_sm = din("s5_sm", [128, 3, 2 * GQ])
    s5_bc = din("s5_bc", [128, 4, 2 * GQ * 16])
    s5_h0 = din("s5_h0", [128, 2, 2, GQ])
    ssm_dv = din("ssm_dv", [128, NG])
    consts = din("consts", [128, 3, 128])
    qmask = din("qmask", [128, 16])

    y_out = dout("y_out", [3, 4, 128, D])
    st_out = dout("st_out", [2, 128, 2 * 2 * 2 * GQ])

    Mtab = dscr("Mtab", [NU, 128, 8 * 128], BF16)
    Wintab = dscr("Wintab", [NU, 128, 8 * 2 * 2 * 64], BF16)
    Wouttab = dscr("Wouttab", [NU, 128, TU * 2 * 2 * 128], BF16)

    RX = Region(nc, "RX", 32768)
    RH = Region(nc, "RH", 32768)
    NWS = 3
    RW = Region(nc, "RW", NWS * 8192, gran=8192)
    RU = Region(nc, "RU", 16384)
    RS = Region(nc, "RS", 32768)
    RY = Region(nc, "RY", 16384)
    RT = Region(nc, "RT", 2 * 10240, gran=2048)
    RM = Region(nc, "RM", 36608, gran=256)
    PS = Region(nc, "PSR", 16384, gran=2048, psum=True)

    misc_off = [0]

    def malloc(dtype, shape):
        n = 1
        for s in shape:
            n *= s
        nb = n * (4 if dtype == F32 else 2)
        nb = (nb + 255) // 256 * 256
        v = RM.view(misc_off[0], dtype, shape)
        misc_off[0] += nb
        return v

    def psv(bank, dtype, shape, off=0):
        return PS.view(bank * 2048 + off, dtype, shape)

    pe, act, dve, pool = nc.tensor, nc.scalar, nc.vector, nc.gpsimd

    flip = [0]

    def evac_eng():
        flip[0] ^= 1
        return 'act' if flip[0] else 'dve'

    def copy_on(e, out_ap, in_ap, reads, writes):
        if e == 'act':
            T.op('act', lambda: act.copy(out=out_ap, in_=in_ap), reads, writes)
        elif e == 'dve':
            T.op('dve', lambda: dve.tensor_copy(out=out_ap, in_=in_ap), reads, writes)
        else:
            T.op('pool', lambda: pool.tensor_copy(out=out_ap, in_=in_ap), reads, writes)

    def tt(e, out_ap, a, b, op, reads, writes):
        en = dve if e == 'dve' else pool
        T.op(e, lambda: en.tensor_tensor(out=out_ap, in0=a, in1=b, op=op), reads, writes)

    cst = malloc(F32, [3, 128])
    T.dma('sp', 'su1', lambda: nc.sync.dma_start(out=cst.ap, in_=consts), [], [cst])
    ident_f = cst.ap[:, 0, :]
    identb = malloc(BF16, [128])
    T.op('dve', lambda: dve.tensor_copy(out=identb.ap, in_=cst.ap[:, 0, :]), [cst], [identb])
    ident_b = identb.ap
    maskL = cst.ap[:, 1, :]
    maskU = cst.ap[:, 2, :]
    qm = malloc(F32, [16])
    T.dma('sp', 'su2', lambda: nc.sync.dma_start(out=qm.ap, in_=qmask), [], [qm])
    bmod = malloc(F32, [NMOD])
    T.dma('sp', 'su3', lambda: nc.sync.dma_start(out=bmod.ap, in_=b_mod), [], [bmod])
    bglu = malloc(F32, [KS])
    T.dma('sp', 'su4', lambda: nc.sync.dma_start(out=bglu.ap, in_=b_glu), [], [bglu])
    cw = malloc(F32, [3, KCV])
    T.dma('sp', 'su5', lambda: nc.sync.dma_start(out=cw.ap, in_=conv_w), [], [cw])
    cb = malloc(F32, [KCV])
    T.dma('sp', 'su6', lambda: nc.sync.dma_start(out=cb.ap, in_=conv_b), [], [cb])
    dvec = malloc(F32, [NG])
    T.dma('sp', 'su7', lambda: nc.sync.dma_start(out=dvec.ap, in_=ssm_dv), [], [dvec])
    cnd = malloc(F32, [KC, 2])
    T.dma('sp', 'su8', lambda: nc.sync.dma_start(out=cnd.ap, in_=condT), [], [cnd])

    wslot = [0]

    def load_w(src2d, kcn):
        s = wslot[0] % NWS
        wslot[0] += 1
        v = RW.view(s * 8192, BF16, [kcn, 128])
        flat = RW.view(s * 8192, BF16, [kcn * 128])
        T.dma('pool', 'w%d' % s,
              lambda: nc.gpsimd.dma_start(out=flat.ap, in_=src2d, max_dma_last_dim=4096),
              [], [v])
        return v

    pbank = [0]

    def proj(wv, kcn, rhs_view, ncols=TB):
        b = pbank[0] % 4
        pbank[0] += 1
        pv = psv(b, F32, [ncols])
        for kc in range(kcn):
            T.op('pe', lambda kc=kc: pe.matmul(pv.ap, lhsT=wv.ap[:, kc, :], rhs=rhs_view.ap[:, kc, :ncols],
                                               start=(kc == 0), stop=(kc == kcn - 1)),
                 [wv, rhs_view], [pv], inc=(kc == kcn - 1))
        return pv

    csg = malloc(F32, [KC, 2])
    T.op('act', lambda: act.activation(out=csg.ap, in_=cnd.ap, func=AF.Sigmoid), [cnd], [csg])
    csl = malloc(BF16, [KC, 2])
    tt('dve', csl.ap, csg.ap, cnd.ap, ALU.mult, [csg, cnd], [csl])
    modv = malloc(F32, [NMOD, 2])
    def mod_chunk(mc):
        wv = load_w(w_mod[mc], KC)
        pv = proj(wv, KC, csl, ncols=2)
        add1 = 1.0 if (KC <= mc < 2 * KC) else 0.0
        T.op('dve', lambda: dve.tensor_scalar(
            out=modv.ap[:, mc, :], in0=pv.ap, scalar1=bmod.ap[:, mc:mc + 1], scalar2=add1,
            op0=ALU.add, op1=ALU.add), [pv, bmod], [modv.sub(mc * 2, mc * 2 + 2)])

    mod_next = [0]
    mod_per = (NMOD + NU - 1) // NU

    def mod_some(n):
        for _ in range(n):
            if mod_next[0] < NMOD:
                mod_chunk(mod_next[0])
                mod_next[0] += 1

    G2 = 2 * GQ
    sm_in = None
    h0 = malloc(F32, [2, 2, GQ])
    T.dma('sp', 'su10', lambda: nc.sync.dma_start(out=h0.ap, in_=s5_h0), [], [h0])

    su_regs = [[RU, 0], [RY, 0], [RT, 0]]

    def salloc(dtype, shape):
        n = 1
        for s_ in shape:
            n *= s_
        nb = (n * (4 if dtype == F32 else 2) + 511) // 512 * 512
        for rr in su_regs:
            if rr[1] + nb <= rr[0].nbytes:
                v = rr[0].view(rr[1], dtype, shape)
                rr[1] += nb
                return v
        raise RuntimeError("setup scratch exhausted")

    def sm(persist=False):
        return malloc(F32, [G2]) if persist else salloc(F32, [G2])

    sm_in = RX.view(32768 - 2048, F32, [3, G2])
    T.dma('sp', 'su_smin', lambda: nc.sync.dma_start(out=sm_in.ap, in_=s5_sm), [], [sm_in])

    def dv(fn, reads, writes):
        T.op('dve', fn, reads, writes)

    def av(fn, reads, writes):
        T.op('act', fn, reads, writes)

    dt_ = sm(); ar = sm(); th = sm(); er = sm(); kk = sm(); sn = sm(); cs = sm()
    av(lambda: act.activation(out=dt_.ap, in_=sm_in.ap[:, 0, :], func=AF.Exp), [sm_in], [dt_])
    dv(lambda: dve.tensor_tensor(out=ar.ap, in0=sm_in.ap[:, 1, :], in1=dt_.ap, op=ALU.mult), [sm_in, dt_], [ar])
    dv(lambda: dve.tensor_tensor(out=th.ap, in0=sm_in.ap[:, 2, :], in1=dt_.ap, op=ALU.mult), [sm_in, dt_], [th])
    av(lambda: act.activation(out=er.ap, in_=ar.ap, func=AF.Exp), [ar], [er])

    def sin_of(dst, src, shift):
        t1 = sm()
        dv(lambda: dve.tensor_scalar(out=t1.ap, in0=src.ap, scalar1=shift, scalar2=1.0 / TWO_PI,
                                     op0=ALU.add, op1=ALU.mult), [src], [t1])
        dv(lambda: dve.tensor_scalar(out=kk.ap, in0=t1.ap, scalar1=MAGIC, scalar2=None, op0=ALU.add),
           [t1], [kk])
        dv(lambda: dve.tensor_scalar(out=kk.ap, in0=kk.ap, scalar1=-MAGIC, scalar2=None, op0=ALU.add),
           [kk], [kk])
        dv(lambda: dve.tensor_tensor(out=t1.ap, in0=t1.ap, in1=kk.ap, op=ALU.subtract), [t1, kk], [t1])
        dv(lambda: dve.tensor_scalar(out=t1.ap, in0=t1.ap, scalar1=TWO_PI, scalar2=3.14159,
                                     op0=ALU.mult, op1=ALU.min), [t1], [t1])
        dv(lambda: dve.tensor_scalar(out=t1.ap, in0=t1.ap, scalar1=-3.14159, scalar2=None, op0=ALU.max),
           [t1], [t1])
        av(lambda: act.activation(out=dst.ap, in_=t1.ap, func=AF.Sin), [t1], [dst])

    sin_of(sn, th, 0.0)
    sin_of(cs, th, math.pi / 2)

    cm_t = [sm() for _ in range(4)]

    def cmul(outr, outi, ar_, ai_, br_, bi_, n_=None):
        t1, t2, t3, t4 = cm_t
        dv(lambda: dve.tensor_tensor(out=t1.ap, in0=ar_.ap, in1=br_.ap, op=ALU.mult), [ar_, br_], [t1])
        dv(lambda: dve.tensor_tensor(out=t2.ap, in0=ai_.ap, in1=bi_.ap, op=ALU.mult), [ai_, bi_], [t2])
        dv(lambda: dve.tensor_tensor(out=t3.ap, in0=ar_.ap, in1=bi_.ap, op=ALU.mult), [ar_, bi_], [t3])
        dv(lambda: dve.tensor_tensor(out=t4.ap, in0=ai_.ap, in1=br_.ap, op=ALU.mult), [ai_, br_], [t4])
        dv(lambda: dve.tensor_tensor(out=outr.ap, in0=t1.ap, in1=t2.ap, op=ALU.subtract), [t1, t2], [outr])
        dv(lambda: dve.tensor_tensor(out=outi.ap, in0=t3.ap, in1=t4.ap, op=ALU.add), [t3, t4], [outi])

    PWr = [sm(persist=(k_ in (0, 8))) for k_ in range(9)]
    PWi = [sm(persist=(k_ in (0, 8))) for k_ in range(9)]
    dv(lambda: dve.memset(PWr[0].ap, 1.0), [], [PWr[0]])
    dv(lambda: dve.memset(PWi[0].ap, 0.0), [], [PWi[0]])
    dv(lambda: dve.tensor_tensor(out=PWr[1].ap, in0=er.ap, in1=cs.ap, op=ALU.mult), [er, cs], [PWr[1]])
    dv(lambda: dve.tensor_tensor(out=PWi[1].ap, in0=er.ap, in1=sn.ap, op=ALU.mult), [er, sn], [PWi[1]])
    for k in range(2, 9):
        cmul(PWr[k], PWi[k], PWr[k - 1], PWi[k - 1], PWr[1], PWi[1])
    nr = sm(); den = sm(); qr = sm(); qi = sm(); t5 = sm(); t6 = sm()
    a_re = sm_in.ap[:, 1, :]
    a_im = sm_in.ap[:, 2, :]
    dv(lambda: dve.tensor_scalar(out=nr.ap, in0=PWr[1].ap, scalar1=-1.0, scalar2=None, op0=ALU.add), [PWr[1]], [nr])
    dv(lambda: dve.tensor_tensor(out=den.ap, in0=a_re, in1=a_re, op=ALU.mult), [sm_in], [den])
    dv(lambda: dve.tensor_tensor(out=t5.ap, in0=a_im, in1=a_im, op=ALU.mult), [sm_in], [t5])
    dv(lambda: dve.tensor_tensor(out=den.ap, in0=den.ap, in1=t5.ap, op=ALU.add), [den, t5], [den])
    dv(lambda: dve.reciprocal(out=den.ap, in_=den.ap), [den], [den])
    dv(lambda: dve.tensor_tensor(out=t5.ap, in0=nr.ap, in1=a_re, op=ALU.mult), [nr, sm_in], [t5])
    dv(lambda: dve.tensor_tensor(out=t6.ap, in0=PWi[1].ap, in1=a_im, op=ALU.mult), [PWi[1], sm_in], [t6])
    dv(lambda: dve.tensor_tensor(out=t5.ap, in0=t5.ap, in1=t6.ap, op=ALU.add), [t5, t6], [t5])
    dv(lambda: dve.tensor_tensor(out=qr.ap, in0=t5.ap, in1=den.ap, op=ALU.mult), [t5, den], [qr])
    dv(lambda: dve.tensor_tensor(out=t5.ap, in0=PWi[1].ap, in1=a_re, op=ALU.mult), [PWi[1], sm_in], [t5])
    dv(lambda: dve.tensor_tensor(out=t6.ap, in0=nr.ap, in1=a_im, op=ALU.mult), [nr, sm_in], [t6])
    dv(lambda: dve.tensor_tensor(out=t5.ap, in0=t5.ap, in1=t6.ap, op=ALU.subtract), [t5, t6], [t5])
    dv(lambda: dve.tensor_tensor(out=qi.ap, in0=t5.ap, in1=den.ap, op=ALU.mult), [t5, den], [qi])
    ivr = sm(); ivi = sm()
    dv(lambda: dve.tensor_tensor(out=t5.ap, in0=PWr[8].ap, in1=PWr[8].ap, op=ALU.mult), [PWr[8]], [t5])
    dv(lambda: dve.tensor_tensor(out=t6.ap, in0=PWi[8].ap, in1=PWi[8].ap, op=ALU.mult), [PWi[8]], [t6])
    dv(lambda: dve.tensor_tensor(out=t5.ap, in0=t5.ap, in1=t6.ap, op=ALU.add), [t5, t6], [t5])
    dv(lambda: dve.reciprocal(out=t5.ap, in_=t5.ap), [t5], [t5])
    dv(lambda: dve.tensor_tensor(out=ivr.ap, in0=PWr[8].ap, in1=t5.ap, op=ALU.mult), [PWr[8], t5], [ivr])
    dv(lambda: dve.scalar_tensor_tensor(out=ivi.ap, in0=PWi[8].ap, scalar=-1.0, in1=t5.ap,
                                        op0=ALU.mult, op1=ALU.mult), [PWi[8], t5], [ivi])

    PWin = [salloc(F32, [2, GQ, 8]) for _ in range(2)]
    PWout = [salloc(F32, [2, GQ, 8]) for _ in range(2)]
    PZ = [salloc(F32, [2, GQ, 8]) for _ in range(2)]
    for ri, PWx in ((0, PWr), (1, PWi)):
        for s in range(8):
            for d in range(2):
                kin = (7 - s) if d == 0 else s
                kout = (s + 1) if d == 0 else (8 - s)
                src_in = PWx[kin].ap[:, d * GQ:(d + 1) * GQ]
                src_out = PWx[kout].ap[:, d * GQ:(d + 1) * GQ]
                T.op('pool', lambda ri=ri, d=d, s=s, src_in=src_in: pool.tensor_copy(
                    out=PWin[ri].ap[:, d, :, s], in_=src_in), [PWx[kin]], [PWin[ri]])
                T.op('pool', lambda ri=ri, d=d, s=s, src_out=src_out: pool.tensor_copy(
                    out=PWout[ri].ap[:, d, :, s], in_=src_out), [PWx[kout]], [PWout[ri]])
    ivr3 = ivr.ap.rearrange("p (d g) -> p d g", d=2).unsqueeze(3).to_broadcast([128, 2, GQ, 8])
    ivi3 = ivi.ap.rearrange("p (d g) -> p d g", d=2).unsqueeze(3).to_broadcast([128, 2, GQ, 8])
    z1 = salloc(F32, [2, GQ, 8]); z2 = salloc(F32, [2, GQ, 8])
    tt('dve', z1.ap, PWout[0].ap, ivr3, ALU.mult, [PWout[0], ivr], [z1])
    tt('dve', z2.ap, PWout[1].ap, ivi3, ALU.mult, [PWout[1], ivi], [z2])
    tt('dve', PZ[0].ap, z1.ap, z2.ap, ALU.subtract, [z1, z2], [PZ[0]])
    tt('dve', z1.ap, PWout[0].ap, ivi3, ALU.mult, [PWout[0], ivi], [z1])
    tt('dve', z2.ap, PWout[1].ap, ivr3, ALU.mult, [PWout[1], ivr], [z2])
    tt('dve', PZ[1].ap, z1.ap, z2.ap, ALU.add, [z1, z2], [PZ[1]])

    L8r = PWr[8].ap.rearrange("p (d g) -> p d g", d=2)
    L8i = PWi[8].ap.rearrange("p (d g) -> p d g", d=2)
    LrrN = {}
    LiiN = {}
    for ns_ in (1, 2):
        LrrN[ns_] = malloc(F32, [2, GQ, ns_, 2])
        LiiN[ns_] = malloc(F32, [2, GQ, ns_, 2])
        for s_ in range(ns_):
            for ri in range(2):
                dv(lambda ns_=ns_, s_=s_, ri=ri: dve.tensor_copy(out=LrrN[ns_].ap[:, :, :, s_, ri], in_=L8r),
                   [PWr[8]], [LrrN[ns_]])
            dv(lambda ns_=ns_, s_=s_: dve.tensor_scalar(out=LiiN[ns_].ap[:, :, :, s_, 0], in0=L8i, scalar1=-1.0,
                                                        scalar2=None, op0=ALU.mult), [PWi[8]], [LiiN[ns_]])
            dv(lambda ns_=ns_, s_=s_: dve.tensor_copy(out=LiiN[ns_].ap[:, :, :, s_, 1], in_=L8i), [PWi[8]], [LiiN[ns_]])

    curR, curI = PWr[8], PWi[8]
    pp = [(sm(), sm()), (sm(), sm())]
    bp1 = (sm(True), sm(True))
    for it in range(6):
        nr_, ni_ = bp1 if it == 5 else pp[it % 2]
        cmul(nr_, ni_, curR, curI, curR, curI)
        curR, curI = nr_, ni_
    BPr = [PWr[0], curR, sm(True), sm(True)]
    BPi = [PWi[0], curI, sm(True), sm(True)]
    cmul(BPr[2], BPi[2], curR, curI, curR, curI)
    cmul(BPr[3], BPi[3], BPr[2], BPi[2], curR, curI)

    bc_in = RS.view(0, F32, [4, 2, GQ, 16])
    T.dma('sp', 'su11', lambda: nc.sync.dma_start(out=RS.view(0, F32, [4, G2 * 16]).ap, in_=s5_bc), [], [bc_in])
    BB = RH.view(0, F32, [2, 2, GQ, 16])
    tb1 = RH.view(16384, F32, [2, GQ, 16])
    tb2 = RH.view(24576, F32, [2, GQ, 16])
    q3r = qr.ap.rearrange("p (d g) -> p d g", d=2).unsqueeze(3).to_broadcast([128, 2, GQ, 16])
    q3i = qi.ap.rearrange("p (d g) -> p d g", d=2).unsqueeze(3).to_broadcast([128, 2, GQ, 16])
    tt('dve', tb1.ap, bc_in.ap[:, 0], q3r, ALU.mult, [bc_in, qr], [tb1])
    tt('dve', tb2.ap, bc_in.ap[:, 1], q3i, ALU.mult, [bc_in, qi], [tb2])
    tt('dve', BB.ap[:, 0], tb1.ap, tb2.ap, ALU.subtract, [tb1, tb2], [BB])
    tt('dve', tb1.ap, bc_in.ap[:, 0], q3i, ALU.mult, [bc_in, qi], [tb1])
    tt('dve', tb2.ap, bc_in.ap[:, 1], q3r, ALU.mult, [bc_in, qr], [tb2])
    tt('dve', BB.ap[:, 1], tb1.ap, tb2.ap, ALU.add, [tb1, tb2], [BB])

    uo = [0]

    def rxv(dtype, shape):
        n = 1
        for s_ in shape:
            n *= s_
        nb = n * (4 if dtype == F32 else 2)
        v = RX.view(uo[0], dtype, shape)
        uo[0] += (nb + 511) // 512 * 512
        return v

    m1 = rxv(F32, [2, TU, 8, 16]); m2 = rxv(F32, [2, TU, 8, 16])
    A_t = [rxv(BF16, [2, TU, 128]) for _ in range(2)]
    Wo_t = rxv(BF16, [TU, 2, 2, 128])
    Z_t = [rxv(BF16, [2, TU, 128]) for _ in range(2)]
    Mt_sb = rxv(BF16, [8, 128])
    Win_sb = rxv(BF16, [8, 2, 2, 64])
    mtmp = rxv(F32, [128])
    assert uo[0] <= 32768 - 2048

    def ctable(outr_ap, outi_ap, neg_im, pwr, pwi, br_ap, bi_ap, u, w_keys, rk):
        g0 = u * TU
        for d in range(2):
            pr = pwr.ap[:, d, g0:g0 + TU, :].unsqueeze(3).to_broadcast([128, TU, 8, 16])
            pi_ = pwi.ap[:, d, g0:g0 + TU, :].unsqueeze(3).to_broadcast([128, TU, 8, 16])
            br = br_ap[:, d, g0:g0 + TU, :].unsqueeze(2).to_broadcast([128, TU, 8, 16])
            bi = bi_ap[:, d, g0:g0 + TU, :].unsqueeze(2).to_broadcast([128, TU, 8, 16])
            m1d = m1.ap[:, d]
            m2d = m2.ap[:, d]
            hsz = TU * 8 * 16
            k1 = m1.sub(d * hsz, (d + 1) * hsz)
            k2 = m2.sub(d * hsz, (d + 1) * hsz)
            tt('dve', m1d, pr, br, ALU.mult, rk, [k1])
            tt('pool', m2d, pi_, bi, ALU.mult, rk, [k2])
            tt('dve', outr_ap(d), m1d, m2d, ALU.subtract, [k1, k2], w_keys)
            tt('dve', m1d, pr, bi, ALU.mult, rk, [k1])
            tt('pool', m2d, pi_, br, ALU.mult, rk, [k2])
            if neg_im:
                T.op('dve', lambda d=d, m1d=m1d, m2d=m2d: dve.scalar_tensor_tensor(
                    out=outi_ap(d), in0=m1d, scalar=-1.0, in1=m2d, op0=ALU.mult, op1=ALU.subtract), [k1, k2], w_keys)
            else:
                tt('dve', outi_ap(d), m1d, m2d, ALU.add, [k1, k2], w_keys)

    def v5(view2):
        return lambda d: view2.ap[:, d].rearrange("p g (k c) -> p g k c", k=8)

    for u in range(NU):
        g0 = u * TU
        ctable(v5(A_t[0]), v5(A_t[1]), False, PWin[0], PWin[1], BB.ap[:, 0], BB.ap[:, 1], u,
               [A_t[0], A_t[1]], [PWin[0], PWin[1], BB])
        wo_r = lambda d: Wo_t.ap[:, :, d, 0, :].rearrange("p g (k c) -> p g k c", k=8)
        wo_i = lambda d: Wo_t.ap[:, :, d, 1, :].rearrange("p g (k c) -> p g k c", k=8)
        ctable(wo_r, wo_i, True, PWout[0], PWout[1], bc_in.ap[:, 2], bc_in.ap[:, 3], u,
               [Wo_t], [PWout[0], PWout[1], bc_in])
        ctable(v5(Z_t[0]), v5(Z_t[1]), True, PZ[0], PZ[1], bc_in.ap[:, 2], bc_in.ap[:, 3], u,
               [Z_t[0], Z_t[1]], [PZ[0], PZ[1], bc_in])
        T.dma('sp', 'tabw', lambda u=u: nc.sync.dma_start(
            out=Wouttab[u], in_=RX.view(Wo_t.off, BF16, [TU * 2 * 2 * 128]).ap), [Wo_t], [('Wouttab',)])
        for gl in range(8):
            gql, gp = gl // 2, gl % 2
            g = (g0 + gql) * 2 + gp
            ps_ = slice(gp * 64, gp * 64 + 64)
            pf = psv(4, F32, [128]); pb = psv(5, F32, [128])
            for d, pv in ((0, pf), (1, pb)):
                T.op('pe', lambda d=d, pv=pv: pe.matmul(pv.ap, lhsT=A_t[0].ap[ps_, d, gql, :],
                                                        rhs=Z_t[0].ap[ps_, d, gql, :], start=True, stop=False),
                     [A_t[0], Z_t[0]], [pv], inc=False)
                T.op('pe', lambda d=d, pv=pv: pe.matmul(pv.ap, lhsT=A_t[1].ap[ps_, d, gql, :],
                                                        rhs=Z_t[1].ap[ps_, d, gql, :], start=False, stop=True),
                     [A_t[1], Z_t[1]], [pv])
            tt('dve', mtmp.ap, pf.ap, maskL, ALU.mult, [pf, cst], [mtmp])
            T.op('dve', lambda g=g: dve.scalar_tensor_tensor(out=mtmp.ap, in0=ident_f, scalar=dvec.ap[:, g:g + 1],
                                                            in1=mtmp.ap, op0=ALU.mult, op1=ALU.add),
                 [mtmp, cst, dvec], [mtmp])
            m3 = m1
            m3ap = RX.view(m1.off, F32, [128]).ap
            tt('dve', m3ap, pb.ap, maskU, ALU.mult, [pb, cst], [m1])
            tt('dve', Mt_sb.ap[:, gl, :], mtmp.ap, m3ap, ALU.add, [mtmp, m1], [Mt_sb])
            for d in range(2):
                for ri in range(2):
                    pt = psv(6, BF16, [2, 2, 64])
                    T.op('pe', lambda d=d, ri=ri, pt=pt: pe.transpose(
                        pt.ap[:, d, ri, :], A_t[ri].ap[ps_, d, gql, :], ident_b[ps_, ps_]),
                        [A_t[ri], identb], [pt], inc=(d == 1 and ri == 1))
            copy_on('act', Win_sb.ap[:, gl], psv(6, BF16, [2, 2, 64]).ap, [psv(6, BF16, [2, 2, 64])], [Win_sb])
        T.dma('sp', 'tabm', lambda u=u: nc.sync.dma_start(
            out=Mtab[u], in_=RX.view(Mt_sb.off, BF16, [8 * 128]).ap), [Mt_sb], [('Mtab',)])
        T.dma('sp', 'tabi', lambda u=u: nc.sync.dma_start(
            out=Wintab[u], in_=RX.view(Win_sb.off, BF16, [8 * 2 * 2 * 64]).ap), [Win_sb], [('Wintab',)])
        mod_some(mod_per)
    mod_some(NMOD)

    hT = RH.view(0, BF16, [KC, TB])
    Utm = RU.view(0, BF16, [NG, 64])
    yaT = RU.view(0, BF16, [KS, TB])
    yT = RY.view(0, BF16, [KS, TB])
    ybT = RY.view(0, BF16, [KCV, TB])
    mergedT = RS.view(0, BF16, [KC, TB])
    Sv = RX.view(0, F32, [2, GQP, NMC, 2])
    Ug = RS.view(0, BF16, [GQP * 2, NMC])
    Hb = RS.view(8192, BF16, [2, 2, GQP, NMC])
    Ytm2 = [RS.view(24576 + i_ * 2048, BF16, [4, 128]) for i_ in range(2)]
    Yg2 = [RS.view(24576 + 1024 + i_ * 2048, BF16, [8, NMC]) for i_ in range(2)]
    XT = [RX.view(0, F32, [D]), RX.view(16384, F32, [D]), RH.view(0, F32, [D]), RH.view(16384, F32, [D])]
    lng_bc = RU.view(0, F32, [D])
    lnb_bc = RY.view(0, F32, [D])
    STt = malloc(F32, [2, 2, 2, GQ])
    Ecar = malloc(F32, [3, 2, GQ, 2])
    Fcar = malloc(F32, [2, GQ, 2])
    tmpA = [malloc(F32, [TB]) for _ in range(4)]
    utmp = [malloc(BF16, [TB]) for _ in range(2)]
    stats = malloc(F32, [8, 6])
    mv = malloc(F32, [2])
    rstd = malloc(F32, [1])
    nmr = malloc(F32, [1])
    sP = RM.view(tmpA[2].off, F32, [2, GQP * 2, 2]); sQ = RM.view(tmpA[3].off, F32, [2, GQP * 2, 2])

    tsl = [0, 0]

    def load_tables(u, which):
        if which == 'front':
            sl = tsl[0] % 2
            tsl[0] += 1
            base = sl * 4096
            Wiv = RT.view(base, BF16, [8, 2, 2, 64])
            T.dma('sp', 'tf%d' % sl, lambda: nc.sync.dma_start(
                out=RT.view(base, BF16, [8 * 2 * 2 * 64]).ap, in_=Wintab[u]), [('Wintab',)], [Wiv])
            return None, Wiv, None
        sl = tsl[1] % 2
        tsl[1] += 1
        base = 8192 + sl * 6144
        Mv = RT.view(base, BF16, [8, 128])
        Wov = RT.view(base + 2048, BF16, [TU, 2, 2, 128])
        T.dma('sp', 'tb%d' % sl, lambda: nc.sync.dma_start(
            out=RT.view(base, BF16, [8 * 128]).ap, in_=Mtab[u]), [('Mtab',)], [Mv, Wov])
        T.dma('sp', 'tb%d' % sl, lambda: nc.sync.dma_start(
            out=RT.view(base + 2048, BF16, [TU * 2 * 2 * 128]).ap, in_=Wouttab[u]), [('Wouttab',)], [Mv, Wov])
        return Mv, None, Wov

    def ln_stats(xv):
        nch = max(1, D // 512)
        cw_ = D // nch
        for c in range(nch):
            T.op('dve', lambda c=c: dve.bn_stats(out=stats.ap[:, c, :], in_=xv.ap[:, c * cw_:(c + 1) * cw_]),
                 [xv], [stats])
        T.op('dve', lambda: dve.bn_aggr(out=mv.ap, in_=stats.ap[:, :nch, :].rearrange('p a b -> p (a b)')), [stats], [mv])
        T.op('act', lambda: act.activation(out=rstd.ap, in_=mv.ap[:, 1:2], func=AF.Sqrt, bias=epsv.ap[:, 0:1]),
             [mv, epsv], [rstd])
        T.op('dve', lambda: dve.reciprocal(out=rstd.ap, in_=rstd.ap), [rstd], [rstd])
        T.op('dve', lambda: dve.scalar_tensor_tensor(out=nmr.ap, in0=mv.ap[:, 0:1], scalar=-1.0, in1=rstd.ap,
                                                     op0=ALU.mult, op1=ALU.mult), [mv, rstd], [nmr])
        T.op('act', lambda: act.activation(out=xv.ap, in_=xv.ap, func=AF.Identity, bias=nmr.ap[:, 0:1],
                                           scale=rstd.ap[:, 0:1]), [xv, nmr, rstd], [xv])

    epsv = malloc(F32, [1])
    dv(lambda: dve.memset(epsv.ap, LN_EPS), [], [epsv])

    def stage_A(blk, cond_i):
        def load_ln(j):
            xv = XT[j % 2]
            T.dma('sp', 'x%d' % (j % 2), lambda: nc.sync.dma_start(out=xv.ap, in_=xb[blk, j]), [], [xv])
            ln_stats(xv)

        def transposes(j):
            xv = XT[j % 2]
            for kc in range(KC):
                pv = psv(4 + (kc // 4) % 4, F32, [128], off=(kc % 4) * 512)
                T.op('pe', lambda kc=kc, pv=pv: pe.transpose(pv.ap, xv.ap[:, kc * 128:(kc + 1) * 128], ident_f),
                     [xv.sub(kc * 128, kc * 128 + 128), cst], [pv])
                dst = hT.ap[:, kc, j * 128:(j + 1) * 128]
                wk = hT.sub(kc * TB + j * 128, kc * TB + j * 128 + 128)
                sc = modv.ap[:, KC + kc, cond_i:cond_i + 1]
                sh = modv.ap[:, kc, cond_i:cond_i + 1]
                if (kc // 4) % 2 == 0:
                    T.op('act', lambda pv=pv, dst=dst, sc=sc, sh=sh: act.activation(
                        out=dst, in_=pv.ap, func=AF.Identity, bias=sh, scale=sc), [pv, modv], [wk])
                else:
                    T.op('dve', lambda pv=pv, dst=dst, sc=sc, sh=sh: dve.tensor_scalar(
                        out=dst, in0=pv.ap, scalar1=sc, scalar2=sh, op0=ALU.mult, op1=ALU.add), [pv, modv], [wk])

        load_ln(0)
        for j in range(4):
            if j + 1 < 4:
                load_ln(j + 1)
            transposes(j)

    def stage_U():
        for ci in range(KS):
            wv = load_w(w_in[ci], KC)
            pv = proj(wv, KC, hT)
            ut = utmp[ci % 2]
            copy_on('act', ut.ap, pv.ap, [pv], [ut])
            pt = psv(4 + ci % 2, BF16, [4, 128])
            for j in range(4):
                T.op('pe', lambda j=j, pt=pt, ut=ut: pe.transpose(pt.ap[:, j, :], ut.ap[:, j * 128:(j + 1) * 128], ident_b),
                     [ut, identb], [pt], inc=(j == 3))
            wk = Utm.sub(ci * 8 * 64, (ci + 1) * 8 * 64)
            T.op('dve', lambda ci=ci, pt=pt: dve.tensor_copy(
                out=Utm.ap[:, ci * 8:(ci + 1) * 8, :].rearrange("p g (j c) -> p g j c", j=4),
                in_=pt.ap.rearrange("p j (g c) -> p g j c", g=8)), [pt], [wk])

    def s5_front(ps_i, tabs, nseq):
        for ul in range(UPP):
            u = ps_i * UPP + ul
            Mv, Wiv, Wov = tabs[ul]
            pu = psv(6 + ul % 2, BF16, [8, NMC])
            for gl in range(8):
                g = u * 8 + gl
                for h in range(2):
                    hs = slice(h * 64, h * 64 + 64)
                    src = Utm.ap[hs, g, :]
                    T.op('pe', lambda gl=gl, hs=hs, src=src, pu=pu: pe.transpose(
                        pu.ap[hs, gl, :], src, ident_b[hs, hs]), [Utm, identb], [pu], inc=(gl == 7 and h == 1))
            ugd = Ug.ap[:, ul * 8:(ul + 1) * 8, :]
            ugk = Ug.sub(ul * 8 * NMC, (ul + 1) * 8 * NMC)
            copy_on('act', ugd, pu.ap, [pu], [ugk])
            pS = PS.view(0 * 2048 + (ul % 2) * 4096, F32, [2, 2, TU, NMC])
            for gl in range(8):
                gql, gp = gl // 2, gl % 2
                ps_ = slice(gp * 64, gp * 64 + 64)
                for d in range(2):
                    for ri in range(2):
                        last = (gl == 7 and d == 1 and ri == 1)
                        T.op('pe', lambda gl=gl, gql=gql, d=d, ri=ri, ul=ul, pS=pS, ps_=ps_, Wiv=Wiv: pe.matmul(
                            pS.ap[ps_, d, ri, gql, :], lhsT=Wiv.ap[:, gl, d, ri, :], rhs=Ug.ap[:, ul * 8 + gl, :],
                            start=True, stop=True), [Wiv, ugk], [pS], inc=last)
            nl_ = NMC // nseq
            wk = []
            for d in range(2):
                o = (d * GQP + ul * TU) * NMC * 2
                wk += Sv.sub(o, o + TU * NMC * 2)
            dst0 = Sv.ap[:, 0, ul * TU:(ul + 1) * TU, :, :]
            src0 = pS.ap[:, 0].rearrange("p r g n -> p g n r")
            T.op('act', lambda dst0=dst0, src0=src0: act.copy(out=dst0, in_=src0), [pS], [wk])
            for sq in range(nseq):
                dst1 = Sv.ap[:, 1, ul * TU:(ul + 1) * TU, sq * nl_:(sq + 1) * nl_, :]
                src1 = pS.ap[:, 1, :, :, sq * nl_:(sq + 1) * nl_].rearrange("p r g n -> p g n r")[:, :, ::-1, :]
                T.op('dve', lambda dst1=dst1, src1=src1: dve.tensor_copy(out=dst1, in_=src1), [pS], [wk])

    def s5_scan(ps_i, nseq, carry):
        nl = NMC // nseq
        g0 = ps_i * GQP
        S6 = Sv.ap.rearrange("p d g (s k) r -> p d (g s) k r", s=nseq)
        lrr = LrrN[nseq].ap.rearrange("p d g s r -> p d (g s) r")[:, :, g0 * nseq:(g0 + GQP) * nseq, :]
        lii = LiiN[nseq].ap.rearrange("p d g s r -> p d (g s) r")[:, :, g0 * nseq:(g0 + GQP) * nseq, :]
        W = GQP * nseq
        p_ = sP.ap[:, :, :W, :]
        q_ = sQ.ap[:, :, :W, :]
        for k in range(nl):
            cur = S6[:, :, :, k, :]
            if k == 0:
                if carry is None:
                    continue
                prev = carry.ap[:, :, g0:g0 + GQP, :]
                prev_sw = carry.ap[:, :, g0:g0 + GQP, ::-1]
                rk = [carry]
            else:
                prev = S6[:, :, :, k - 1, :]
                prev_sw = S6[:, :, :, k - 1, ::-1]
                rk = [Sv]
            tt('dve', p_, lrr, prev, ALU.mult, rk + [LrrN[nseq]], [sP])
            tt('dve', q_, lii, prev_sw, ALU.mult, rk + [LiiN[nseq]], [sQ])
            tt('dve', p_, p_, q_, ALU.add, [sP, sQ], [sP])
            tt('dve', cur, cur, p_, ALU.add, [Sv, sP], [Sv])

    def s5_back(ps_i, nseq, carry, tabs):
        nl = NMC // nseq
        g0 = ps_i * GQP
        S6 = Sv.ap.rearrange("p d g (s k) r -> p d g s k r", s=nseq)
        H6 = Hb.ap.rearrange("p d r g (s n) -> p d r g s n", s=nseq)
        for r in range(2):
            T.op('act', lambda r=r: act.copy(out=H6[:, 0, r, :, :, 1:], in_=S6[:, 0, :, :, :nl - 1, r]), [Sv], [Hb])
            T.op('dve', lambda r=r: dve.tensor_copy(out=H6[:, 1, r, :, :, :nl - 1], in_=S6[:, 1, :, :, nl - 2::-1, r]), [Sv], [Hb])
        Hm = Hb.ap.rearrange("p d r g (s n) -> p d (r g) s n", s=nseq)
        if carry is None:
            T.op('dve', lambda: dve.memset(Hm[:, 0, :, :, 0], 0.0), [], [Hb])
            T.op('dve', lambda: dve.memset(Hm[:, 1, :, :, nl - 1], 0.0), [], [Hb])
        else:
            for r in range(2):
                T.op('dve', lambda r=r: dve.tensor_copy(out=H6[:, 0, r, :, 0, 0], in_=carry.ap[:, 0, g0:g0 + GQP, r]), [carry], [Hb])
                T.op('dve', lambda r=r: dve.tensor_copy(out=H6[:, 1, r, :, 0, nl - 1], in_=carry.ap[:, 1, g0:g0 + GQP, r]), [carry], [Hb])
        for ul in range(UPP):
            u = ps_i * UPP + ul
            Mv, Wiv, Wov = tabs[ul]
            ugk = Ug.sub(ul * 8 * NMC, (ul + 1) * 8 * NMC)
            pY = psv(ul % 2, F32, [8, NMC])
            for gl in range(8):
                gql, gp = gl // 2, gl % 2
                gq_l = ul * TU + gql
                ps_ = slice(gp * 64, gp * 64 + 64)
                T.op('pe', lambda gl=gl, pY=pY, Mv=Mv: pe.matmul(pY.ap[:, gl, :], lhsT=Mv.ap[:, gl, :],
                                                                 rhs=Ug.ap[:, ul * 8 + gl, :], start=True, stop=False),
                     [Mv, ugk], [pY], inc=False)
                for d in range(2):
                    for ri in range(2):
                        last = (d == 1 and ri == 1)
                        T.op('pe', lambda gl=gl, gql=gql, d=d, ri=ri, pY=pY, Wov=Wov, ps_=ps_, gq_l=gq_l, last=last: pe.matmul(
                            pY.ap[:, gl, :], lhsT=Wov.ap[ps_, gql, d, ri, :], rhs=Hb.ap[ps_, d, ri, gq_l, :],
                            start=False, stop=last), [Wov, Hb], [pY], inc=(last and gl == 7))
            x = PS.view(pY.off, F32, [8 * NMC])
            t0, t1 = (tmpA[0], tmpA[1]) if ul % 2 == 0 else (tmpA[2], tmpA[3])
            Ytm = Ytm2[ul % 2]
            Yg = Yg2[ul % 2]
            T.op('act', lambda: act.activation(out=t0.ap, in_=x.ap, func=AF.Square), [x], [t0])
            T.op('dve', lambda: dve.tensor_scalar(out=t0.ap, in0=t0.ap, scalar1=0.044715, scalar2=1.0,
                                                  op0=ALU.mult, op1=ALU.add), [t0], [t0])
            tt('dve', t0.ap, t0.ap, x.ap, ALU.mult, [t0, x], [t0])
            T.op('act', lambda: act.activation(out=t1.ap, in_=t0.ap, func=AF.Sigmoid, scale=1.5957691216), [t0], [t1])
            tt('dve', Yg.ap.rearrange("p g n -> p (g n)"), t1.ap, x.ap, ALU.mult, [t1, x], [Yg])
            pt = psv(6 + ul % 2, BF16, [8, 4, 16])
            for gl in range(8):
                for h in range(2):
                    hs = slice(h * 64, h * 64 + 64)
                    T.op('pe', lambda gl=gl, hs=hs, pt=pt: pe.transpose(
                        pt.ap[hs, gl, :, :], Yg.ap[hs, gl, :], ident_b[hs, hs]), [Yg, identb], [pt],
                        inc=(gl == 7 and h == 1))
            T.op('dve', lambda pt=pt: dve.tensor_copy(out=Ytm.ap.rearrange("p j (g c) -> p g j c", g=8), in_=pt.ap),
                 [pt], [Ytm])
            pf = psv(4 + ul % 2, BF16, [4, 128])
            for j in range(4):
                T.op('pe', lambda j=j, pf=pf: pe.transpose(pf.ap[:, j, :], Ytm.ap[:, j, :], ident_b),
                     [Ytm, identb], [pf], inc=(j == 3))
            copy_on('act', yT.ap[:, u, :], pf.ap.rearrange("p j c -> p (j c)"), [pf], [yT.sub(u * TB, (u + 1) * TB)])

    def save_states(ps_i, nseq):
        nl = NMC // nseq
        gsl = slice(ps_i * GQP, (ps_i + 1) * GQP)
        S6 = Sv.ap.rearrange("p d g (s k) r -> p d g s k r", s=nseq)
        for sq in range(nseq):
            for d in range(2):
                T.op('dve', lambda sq=sq, d=d: dve.tensor_copy(
                    out=STt.ap[:, sq, d, :, gsl], in_=S6[:, d, :, sq, nl - 1, :].rearrange("p g r -> p r g")),
                    [Sv], [STt])

    def save_E(ps_i, slot):
        gsl = slice(ps_i * GQP, (ps_i + 1) * GQP)
        T.op('dve', lambda: dve.tensor_copy(out=Ecar.ap[:, slot, :, gsl, :], in_=Sv.ap[:, :, :, NMC - 1, :]), [Sv], [Ecar])

    def compute_carry():
        cc_ = [RM.view(tmpA[0].off + i_ * 256, F32, [GQ]) for i_ in range(8)]
        accr, acci, c1, c2, c3, c4, cpr, cpi = cc_
        for d in range(2):
            ds_ = slice(d * GQ, (d + 1) * GQ)

            def mac(pr_ap, pi_ap, er_ap, ei_ap, msk, first, rk):
                tt('dve', c1.ap, pr_ap, er_ap, ALU.mult, rk, [c1])
                tt('dve', c2.ap, pi_ap, ei_ap, ALU.mult, rk, [c2])
                tt('dve', c3.ap, pr_ap, ei_ap, ALU.mult, rk, [c3])
                tt('dve', c4.ap, pi_ap, er_ap, ALU.mult, rk, [c4])
                tt('dve', c1.ap, c1.ap, c2.ap, ALU.subtract, [c1, c2], [c1])
                tt('dve', c3.ap, c3.ap, c4.ap, ALU.add, [c3, c4], [c3])
                if msk is None:
                    if first:
                        dv(lambda: dve.tensor_copy(out=accr.ap, in_=c1.ap), [c1], [accr])
                        dv(lambda: dve.tensor_copy(out=acci.ap, in_=c3.ap), [c3], [acci])
                    else:
                        tt('dve', accr.ap, accr.ap, c1.ap, ALU.add, [accr, c1], [accr])
                        tt('dve', acci.ap, acci.ap, c3.ap, ALU.add, [acci, c3], [acci])
                else:
                    dv(lambda: dve.scalar_tensor_tensor(out=accr.ap, in0=c1.ap, scalar=msk, in1=accr.ap,
                                                        op0=ALU.mult, op1=ALU.add), [c1, accr, qm], [accr])
                    dv(lambda: dve.scalar_tensor_tensor(out=acci.ap, in0=c3.ap, scalar=msk, in1=acci.ap,
                                                        op0=ALU.mult, op1=ALU.add), [c3, acci, qm], [acci])

            ohb = 6 + 4 * d
            dv(lambda: dve.tensor_scalar(out=cpr.ap, in0=BPr[0].ap[:, ds_], scalar1=qm.ap[:, ohb:ohb + 1], scalar2=None,
                                         op0=ALU.mult), [BPr[0], qm], [cpr])
            dv(lambda: dve.tensor_scalar(out=cpi.ap, in0=BPi[0].ap[:, ds_], scalar1=qm.ap[:, ohb:ohb + 1], scalar2=None,
                                         op0=ALU.mult), [BPi[0], qm], [cpi])
            for i in range(1, 4):
                dv(lambda i=i: dve.scalar_tensor_tensor(out=cpr.ap, in0=BPr[i].ap[:, ds_], scalar=qm.ap[:, ohb + i:ohb + i + 1],
                                                        in1=cpr.ap, op0=ALU.mult, op1=ALU.add), [BPr[i], qm, cpr], [cpr])
                dv(lambda i=i: dve.scalar_tensor_tensor(out=cpi.ap, in0=BPi[i].ap[:, ds_], scalar=qm.ap[:, ohb + i:ohb + i + 1],
                                                        in1=cpi.ap, op0=ALU.mult, op1=ALU.add), [BPi[i], qm, cpi], [cpi])
            mac(cpr.ap, cpi.ap, h0.ap[:, d, 0, :], h0.ap[:, d, 1, :], None, True, [cpr, cpi, h0])
            for j in range(1, 4):
                e = (3 - j) if d == 0 else (j - 1)
                mi = (j - 1) + 3 * d
                mac(BPr[e].ap[:, ds_], BPi[e].ap[:, ds_], Ecar.ap[:, j - 1, d, :, 0], Ecar.ap[:, j - 1, d, :, 1],
                    qm.ap[:, mi:mi + 1], False, [BPr[e], BPi[e], Ecar])
            dv(lambda d=d: dve.tensor_copy(out=Fcar.ap[:, d, :, 0], in_=accr.ap), [accr], [Fcar])
            dv(lambda d=d: dve.tensor_copy(out=Fcar.ap[:, d, :, 1], in_=acci.ap), [acci], [Fcar])

    class _Lazy:
        def __init__(self, ps_i, which):
            self.ps_i = ps_i
            self.which = which
            self.cache = {}

        def __getitem__(self, ul):
            if ul not in self.cache:
                self.cache[ul] = load_tables(self.ps_i * UPP + ul, self.which)
            return self.cache[ul]

    def s5_front_units(ps_i, nseq):
        s5_front(ps_i, _Lazy(ps_i, 'front'), nseq)

    def s5_back_units(ps_i, nseq, carry):
        s5_back(ps_i, nseq, carry, _Lazy(ps_i, 'back'))

    def stage_S5_clean(kind, slot=None):
        nseq = 2 if kind == 'prompt' else 1
        carry = Fcar if kind == 'sample' else None
        for ps_i in range(cfg.NPASS):
            s5_front_units(ps_i, nseq)
            s5_scan(ps_i, nseq, carry)
            if kind == 'prompt':
                save_states(ps_i, nseq)
            if kind == 'light':
                save_E(ps_i, slot)
            else:
                s5_back_units(ps_i, nseq, carry)

    def stage_GLU():
        for co in range(KS):
            wv = load_w(w_glu[co], KS)
            pv = proj(wv, KS, yT)
            wz = load_w(w_in[KS + co], KC)
            pz = proj(wz, KC, hT)
            t0, t1, t2 = tmpA[0], tmpA[1], tmpA[2]
            T.op('act', lambda pv=pv, co=co: act.activation(out=t0.ap, in_=pv.ap, func=AF.Sigmoid,
                                                           bias=bglu.ap[:, co:co + 1]), [pv, bglu], [t0])
            tt('dve', t0.ap, t0.ap, yT.ap[:, co, :], ALU.mult, [t0, yT.sub(co * TB, (co + 1) * TB)], [t0])
            T.op('act', lambda pz=pz: act.activation(out=t1.ap, in_=pz.ap, func=AF.Sigmoid), [pz], [t1])
            tt('dve', t1.ap, t1.ap, pz.ap, ALU.mult, [t1, pz], [t1])
            tt('dve', yaT.ap[:, co, :], t0.ap, t1.ap, ALU.mult, [t0, t1], [yaT.sub(co * TB, (co + 1) * TB)])

    def stage_conv(kind):
        per = 32 if kind == 'prompt' else 8
        base_c = 2 * KS
        for cc in range(KCV):
            w1 = load_w(w_in[base_c + cc], KC)
            p_hb = proj(w1, KC, hT)
            w2 = load_w(w_in[base_c + 2 * KCV + cc], KC)
            p_gc = proj(w2, KC, hT)
            t0, t1, t2, t3 = tmpA
            copy_on('act', t0.ap, p_hb.ap, [p_hb], [t0])
            tt('dve', t0.ap, t0.ap, p_gc.ap, ALU.mult, [t0, p_gc], [t0])
            T.op('dve', lambda cc=cc: dve.tensor_scalar(out=t1.ap, in0=t0.ap, scalar1=cw.ap[:, 1, cc:cc + 1],
                                                        scalar2=cb.ap[:, cc:cc + 1], op0=ALU.mult, op1=ALU.add),
                 [t0, cw, cb], [t1])
            v4 = t0.ap.rearrange("p (j h n) -> p j h n", j=4, h=2)
            a4 = t1.ap.rearrange("p (j h n) -> p j h n", j=4, h=2)

            def tap(dst, src, k, cc=cc):
                T.op('dve', lambda: dve.scalar_tensor_tensor(out=dst, in0=src, scalar=cw.ap[:, k, cc:cc + 1], in1=dst,
                                                             op0=ALU.mult, op1=ALU.add), [t0, t1, cw], [t1])
            tap(a4[:, 1:4], v4[:, 0:3], 0)
            tap(a4[:, 0, 1, :], v4[:, 3, 0, :], 0)
            d5 = a4[:, 0, 0, :].rearrange("p (q r) -> p q r", r=per)
            s5_ = v4[:, 3, 1, :].rearrange("p (q r) -> p q r", r=per)
            tap(d5[:, :, 1:], s5_[:, :, :per - 1], 0)
            tap(a4[:, 0:3], v4[:, 1:4], 2)
            tap(a4[:, 3, 0, :], v4[:, 0, 1, :], 2)
            d6 = a4[:, 3, 1, :].rearrange("p (q r) -> p q r", r=per)
            s6 = v4[:, 0, 0, :].rearrange("p (q r) -> p q r", r=per)
            tap(d6[:, :, :per - 1], s6[:, :, 1:], 2)
            w3 = load_w(w_in[base_c + KCV + cc], KC)
            p_gb = proj(w3, KC, hT)
            w4 = load_w(w_in[base_c + 3 * KCV + cc], KC)
            p_zb = proj(w4, KC, hT)
            tt('dve', t1.ap, t1.ap, p_gb.ap, ALU.mult, [t1, p_gb], [t1])
            T.op('act', lambda p_zb=p_zb: act.activation(out=t2.ap, in_=p_zb.ap, func=AF.Sigmoid), [p_zb], [t2])
            tt('dve', t2.ap, t2.ap, p_zb.ap, ALU.mult, [t2, p_zb], [t2])
            tt('dve', ybT.ap[:, cc, :], t1.ap, t2.ap, ALU.mult, [t1, t2], [ybT.sub(cc * TB, (cc + 1) * TB)])

    def stage_merge():
        base_r = 2 * KS + 4 * KCV
        for fc in range(KC):
            wa = load_w(w_ba[fc], KS)
            pa = proj(wa, KS, yaT)
            wb = load_w(w_bb[fc], KCV)
            pb_ = proj(wb, KCV, ybT)
            wra = load_w(w_in[base_r + fc], KC)
            pra = proj(wra, KC, hT)
            wrb = load_w(w_in[base_r + KC + fc], KC)
            prb = proj(wrb, KC, hT)
            t0, t1 = tmpA[0], tmpA[1]
            T.op('act', lambda pra=pra: act.activation(out=t0.ap, in_=pra.ap, func=AF.Sigmoid), [pra], [t0])
            T.op('act', lambda prb=prb: act.activation(out=t1.ap, in_=prb.ap, func=AF.Sigmoid), [prb], [t1])
            tt('dve', t0.ap, t0.ap, pa.ap, ALU.mult, [t0, pa], [t0])
            tt('dve', t1.ap, t1.ap, pb_.ap, ALU.mult, [t1, pb_], [t1])
            tt('dve', mergedT.ap[:, fc, :], t0.ap, t1.ap, ALU.add, [t0, t1], [mergedT.sub(fc * TB, (fc + 1) * TB)])

    def stage_out(blk, cond_i, out_i):
        for j in range(4):
            T.dma('sp', 'xo%d' % j, lambda j=j: nc.sync.dma_start(out=XT[j].ap, in_=xb[blk, j]), [], [XT[j]])
        T.dma('sp', 'lng', lambda: nc.sync.dma_start(out=lng_bc.ap, in_=ln_g.partition_broadcast(128)), [], [lng_bc])
        T.dma('sp', 'lnb', lambda: nc.sync.dma_start(out=lnb_bc.ap, in_=ln_b.partition_broadcast(128)), [], [lnb_bc])
        for fc in range(KC):
            wv = load_w(w_o[fc], KC)
            pv = proj(wv, KC, mergedT)
            t0 = tmpA[2 + fc % 2]
            T.op('act', lambda fc=fc, pv=pv, t0=t0: act.activation(out=t0.ap, in_=pv.ap, func=AF.Identity,
                                                                  scale=modv.ap[:, 2 * KC + fc, cond_i:cond_i + 1]),
                 [pv, modv], [t0])
            pt = psv(4 + fc % 2, F32, [4, 128])
            for j in range(4):
                T.op('pe', lambda j=j, pt=pt, t0=t0: pe.transpose(pt.ap[:, j, :], t0.ap[:, j * 128:(j + 1) * 128], ident_f),
                     [t0, cst], [pt], inc=(j == 3))
            for j in range(4):
                xs = XT[j].ap[:, fc * 128:(fc + 1) * 128]
                T.op('dve', lambda j=j, pt=pt, xs=xs: dve.scalar_tensor_tensor(
                    out=xs, in0=xs, scalar=cfg.ALPHA, in1=pt.ap[:, j, :], op0=ALU.mult, op1=ALU.add),
                    [pt, XT[j].sub(fc * 128, fc * 128 + 128)], [XT[j].sub(fc * 128, fc * 128 + 128)])
        for j in range(4):
            xv = XT[j]
            ln_stats(xv)
            tt('dve', xv.ap, xv.ap, lng_bc.ap, ALU.mult, [xv, lng_bc], [xv])
            tt('dve', xv.ap, xv.ap, lnb_bc.ap, ALU.add, [xv, lnb_bc], [xv])
            T.dma('sp', 'yo%d' % j, lambda j=j, xv=xv: nc.sync.dma_start(out=y_out[out_i, j], in_=xv.ap), [xv], [('y_out', out_i, j)])

    for slot in range(3):
        stage_A(3 + slot, 1)
        stage_U()
        stage_S5_clean('light', slot)
    compute_carry()
    order = [(2, 'sample', 1, 2), (0, 'prompt', 0, 0), (1, 'prompt', 0, 1)]
    for blk, kind, cond_i, out_i in order:
        stage_A(blk, cond_i)
        stage_U()
        stage_S5_clean(kind)
        if kind == 'prompt':
            T.dma('sp', 'sto', lambda out_i=out_i: nc.sync.dma_start(
                out=st_out[out_i], in_=RM.view(STt.off, F32, [2 * 2 * 2 * GQ]).ap), [STt], [('st_out', out_i)])
        stage_GLU()
        stage_conv(kind)
        stage_merge()
        stage_out(blk, cond_i, out_i)
    T.final_wait('sp')
    return nc, T


def _chunk_w(w, kcn):
    K, N = w.shape
    a = w.reshape(kcn, 128, N // 128, 128)
    a = np.ascontiguousarray(a.transpose(2, 1, 0, 3)).reshape(N // 128, 128, kcn * 128)
    return a


def _tok_perm():
    j = np.arange(4)[:, None]
    pi = np.arange(128)[None, :]
    h = pi // 64
    n = pi % 64
    return 8 * n + 4 * h + j


def _gp_layout(a, GQ):
    d_, NG, P = a.shape[:3]
    rest = a.shape[3:]
    b = a.reshape(d_, GQ, 2, P, *rest)
    b = np.moveaxis(b, [2, 3], [0, 1])
    return np.ascontiguousarray(b.reshape(2 * P, d_, GQ, *rest))


def prepare_inputs(cfg, inp, n_cores=8):
    D, DS, DC, KC, KS, KCV, GQ, NG = cfg.D, cfg.DS, cfg.DC, cfg.KC, cfg.KS, cfg.KCV, cfg.GQ, cfg.NG
    f = np.float32
    xp = np.asarray(inp['x_prompt'], f)
    xs = np.asarray(inp['x_sample'], f)
    perm = _tok_perm()
    shared = {}
    shared['w_mod'] = _chunk_w(np.asarray(inp['w_mod'][0], f), KC)
    shared['b_mod'] = np.ascontiguousarray(np.asarray(inp['b_mod'][0], f).reshape(-1, 128).T)
    shared['w_in'] = _chunk_w(np.asarray(inp['w_in'][0], f), KC)
    shared['w_glu'] = _chunk_w(np.asarray(inp['w_glu'][0], f), KS)
    shared['b_glu'] = np.ascontiguousarray(np.asarray(inp['b_glu'][0], f).reshape(-1, 128).T)
    shared['w_ba'] = _chunk_w(np.asarray(inp['w_branch_a'][0], f), KS)
    shared['w_bb'] = _chunk_w(np.asarray(inp['w_branch_b'][0], f), KCV)
    shared['w_o'] = _chunk_w(np.asarray(inp['w_o'][0], f), KC)
    cwv = np.asarray(inp['conv_w'][0], f)
    shared['conv_w'] = np.ascontiguousarray(cwv.reshape(3, KCV, 128).transpose(2, 0, 1))
    shared['conv_b'] = np.ascontiguousarray(np.asarray(inp['conv_b'][0], f).reshape(KCV, 128).T)
    shared['ln_g'] = np.asarray(inp['ln_g'][0], f).reshape(1, D)
    shared['ln_b'] = np.asarray(inp['ln_b'][0], f).reshape(1, D)
    ldt = np.asarray(inp['ssm_log_dt'][0], f)
    ldt_b = np.broadcast_to(ldt[:, :, None], (2, NG, 64))
    sm_ = np.stack([_gp_layout(ldt_b, GQ), _gp_layout(np.asarray(inp['ssm_a_re'][0], f), GQ),
                    _gp_layout(np.asarray(inp['ssm_a_im'][0], f), GQ)], axis=1)
    shared['s5_sm'] = np.ascontiguousarray(sm_.reshape(128, 3, 2 * GQ))
    b_re = _gp_layout(np.asarray(inp['ssm_b_re'][0], f), GQ)
    b_im = _gp_layout(np.asarray(inp['ssm_b_im'][0], f), GQ)
    c_re = _gp_layout(np.asarray(inp['ssm_c_re'][0], f).transpose(0, 1, 3, 2), GQ)
    c_im = _gp_layout(np.asarray(inp['ssm_c_im'][0], f).transpose(0, 1, 3, 2), GQ)
    shared['s5_bc'] = np.ascontiguousarray(np.stack([b_re, b_im, c_re, c_im], axis=1).reshape(128, 4, 2 * GQ * 16))
    sd = np.asarray(inp['ssm_d'][0], f).reshape(NG, 16)
    shared['ssm_dv'] = np.ascontiguousarray(np.broadcast_to(sd.T[None, :, :], (8, 16, NG)).reshape(128, NG))
    ident = np.eye(128, dtype=f)
    blk = np.arange(128) // 16
    mL = (blk[None, :] >= blk[:, None]).astype(f)
    mU = (blk[None, :] <= blk[:, None]).astype(f)
    shared['consts'] = np.ascontiguousarray(np.stack([ident, mL, mU], axis=1))
    c_ctx = np.asarray(inp['c_ctx'], f)
    cs_ = np.asarray(inp['c'], f)
    sre = np.asarray(inp['state_ssm_re'], f)[:, 0]
    sim_ = np.asarray(inp['state_ssm_im'], f)[:, 0]
    maps = []
    for core in range(n_cores):
        m = dict(shared)
        si, q = core // 4, core % 4
        xblk = np.empty((6, 4, 128, D), f)
        for b in range(2):
            seqs = xp[core * 4 + 2 * b: core * 4 + 2 * b + 2].reshape(TB, D)
            xblk[b] = seqs[perm]
        for slot in range(4):
            qq = (q + slot) % 4
            dst = 2 if slot == 0 else 2 + slot
            xblk[dst] = xs[si, qq * TB:(qq + 1) * TB][perm]
        m['xb'] = xblk
        cond = np.stack([c_ctx, cs_[si]], axis=1)
        m['condT'] = np.ascontiguousarray(cond.reshape(KC, 128, 2).transpose(1, 0, 2))
        h0 = np.stack([_gp_layout(sre[si], GQ), _gp_layout(sim_[si], GQ)], axis=2)
        m['s5_h0'] = np.ascontiguousarray(h0)
        qm = np.zeros((128, 16), f)
        for j in range(1, 4):
            qm[:, j - 1] = 1.0 if (q + j >= 4) else 0.0
            qm[:, 3 + j - 1] = 1.0 if (q + j <= 3) else 0.0
        qm[:, 6 + q] = 1.0
        qm[:, 10 + (3 - q)] = 1.0
        m['qmask'] = qm
        maps.append(m)
    return maps


def assemble_outputs(cfg, results, n_cores=8, batch=32, seq=256, dec_batch=2, dec_seq=2048):
    D, GQ, NG = cfg.D, cfg.GQ, cfg.NG
    perm = _tok_perm()
    y_p = np.empty((batch, seq, D), np.float32)
    y_s = np.empty((dec_batch, dec_seq, D), np.float32)
    st_re = np.empty((batch, 1, 2, NG, 64), np.float32)
    st_im = np.empty((batch, 1, 2, NG, 64), np.float32)
    for core in range(n_cores):
        r = results[core]
        yo = np.asarray(r['y_out'])
        so = np.asarray(r['st_out']).reshape(2, 2, 64, 2, 2, 2, GQ)
        si, q = core // 4, core % 4
        for b in range(2):
            tmp = np.empty((TB, D), np.float32)
            tmp[perm] = yo[b]
            y_p[core * 4 + 2 * b: core * 4 + 2 * b + 2] = tmp.reshape(2, seq, D)
            for sq in range(2):
                s_ = so[b, :, :, sq]
                s_ = s_.transpose(2, 3, 4, 0, 1).reshape(2, 2, NG, 64)
                st_re[core * 4 + 2 * b + sq, 0] = s_[:, 0]
                st_im[core * 4 + 2 * b + sq, 0] = s_[:, 1]
        tmp = np.empty((TB, D), np.float32)
        tmp[perm] = yo[2]
        y_s[si, q * TB:(q + 1) * TB] = tmp
    return y_p, y_s, st_re, st_im


_CACHE = {}


def kernel(**inputs):
    cfg = Cfg()
    maps = prepare_inputs(cfg, inputs)
    nc, _ = build(cfg)
    res = run_bass_kernel_spmd(nc, maps, core_ids=list(range(8)))
    return assemble_outputs(cfg, res.results)
```
